# Optimizing a Trainium2 kernel written in Bass

```python
import math
import jax, jax.numpy as jnp
from jax import lax
import numpy as np

D_MODEL = 1024
BATCH = 16
SEQ = 256
DEPTH = 4
DEC_BATCH = 2
DEC_SEQ = 4096
PAST_LEN = 256

GRID_W = 64
D_MIX = 2 * D_MODEL
N_MIXERS = 4
D_BRANCH = D_MIX // N_MIXERS
HEAD_DIM = 64
N_HEADS = D_BRANCH // HEAD_DIM
N_DIR = 2
CHUNK = 64
CONV_W = 4
CONV_PAD = ((CONV_W - 1) // 2, CONV_W // 2)
LRU_C = 8.0
LORA_W = 64
LORA_A = 64
ROPE_BASE = 100.0
EPS = 1e-6

P_MLSTM = 5 * D_BRANCH + 4 * N_HEADS
P_LRU = 2 * D_BRANCH
P_RET = 4 * D_BRANCH
RWKV_SHIFT = 3 * D_BRANCH + LORA_W + LORA_A
P_RWKV = RWKV_SHIFT + D_BRANCH
P_IN = P_MLSTM + P_LRU + P_RET + P_RWKV
SPLITS = [P_MLSTM, P_MLSTM + P_LRU, P_MLSTM + P_LRU + P_RET]

kernel_name = 'hybrid_parallel_heads_diffusion_step'


def rmsnorm(x, g):
    x32 = x.astype(jnp.float32)
    y = x32 * lax.rsqrt(jnp.mean(x32 * x32, -1, keepdims=True) + EPS)
    return (y * g.astype(jnp.float32)).astype(x.dtype)


def head_norm(x):
    s = x.shape
    xh = x.reshape(s[:-1] + (N_HEADS, HEAD_DIM))
    xh = xh * lax.rsqrt(jnp.mean(xh * xh, -1, keepdims=True) + EPS)
    return xh.reshape(s)


def to_heads(x):
    b, n, _ = x.shape
    return x.reshape(b, n, N_HEADS, HEAD_DIM).transpose(0, 2, 1, 3).astype(jnp.float32)


def from_heads(x):
    b, h, n, d = x.shape
    return x.transpose(0, 2, 1, 3).reshape(b, n, h * d)


def to_chunks(x):
    b, h, n = x.shape[:3]
    return jnp.moveaxis(x.reshape((b, h, n // CHUNK, CHUNK) + x.shape[3:]), 2, 0)


def from_chunks(x):
    nc, b, h, l = x.shape[:4]
    return jnp.moveaxis(x, 0, 2).reshape((b, h, nc * l) + x.shape[4:])


def flip_t(x):
    return jnp.flip(x, axis=2)


def mlstm_scan(q, k, v, ig, lf, c0, n0, m0):
    tril = jnp.tril(jnp.ones((CHUNK, CHUNK), dtype=bool))

    def step(carry, xs):
        c, n, m = carry
        qc, kc, vc, igc, lfc = xs
        b = jnp.cumsum(lfc, axis=-1)
        dmat = jnp.where(tril, b[..., :, None] - b[..., None, :] + igc[..., None, :], -jnp.inf)
        m_inter = b + m[..., None]
        m_t = jnp.maximum(m_inter, jnp.max(dmat, -1))
        s = jnp.einsum('bhtd,bhsd->bhts', qc, kc) * jnp.exp(dmat - m_t[..., None])
        w_inter = jnp.exp(m_inter - m_t)
        num = jnp.einsum('bhts,bhsv->bhtv', s, vc) + w_inter[..., None] * jnp.einsum('bhtd,bhdv->bhtv', qc, c)
        den = jnp.sum(s, -1) + w_inter * jnp.einsum('bhtd,bhd->bht', qc, n)
        h = num / jnp.maximum(jnp.abs(den), jnp.exp(-m_t))[..., None]
        b_last = b[..., -1]
        w_log = b_last[..., None] - b + igc
        m_new = jnp.maximum(b_last + m, jnp.max(w_log, -1))
        dec = jnp.exp(b_last + m - m_new)
        wk = kc * jnp.exp(w_log - m_new[..., None])[..., None]
        c_new = dec[..., None, None] * c + jnp.einsum('bhsd,bhsv->bhdv', wk, vc)
        n_new = dec[..., None] * n + jnp.sum(wk, axis=2)
        return (c_new, n_new, m_new), h

    xs = (to_chunks(q), to_chunks(k), to_chunks(v), to_chunks(ig), to_chunks(lf))
    (c, n, m), hs = lax.scan(step, (c0, n0, m0), xs)
    return from_chunks(hs), c, n, m


def mlstm_branch(u, gate_b, c0, n0, m0):
    f32 = jnp.float32
    b, n, _ = u.shape
    q, k, v, o, g = (u[..., i * D_BRANCH:(i + 1) * D_BRANCH] for i in range(5))
    gates = u[..., 5 * D_BRANCH:].astype(f32) + gate_b.astype(f32)
    gates = gates.reshape(b, n, 4, N_HEADS).transpose(2, 0, 3, 1)
    qh, kh, vh = to_heads(q), to_heads(k) * HEAD_DIM ** -0.5, to_heads(v)
    hf, cf, nf, mf = mlstm_scan(qh, kh, vh, gates[0], jax.nn.log_sigmoid(gates[1]),
                                c0[:, 0].astype(f32), n0[:, 0].astype(f32), m0[:, 0].astype(f32))
    hb, cb, nb, mb = mlstm_scan(flip_t(qh), flip_t(kh), flip_t(vh), flip_t(gates[2]),
                                flip_t(jax.nn.log_sigmoid(gates[3])),
                                c0[:, 1].astype(f32), n0[:, 1].astype(f32), m0[:, 1].astype(f32))
    h = from_heads(hf + flip_t(hb)) * jax.nn.sigmoid(o.astype(f32))
    y = head_norm(h) * jax.nn.silu(g.astype(f32))
    return (y.astype(u.dtype), jnp.stack([cf, cb], 1), jnp.stack([nf, nb], 1), jnp.stack([mf, mb], 1))


def linear_scan(a, bx, h0):
    def combine(l, r):
        return (l[0] * r[0], r[0] * l[1] + r[1])
    a_cum, b_cum = lax.associative_scan(combine, (a, bx), axis=1)
    return a_cum * h0[:, None] + b_cum


def rglru_branch(u, conv_w, conv_b, gate_w, gate_b, lam, h0):
    f32 = jnp.float32
    b, n, _ = u.shape
    xb, g = u[..., :D_BRANCH], u[..., D_BRANCH:]
    xc = lax.conv_general_dilated(xb, conv_w.astype(xb.dtype)[:, None, :], window_strides=(1,),
                                  padding=[CONV_PAD], dimension_numbers=('NWC', 'WIO', 'NWC'),
                                  feature_group_count=D_BRANCH)
    xc = xc.astype(f32) + conv_b.astype(f32)

    def direction(d, xs, h_init):
        xr = xs.reshape(b, n, N_HEADS, HEAD_DIM)
        gr = jnp.einsum('bnhi,hij->bnhj', xr, gate_w[d, 0].astype(f32)).reshape(b, n, D_BRANCH) + gate_b[d, 0].astype(f32)
        gi = jnp.einsum('bnhi,hij->bnhj', xr, gate_w[d, 1].astype(f32)).reshape(b, n, D_BRANCH) + gate_b[d, 1].astype(f32)
        log_a = -LRU_C * jax.nn.sigmoid(gr) * jax.nn.softplus(-lam[d].astype(f32))
        a = jnp.exp(log_a)
        beta = jnp.sqrt(-jnp.expm1(2.0 * log_a))
        return linear_scan(a, beta * jax.nn.sigmoid(gi) * xs, h_init)

    hf = direction(0, xc, h0[:, 0].astype(f32))
    hb = direction(1, jnp.flip(xc, 1), h0[:, 1].astype(f32))
    y = (hf + jnp.flip(hb, 1)) * jax.nn.silu(g.astype(f32))
    return y.astype(u.dtype), jnp.stack([hf[:, -1], hb[:, -1]], 1)


def apply_rope(x, rope):
    cos, sin = rope
    x1, x2 = x[..., :HEAD_DIM // 2], x[..., HEAD_DIM // 2:]
    return jnp.concatenate([x1 * cos - x2 * sin, x1 * sin + x2 * cos], -1)


def retention_scan(q, k, v, log_g, r0):
    pos = jnp.arange(CHUNK, dtype=jnp.float32)
    diff = pos[:, None] - pos[None, :]
    dmat = jnp.where(diff >= 0, jnp.exp(log_g[:, None, None] * jnp.maximum(diff, 0.0)), 0.0)
    xi = jnp.exp(log_g[:, None] * (pos + 1.0))[..., None]
    wk = jnp.exp(log_g[:, None] * (CHUNK - 1.0 - pos))[..., None]
    dec = jnp.exp(log_g * CHUNK)[:, None, None]

    def step(r, xs):
        qc, kc, vc = xs
        s = jnp.einsum('bhtd,bhsd->bhts', qc, kc) * dmat
        o = jnp.einsum('bhts,bhsv->bhtv', s, vc) + xi * jnp.einsum('bhtd,bhdv->bhtv', qc, r)
        r_new = dec * r + jnp.einsum('bhsd,bhsv->bhdv', kc * wk, vc)
        return r_new, o

    r, outs = lax.scan(step, r0, (to_chunks(q), to_chunks(k), to_chunks(v)))
    return from_chunks(outs), r


def retention_branch(u, theta, r0, rope):
    f32 = jnp.float32
    q, k, v, g = (u[..., i * D_BRANCH:(i + 1) * D_BRANCH] for i in range(4))
    qh, kh, vh = to_heads(q), to_heads(k) * HEAD_DIM ** -0.5, to_heads(v)
    if rope is not None:
        qh, kh = apply_rope(qh, rope), apply_rope(kh, rope)
    log_g = jax.nn.log_sigmoid(theta.astype(f32))
    yf, rf = retention_scan(qh, kh, vh, log_g[0], r0[:, 0].astype(f32))
    yb, rb = retention_scan(flip_t(qh), flip_t(kh), flip_t(vh), log_g[1], r0[:, 1].astype(f32))
    y = head_norm(from_heads(yf + flip_t(yb))) * jax.nn.silu(g.astype(f32))
    return y.astype(u.dtype), jnp.stack([rf, rb], 1)


def shift_1d(s):
    half = s.shape[-1] // 2
    prev = jnp.pad(s[:, :-1, :half], ((0, 0), (1, 0), (0, 0)))
    nxt = jnp.pad(s[:, 1:, half:], ((0, 0), (0, 1), (0, 0)))
    return jnp.concatenate([prev, nxt], -1)


def shift_grid(s, rows):
    b, n, c = s.shape
    qc = c // 4
    x = s.reshape(b, rows, GRID_W, c)
    left = jnp.pad(x[:, :, :-1, :qc], ((0, 0), (0, 0), (1, 0), (0, 0)))
    right = jnp.pad(x[:, :, 1:, qc:2 * qc], ((0, 0), (0, 0), (0, 1), (0, 0)))
    up = jnp.pad(x[:, :-1, :, 2 * qc:3 * qc], ((0, 0), (1, 0), (0, 0), (0, 0)))
    down = jnp.pad(x[:, 1:, :, 3 * qc:], ((0, 0), (0, 1), (0, 0), (0, 0)))
    return jnp.concatenate([left, right, up, down], -1).reshape(b, n, c)


def rwkv_scan(r, w, k, v, kk, bb, s0):
    bsz, n, _ = r.shape

    def seq(t):
        return jnp.moveaxis(t.reshape(bsz, n, N_HEADS, HEAD_DIM), 1, 0)

    def step(s, xs):
        rt, wt, kt, vt, kkt, bt = xs
        sa = jnp.einsum('bhij,bhj->bhi', s, -kkt)
        s = s * wt[:, :, None, :] + sa[..., None] * bt[:, :, None, :] + vt[..., None] * kt[:, :, None, :]
        return s, jnp.einsum('bhij,bhj->bhi', s, rt)

    s, ys = lax.scan(step, s0, (seq(r), seq(w), seq(k), seq(v), seq(kk), seq(bb)))
    return jnp.moveaxis(ys, 0, 1).reshape(bsz, n, D_BRANCH), s


def rwkv_branch(u, mu, w0, w2, a0, a2, k_k, k_a, r_k, s0, rows):
    f32 = jnp.float32
    b, n, _ = u.shape
    s, g = u[..., :RWKV_SHIFT], u[..., RWKV_SHIFT:]
    sh = shift_1d(s) if rows is None else shift_grid(s, rows)
    s = (s + mu * (sh - s)).astype(f32)
    r = s[..., :D_BRANCH]
    k = s[..., D_BRANCH:2 * D_BRANCH]
    v = s[..., 2 * D_BRANCH:3 * D_BRANCH]
    lw = jnp.tanh(s[..., 3 * D_BRANCH:3 * D_BRANCH + LORA_W])
    la = s[..., 3 * D_BRANCH + LORA_W:]
    kkh = (k * k_k.astype(f32)).reshape(b, n, N_HEADS, HEAD_DIM)
    kk = (kkh / jnp.maximum(jnp.sqrt(jnp.sum(kkh * kkh, -1, keepdims=True)), 1e-12)).reshape(b, n, D_BRANCH)
    outs, finals = [], []
    for d in range(N_DIR):
        w_log = -jax.nn.softplus(-(w0[d].astype(f32) + lw @ w2[d].astype(f32))) - 0.5
        decay = jnp.exp(-jnp.exp(w_log))
        a = jax.nn.sigmoid(a0[d].astype(f32) + la @ a2[d].astype(f32))
        kt = k * (1.0 + (a - 1.0) * k_a.astype(f32))
        seqs = (r, decay, kt, v, kk, kk * a)
        if d == 1:
            seqs = tuple(jnp.flip(t, 1) for t in seqs)
        y, sf = rwkv_scan(*seqs, s0[:, d].astype(f32))
        outs.append(y if d == 0 else jnp.flip(y, 1))
        finals.append(sf)
    bonus = jnp.sum((r * k * r_k.astype(f32)).reshape(b, n, N_HEADS, HEAD_DIM), -1, keepdims=True) * v.reshape(b, n, N_HEADS, HEAD_DIM)
    y = (head_norm(outs[0] + outs[1]) + bonus.reshape(b, n, D_BRANCH)) * jax.nn.silu(g.astype(f32))
    return y.astype(u.dtype), jnp.stack(finals, 1)


def trunk_layer(x, mod, p, states, rows, rope):
    f32 = jnp.float32
    shift, scale, gate = jnp.split(mod[:, None, :].astype(f32), 3, axis=-1)
    h = (rmsnorm(x, p['norm_g']).astype(f32) * (1.0 + scale) + shift).astype(x.dtype)
    u = h @ p['w_in'].astype(x.dtype)
    u_a, u_b, u_c, u_d = jnp.split(u, SPLITS, axis=-1)
    c_m, n_m, m_m, h_l, r_r, s_w = states
    y_a, c_new, n_new, m_new = mlstm_branch(u_a, p['mlstm_gate_b'], c_m, n_m, m_m)
    y_b, h_new = rglru_branch(u_b, p['lru_conv_w'], p['lru_conv_b'], p['lru_gate_w'], p['lru_gate_b'],
                              p['lru_lambda'], h_l)
    y_c, r_new = retention_branch(u_c, p['ret_theta'], r_r, rope)
    y_d, s_new = rwkv_branch(u_d, p['rwkv_mu'], p['rwkv_w0'], p['rwkv_w2'], p['rwkv_a0'], p['rwkv_a2'],
                             p['rwkv_kk'], p['rwkv_ka'], p['rwkv_rk'], s_w, rows)
    y = jnp.concatenate([y_a, y_b, y_c, y_d], -1) @ p['w_out'].astype(x.dtype)
    x = (x.astype(f32) + gate * y.astype(f32)).astype(x.dtype)
    return x, (c_new, n_new, m_new, h_new, r_new, s_new)


def setup_inputs(seed: int = 0) -> dict:
    key = jax.random.key(seed)
    ks = iter(jax.random.split(key, 40))
    f32 = jnp.float32

    def nrm(shape, s):
        return s * jax.random.normal(next(ks), shape, f32)

    H, HD, DB = N_HEADS, HEAD_DIM, D_BRANCH
    x_prompt = nrm((BATCH, SEQ, D_MODEL), 1.0)
    x_sample = nrm((DEC_BATCH, DEC_SEQ, D_MODEL), 1.0)
    c = nrm((DEC_BATCH, D_MODEL), 1.0)
    state_mlstm_c = nrm((DEC_BATCH, DEPTH, N_DIR, H, HD, HD), 0.1)
    state_mlstm_n = nrm((DEC_BATCH, DEPTH, N_DIR, H, HD), 0.1)
    state_mlstm_m = nrm((DEC_BATCH, DEPTH, N_DIR, H), 1.0)
    state_lru_h = nrm((DEC_BATCH, DEPTH, N_DIR, DB), 0.5)
    state_ret_r = nrm((DEC_BATCH, DEPTH, N_DIR, H, HD, HD), 0.3)
    state_rwkv_s = nrm((DEC_BATCH, DEPTH, N_DIR, H, HD, HD), 0.3)
    c_ctx = nrm((D_MODEL,), 1.0)
    norm_g = 1.0 + nrm((DEPTH, D_MODEL), 0.02)
    w_mod = nrm((DEPTH, D_MODEL, 3 * D_MODEL), D_MODEL ** -0.5)
    b_mod = nrm((DEPTH, 3 * D_MODEL), 0.02)
    w_in = nrm((DEPTH, D_MODEL, P_IN), D_MODEL ** -0.5)
    w_out = nrm((DEPTH, D_MIX, D_MODEL), D_MIX ** -0.5)
    fb = jnp.linspace(3.0, 6.0, H, dtype=f32)
    zb = jnp.zeros((H,), f32)
    mlstm_gate_b = jnp.concatenate([zb, fb, zb, fb])[None] + nrm((DEPTH, 4 * H), 0.1)
    lru_conv_w = nrm((DEPTH, CONV_W, DB), CONV_W ** -0.5)
    lru_conv_b = nrm((DEPTH, DB), 0.02)
    lru_gate_w = nrm((DEPTH, N_DIR, 2, H, HD, HD), HD ** -0.5)
    lru_gate_b = nrm((DEPTH, N_DIR, 2, DB), 0.02)
    a_target = jax.random.uniform(next(ks), (DEPTH, N_DIR, DB), f32, minval=0.9, maxval=0.999)
    sig = a_target ** (1.0 / LRU_C)
    lru_lambda = jnp.log(sig) - jnp.log1p(-sig)
    gamma = 1.0 - 2.0 ** (-5.0 - jnp.arange(H, dtype=f32))
    ret_theta = (jnp.log(gamma) - jnp.log1p(-gamma)) + nrm((DEPTH, N_DIR, H), 0.01)
    rwkv_mu = jax.random.uniform(next(ks), (DEPTH, RWKV_SHIFT), f32)
    rwkv_w0 = jnp.linspace(-6.0, -1.0, DB, dtype=f32) + nrm((DEPTH, N_DIR, DB), 0.1)
    rwkv_w2 = nrm((DEPTH, N_DIR, LORA_W, DB), 0.1 * LORA_W ** -0.5)
    rwkv_a0 = nrm((DEPTH, N_DIR, DB), 0.1)
    rwkv_a2 = nrm((DEPTH, N_DIR, LORA_A, DB), 0.5 * LORA_A ** -0.5)
    rwkv_kk = 0.85 + nrm((DEPTH, DB), 0.02)
    rwkv_ka = 1.0 + nrm((DEPTH, DB), 0.02)
    rwkv_rk = nrm((DEPTH, DB), 0.1)
    final_g = 1.0 + nrm((D_MODEL,), 0.02)
    return {'x_prompt': x_prompt, 'x_sample': x_sample, 'c': c,
            'state_mlstm_c': state_mlstm_c, 'state_mlstm_n': state_mlstm_n, 'state_mlstm_m': state_mlstm_m,
            'state_lru_h': state_lru_h, 'state_ret_r': state_ret_r, 'state_rwkv_s': state_rwkv_s,
            'c_ctx': c_ctx, 'norm_g': norm_g, 'w_mod': w_mod, 'b_mod': b_mod, 'w_in': w_in, 'w_out': w_out,
            'mlstm_gate_b': mlstm_gate_b, 'lru_conv_w': lru_conv_w, 'lru_conv_b': lru_conv_b,
            'lru_gate_w': lru_gate_w, 'lru_gate_b': lru_gate_b, 'lru_lambda': lru_lambda,
            'ret_theta': ret_theta, 'rwkv_mu': rwkv_mu, 'rwkv_w0': rwkv_w0, 'rwkv_w2': rwkv_w2,
            'rwkv_a0': rwkv_a0, 'rwkv_a2': rwkv_a2, 'rwkv_kk': rwkv_kk, 'rwkv_ka': rwkv_ka,
            'rwkv_rk': rwkv_rk, 'final_g': final_g}


def reference(x_prompt, x_sample, c, state_mlstm_c, state_mlstm_n, state_mlstm_m, state_lru_h, state_ret_r,
              state_rwkv_s, c_ctx, norm_g, w_mod, b_mod, w_in, w_out, mlstm_gate_b, lru_conv_w, lru_conv_b,
              lru_gate_w, lru_gate_b, lru_lambda, ret_theta, rwkv_mu, rwkv_w0, rwkv_w2, rwkv_a0, rwkv_a2,
              rwkv_kk, rwkv_ka, rwkv_rk, final_g):
    f32 = jnp.float32

    def layer_params(l):
        return dict(norm_g=norm_g[l], w_in=w_in[l], w_out=w_out[l], mlstm_gate_b=mlstm_gate_b[l],
                    lru_conv_w=lru_conv_w[l], lru_conv_b=lru_conv_b[l], lru_gate_w=lru_gate_w[l],
                    lru_gate_b=lru_gate_b[l], lru_lambda=lru_lambda[l], ret_theta=ret_theta[l],
                    rwkv_mu=rwkv_mu[l], rwkv_w0=rwkv_w0[l], rwkv_w2=rwkv_w2[l], rwkv_a0=rwkv_a0[l],
                    rwkv_a2=rwkv_a2[l], rwkv_kk=rwkv_kk[l], rwkv_ka=rwkv_ka[l], rwkv_rk=rwkv_rk[l])

    bp = x_prompt.shape[0]
    zero_states = (jnp.zeros((bp, N_DIR, N_HEADS, HEAD_DIM, HEAD_DIM), f32),
                   jnp.zeros((bp, N_DIR, N_HEADS, HEAD_DIM), f32),
                   jnp.zeros((bp, N_DIR, N_HEADS), f32),
                   jnp.zeros((bp, N_DIR, D_BRANCH), f32),
                   jnp.zeros((bp, N_DIR, N_HEADS, HEAD_DIM, HEAD_DIM), f32),
                   jnp.zeros((bp, N_DIR, N_HEADS, HEAD_DIM, HEAD_DIM), f32))
    sc_ctx = jax.nn.silu(c_ctx.astype(f32))
    xp = x_prompt
    per_layer = []
    for l in range(DEPTH):
        mod = (sc_ctx @ w_mod[l].astype(f32) + b_mod[l].astype(f32))[None]
        xp, st = trunk_layer(xp, mod, layer_params(l), zero_states, None, None)
        per_layer.append(st)
    y_prompt = rmsnorm(xp, final_g)
    new_mlstm_c = jnp.stack([st[0] for st in per_layer], 1)
    new_mlstm_n = jnp.stack([st[1] for st in per_layer], 1)
    new_mlstm_m = jnp.stack([st[2] for st in per_layer], 1)
    new_lru_h = jnp.stack([st[3] for st in per_layer], 1)
    new_ret_r = jnp.stack([st[4] for st in per_layer], 1)
    new_rwkv_s = jnp.stack([st[5] for st in per_layer], 1)

    n_lat = x_sample.shape[1]
    rows = n_lat // GRID_W
    row_idx = jnp.broadcast_to(jnp.arange(rows, dtype=f32)[:, None], (rows, GRID_W)).reshape(-1)
    col_idx = jnp.broadcast_to(jnp.arange(GRID_W, dtype=f32)[None, :], (rows, GRID_W)).reshape(-1)
    n_freq = HEAD_DIM // 4
    freqs = ROPE_BASE ** (-jnp.arange(n_freq, dtype=f32) / n_freq)
    ang = jnp.concatenate([row_idx[:, None] * freqs, col_idx[:, None] * freqs], -1)
    rope = (jnp.cos(ang), jnp.sin(ang))
    sc = jax.nn.silu(c.astype(f32))
    xs = x_sample
    for l in range(DEPTH):
        mod = sc @ w_mod[l].astype(f32) + b_mod[l].astype(f32)
        st = (state_mlstm_c[:, l], state_mlstm_n[:, l], state_mlstm_m[:, l], state_lru_h[:, l],
              state_ret_r[:, l], state_rwkv_s[:, l])
        xs, _ = trunk_layer(xs, mod, layer_params(l), st, rows, rope)
    y_sample = rmsnorm(xs, final_g)
    return (y_prompt, y_sample, new_mlstm_c, new_mlstm_n, new_mlstm_m, new_lru_h, new_ret_r, new_rwkv_s)
```

```python
import contextlib
import numpy as np
import concourse.bass as bass
import concourse.mybir as mybir
from concourse.bass_utils import run_bass_kernel_spmd

F32 = mybir.dt.float32
BF16 = mybir.dt.bfloat16
AF = mybir.ActivationFunctionType
ALU = mybir.AluOpType
AX = mybir.AxisListType

D = 1024
KT = 8
NH = 8
HD = 64
DB = 512
EPS = 1e-6
P_IN = 7840
OFF_A, OFF_B, OFF_C, OFF_D = 0, 2592, 3616, 5664

ENGS = ("pe", "dve", "act", "pool", "sp")


class Buf:
    __slots__ = ("name", "w", "rs", "psum")

    def __init__(self, name):
        self.name = name
        self.w = None
        self.rs = []
        self.psum = False


class Op:
    __slots__ = ("eng", "meth", "kw", "deps", "dma", "sig", "sem", "val", "idx")


class Prog:
    def __init__(self, nc, n_dma_sems=8):
        self.nc = nc
        self.ops = []
        self.stack = contextlib.ExitStack()
        self.n_dma_sems = n_dma_sems
        self.nbuf = 0
        self.last = {e: None for e in ENGS}
        self.dmas_since_barrier = []

    def sb(self, name, shape, dt):
        return self.stack.enter_context(self.nc.sbuf_tensor(name, list(shape), dt))

    def ps(self, name, shape, dt):
        return self.stack.enter_context(self.nc.psum_tensor(name, list(shape), dt))

    def buf(self, name=None):
        self.nbuf += 1
        return Buf(name or f"b{self.nbuf}")

    def bufs(self, n):
        return [self.buf() for _ in range(n)]

    def _rec(self, eng, meth, kw, R, W, dma=False, extra_deps=()):
        deps = set(extra_deps)
        for b in R:
            if b.w is not None:
                deps.add(b.w)
            if b.psum:
                for r in b.rs:
                    if self.ops[r].eng != eng:
                        deps.add(r)
        for b in W:
            if b.w is not None:
                deps.add(b.w)
            deps.update(b.rs)
        i = len(self.ops)
        op = Op()
        op.eng, op.meth, op.kw, op.dma = eng, meth, kw, dma
        op.deps = sorted(deps)
        op.sig = bool(dma)
        op.sem = None
        op.val = 0
        op.idx = i
        self.ops.append(op)
        for d in deps:
            self.ops[d].sig = True
        for b in R:
            b.rs.append(i)
        for b in W:
            b.w = i
            b.rs = []
        self.last[eng] = i
        if dma:
            self.dmas_since_barrier.append(i)
        return op

    def op(self, eng, meth, R=(), W=(), **kw):
        return self._rec(eng, meth, kw, R, W)

    def dma(self, q, out, in_, R=(), W=(), **kw):
        kw = dict(kw)
        kw["out"] = out
        kw["in_"] = in_
        return self._rec(q, "dma_start", kw, R, W, dma=True)

    def barrier(self):
        lasts = [v for v in self.last.values() if v is not None] + list(self.dmas_since_barrier)
        self.dmas_since_barrier = []
        for e in ("pe", "dve", "act", "pool", "sp"):
            self._rec(e, "nop", {}, (), (), extra_deps=lasts)

    def emit(self):
        nc, st = self.nc, self.stack
        sems = {e: st.enter_context(nc.semaphore(f"s_{e}")) for e in ("pe", "dve", "act", "pool", "sp")}
        dsems = {q: [st.enter_context(nc.semaphore(f"d_{q}{i}")) for i in range(self.n_dma_sems)]
                 for q in ("sp", "act", "pool")}
        cnt = {e: 0 for e in sems}
        dcnt = {q: 0 for q in dsems}
        dval = {q: [0] * self.n_dma_sems for q in dsems}
        prev_on_sem = {}
        for op in self.ops:
            if op.dma:
                k = dcnt[op.eng] % self.n_dma_sems
                dcnt[op.eng] += 1
                dval[op.eng][k] += 16
                op.sem = ("d", op.eng, k)
                op.val = dval[op.eng][k]
                p = prev_on_sem.get(op.sem)
                if p is not None and p not in op.deps:
                    op.deps = sorted(set(op.deps) | {p})
                prev_on_sem[op.sem] = op.idx
            elif op.sig:
                cnt[op.eng] += 1
                op.sem = ("c", op.eng)
                op.val = cnt[op.eng]

        def semobj(key):
            return sems[key[1]] if key[0] == "c" else dsems[key[1]][key[2]]

        clocks = [None] * len(self.ops)
        known = {e: {} for e in ENGS}
        streams = {e: [] for e in ENGS}
        for op in self.ops:
            kn = known[op.eng]
            wm = {}
            for d in op.deps:
                dop = self.ops[d]
                if kn.get(dop.sem, 0) >= dop.val:
                    continue
                wm[dop.sem] = max(wm.get(dop.sem, 0), dop.val)
                for s, v in clocks[d].items():
                    if kn.get(s, 0) < v:
                        kn[s] = v
            ck = dict(kn)
            if op.sem is not None:
                ck[op.sem] = max(ck.get(op.sem, 0), op.val)
            clocks[op.idx] = ck
            streams[op.eng].append((op, wm))
        self.n_instr = {e: len(streams[e]) for e in ENGS}
        finals = {}
        for q in dsems:
            for k in range(self.n_dma_sems):
                if dval[q][k] > 0:
                    finals[("d", q, k)] = dval[q][k]
        block = st.enter_context(nc.Block())

        def run(engobj, ename):
            for op, wm in streams[ename]:
                for s, v in wm.items():
                    engobj.wait_ge(semobj(s), v)
                if op.meth == "nop":
                    ins = engobj.nop()
                else:
                    ins = getattr(engobj, op.meth)(**op.kw)
                if op.sem is not None:
                    ins.then_inc(semobj(op.sem), 16 if op.dma else 1)
            if ename == "sp":
                for s, v in finals.items():
                    engobj.wait_ge(semobj(s), v)

        block.tensor(lambda e: run(e, "pe"))
        block.vector(lambda e: run(e, "dve"))
        block.scalar(lambda e: run(e, "act"))
        block.gpsimd(lambda e: run(e, "pool"))
        block.sync(lambda e: run(e, "sp"))

    def close(self):
        self.stack.close()


def flip(a, dims=(-1,)):
    ap = [list(x) for x in a.ap]
    off = a.offset
    for d in dims:
        s, n = ap[d]
        off = off + (n - 1) * s
        ap[d] = [-s, n]
    return bass.AP(a.tensor, off, ap)


class Arena:
    def __init__(self, P, name, words, dt):
        self.t = P.sb(name, [128, words], dt)
        self.words = words
        self.pos = 0
        self.P = P
        self.hi = 0

    def reset(self):
        self.pos = 0

    def alloc(self, n, parts=128):
        n2 = (n + 7) // 8 * 8
        assert self.pos + n2 <= self.words, f"arena overflow {self.pos}+{n2}>{self.words}"
        a = self.t[0:parts, self.pos:self.pos + n]
        self.pos += n2
        self.hi = max(self.hi, self.pos)
        return a


class TT:
    __slots__ = ("ap", "b")

    def __init__(self, ap, b):
        self.ap, self.b = ap, b

    def __getitem__(self, k):
        return self.ap[k]


def v3(ap, a, b):
    return ap.rearrange("p (a b) -> p a b", a=a, b=b)


def bc_mid(ap, n):
    p, m = ap.shape
    return ap.unsqueeze(1).broadcast_to([p, n, m])


def bc_last(ap, n):
    shp = list(ap.shape)
    return ap.unsqueeze(len(shp)).broadcast_to(shp + [n])


class Job:
    pass


class KB:
    def __init__(self, cfg):
        self.cfg = cfg
        self.L = cfg["DEPTH"]
        self.mixers = cfg.get("mixers", "ABCD")
        nc = bass.Bass("TRN2", target_bir_lowering=False)
        self.nc = nc
        self.P = Prog(nc)
        self.din = {}
        self.dout = {}
        self.in_shapes = {}

    def inp(self, name, shape):
        t = self.nc.dram_tensor(name, list(shape), F32, kind="ExternalInput")
        self.din[name] = t
        self.in_shapes[name] = tuple(shape)
        return t.ap()

    def outp(self, name, shape):
        t = self.nc.dram_tensor(name, list(shape), F32, kind="ExternalOutput")
        self.dout[name] = t
        return t.ap()

    def scratch(self, name, shape, dt):
        return self.nc.dram_tensor(name, list(shape), dt, kind="Internal").ap()

    def tt(self, arena, n, parts=128):
        return TT(arena.alloc(n, parts), self.P.buf())

    def bank(self):
        i = self.bank_i
        self.bank_i = (i + 1) % len(self.banks)
        return self.banks[i]

    def bbank(self):
        i = self.bbank_i
        self.bbank_i = (i + 1) % len(self.bbanks)
        return self.bbanks[i]

    def op(self, eng, meth, R=(), W=(), **kw):
        return self.P.op(eng, meth, R=[x.b for x in R], W=[x.b for x in W], **kw)

    def dma(self, q, out, in_, R=(), W=(), **kw):
        return self.P.dma(q, out, in_, R=[x.b for x in R], W=[x.b for x in W], **kw)

    def build(self):
        cfg, P, nc, L = self.cfg, self.P, self.nc, self.L
        NP, NPS, NS = cfg["NP"], cfg["NPS"], cfg["NS"]
        self.SEGB = cfg.get("SEGB", 4)
        I = self.inp
        xp = I("xp", [NPS, NP, D]) if NPS else None
        xs = I("xs", [NS, D]) if NS else None
        cvT = I("cvT", [128, KT, 2])
        self.wA = I("wA", [L, 4, 128, KT, 640])
        self.wAg = I("wAg", [L, 128, KT, 32])
        self.wB = I("wB", [L, 4, 128, KT, 256])
        self.wC = I("wC", [L, 4, 128, KT, 768])
        self.wD = I("wD", [L, 4, 128, KT, 512])
        self.wDl = I("wDl", [L, 128, KT, 128])
        self.wout = I("wout", [L, 128, 16, D])
        self.wmod = I("wmod", [L, 128, KT, 3 * D])
        self.ng2 = I("ng2", [L, 2, D])
        self.bmod2 = I("bmod2", [L, 2, 3 * D])
        self.fgb = I("fgb", [128, D])
        self.gbA = I("gbA", [L, 8, 4])
        self.lruP = I("lruP", [L, 4, 128, 12])
        self.lruG = I("lruG", [L, 4, 128, 4, 128])
        self.retP = I("retP", [L, 4, 128, 6])
        self.rwP = I("rwP", [L, 4, 128, 12])
        self.rwLP = I("rwLP", [L, 128, 2])
        self.rwW2 = I("rwW2", [L, 2, 128, DB])
        c_ident = I("c_ident", [128, 128])
        c_masks = I("c_masks", [128, 4, 128])
        c_diff = I("c_diff", [128, 2, 128])
        c_pos = I("c_pos", [128, 4, 128])
        c_sel = I("c_sel", [8, 4, 128])
        c_sel2 = I("c_sel2", [2, 2, 128])
        c_bones = I("c_bones", [128, 128])
        c_hmask = I("c_hmask", [128, 4, 128])
        c_rmask = I("c_rmask", [128, 2, 512])
        self.c_shm = {}
        self.c_shm["p"] = I("c_shm_p", [4, 128, 3, 4])
        self.c_shm["s"] = I("c_shm_s", [4, 128, 3, 4])
        self.c_shl = {"p": I("c_shl_p", [128, 4]), "s": I("c_shl_s", [128, 4])}
        if NS:
            self.rope = I("rope", [2, 128, NS])
            self.st_in = dict(
                c=I("st_c", [L, 2, NH, HD, HD]), n=I("st_n", [L, 2, NH, HD]), m=I("st_m", [L, 2, NH]),
                h=I("st_h", [L, 2, DB]), r=I("st_r", [L, 2, NH, HD, HD]), s=I("st_s", [L, 2, NH, HD, HD]))
        O = self.outp
        if NPS:
            yp = O("yp", [NPS, NP, D])
            self.so = dict(c=O("o_c", [NPS, L, 2, NH, HD, HD]), n=O("o_n", [NPS, L, 2, NH, HD]),
                           m=O("o_m", [NPS, L, 2, NH]), h=O("o_h", [NPS, L, 2, DB]),
                           r=O("o_r", [NPS, L, 2, NH, HD, HD]), s=O("o_s", [NPS, L, 2, NH, HD, HD]))
        if NS:
            ys = O("ys", [NS, D])
        self.dbg = cfg.get("dbg", False)
        jobs = []
        for i in range(NPS):
            j = Job()
            j.name, j.N, j.kind, j.g, j.idx = f"p{i}", NP, "p", 0, i
            j.x_in, j.y_out = xp[i], yp[i]
            jobs.append(j)
        if NS:
            j = Job()
            j.name, j.N, j.kind, j.g, j.idx = "s", NS, "s", 1, 0
            j.x_in, j.y_out = xs, ys
            jobs.append(j)
        for j in jobs:
            j.NB = j.N // 128
            j.HT = self.scratch(f"HT_{j.name}", [KT, 128, j.N], BF16)
            j.YT = self.scratch(f"YT_{j.name}", [16, 128, j.N], BF16)
            j.XR = self.scratch(f"XR_{j.name}", [j.N, D], F32)
            j.GB = self.scratch(f"GB_{j.name}", [2, 2, 8, j.N], F32)
            j.bHT, j.bYT, j.bXR, j.bGB = TT(None, P.buf()), TT(None, P.buf()), TT(None, P.buf()), TT(None, P.buf())
            if self.dbg:
                j.dbgY = O(f"dbgY_{j.name}", [16, 128, j.N])
        self.jobs = jobs
        NMAX = max(j.N for j in jobs)
        NBMAX = NMAX // 128
        self.banks = [TT(P.ps(f"bk{i}", [128, 512], F32)[:], P.buf()) for i in range(6)]
        self.bbanks = [TT(P.ps(f"bb{i}", [128, 1024], BF16)[:], P.buf()) for i in range(2)]
        for t_ in self.banks + self.bbanks:
            t_.b.psum = True
        self.bank_i = 0
        self.bbank_i = 0
        cw = cfg.get("CARENA", 5000)
        CA = Arena(P, "carena", cw, F32)
        self.CA = CA
        CB = Arena(P, "cbarena", 1536, BF16)
        k = self
        k.ident = k.tt(CA, 128)
        k.masks = k.tt(CA, 512)
        k.diff = k.tt(CA, 256)
        k.pos = k.tt(CA, 512)
        k.sel = k.tt(CA, 512, parts=8)
        k.bones = k.tt(CA, 128)
        k.sel2 = k.tt(CA, 256, parts=2)
        k.rmask = k.tt(CA, 1024)
        k.identb = k.tt(CB, 128)
        k.bonesb = k.tt(CB, 128)
        k.masksb = k.tt(CB, 512)
        k.hmaskb = k.tt(CB, 512)
        k.dma("pool", v3(k.hmaskb.ap, 4, 128), c_hmask, W=[k.hmaskb])
        k.dma("sp", k.ident.ap, c_ident, W=[k.ident])
        k.dma("sp", v3(k.masks.ap, 4, 128), c_masks, W=[k.masks])
        k.dma("sp", v3(k.diff.ap, 2, 128), c_diff, W=[k.diff])
        k.dma("sp", v3(k.pos.ap, 4, 128), c_pos, W=[k.pos])
        k.dma("sp", v3(k.sel.ap, 4, 128), c_sel, W=[k.sel])
        k.dma("sp", k.bones.ap, c_bones, W=[k.bones])
        k.dma("sp", v3(k.sel2.ap, 2, 128), c_sel2, W=[k.sel2])
        k.dma("sp", v3(k.rmask.ap, 2, 512), c_rmask, W=[k.rmask])
        k.op("act", "activation", R=[k.ident], W=[k.identb], out=k.identb.ap, in_=k.ident.ap, func=AF.Copy)
        k.op("act", "activation", R=[k.bones], W=[k.bonesb], out=k.bonesb.ap, in_=k.bones.ap, func=AF.Copy)
        k.op("act", "activation", R=[k.masks], W=[k.masksb], out=k.masksb.ap, in_=k.masks.ap, func=AF.Copy)
        k.eps = k.tt(CA, 1)
        k.op("dve", "memset", W=[k.eps], ap=k.eps.ap, constant=EPS)
        k.cv = k.tt(CA, KT * 2)
        k.scT = k.tt(CB, KT * 2)
        k.dma("sp", v3(k.cv.ap, KT, 2), cvT, W=[k.cv])
        k.op("act", "activation", R=[k.cv], W=[k.scT], out=k.scT.ap, in_=k.cv.ap, func=AF.Silu)
        self.MODS = self.scratch("MODS", [L, 2, 3, D], F32)
        self.bMODS = TT(None, P.buf())
        for j in jobs:
            j.Atok = [k.tt(CA, j.NB * 8) for _ in range(2)]
            j.Etok = [k.tt(CA, j.NB * 8) for _ in range(2)]
            j.Fch = [k.tt(CA, j.NB, parts=8) for _ in range(2)]
            j.mfin = [k.tt(CA, 1, parts=8) for _ in range(2)]
        self.AF_ = Arena(P, "arena_f", cfg.get("AF", 19000), F32)
        self.AB_ = Arena(P, "arena_b", cfg.get("AB", 50000), BF16)

        if cfg.get("zero_yt"):
            z = k.tt(self.AB_, 512)
            k.op("dve", "memset", W=[z], ap=z.ap, constant=0.0)
            for j in jobs:
                for kt in range(16):
                    for t0 in range(0, j.N, 512):
                        n = min(512, j.N - t0)
                        k.dma("sp", j.YT[kt][:, t0:t0 + n], z[:, 0:n], R=[z], W=[j.bYT])
        for l in range(L):
            self.modvec(l)
        for l in range(L):
            for j in jobs:
                self.norm_phase(j, l, final=False)
            for j in jobs:
                if "A" in self.mixers:
                    self.mlstm_prepass(j, l)
                    for hp in range(4):
                        self.mlstm_unit(j, l, hp)
                if "B" in self.mixers:
                    for hp in range(4):
                        self.lru_unit(j, l, hp)
                if "C" in self.mixers:
                    for hp in range(4):
                        self.ret_unit(j, l, hp)
                if "D" in self.mixers:
                    self.rwkv_prepass(j, l)
                    for hp in range(4 if cfg.get("rw_stop", 9) > 0 else 0):
                        self.rwkv_unit(j, l, hp)
            for j in jobs:
                self.outproj_phase(j, l)
        for j in jobs:
            self.norm_phase(j, L, final=True)
        P.emit()
        P.close()
        return nc

    def new_phase(self):
        self.P.barrier()
        self.AF_.reset()
        self.AB_.reset()

    def modvec(self, l):
        k, P = self, self.P
        self.new_phase()
        AFa, ABa = self.AF_, self.AB_
        mrow = k.tt(AFa, 3 * D, parts=2)
        brow = k.tt(AFa, 3 * D, parts=2)
        grow = k.tt(AFa, D, parts=2)
        k.dma("sp", brow.ap, self.bmod2[l], W=[brow])
        k.dma("sp", grow.ap, self.ng2[l], W=[grow])
        wbufs = [k.tt(ABa, KT * 512) for _ in range(2)]
        for cb in range(6):
            wb = wbufs[cb % 2]
            k.dma("pool", v3(wb.ap, KT, 512), self.wmod[l][:, :, cb * 512:(cb + 1) * 512], W=[wb])
            bk = k.bank()
            for kt in range(KT):
                k.op("pe", "matmul", R=[wb, k.scT], W=[bk], out=bk[0:2, :],
                     lhsT=v3(k.scT.ap, KT, 2)[:, kt, :], rhs=v3(wb.ap, KT, 512)[:, kt, :],
                     start=(kt == 0), stop=(kt == KT - 1))
            k.op("dve", "tensor_tensor", R=[bk, brow], W=[mrow], out=mrow[:, cb * 512:(cb + 1) * 512],
                 in0=bk[0:2, :], in1=brow[:, cb * 512:(cb + 1) * 512], op=ALU.add)
        arow = k.tt(AFa, D, parts=2)
        k.op("dve", "scalar_tensor_tensor", R=[mrow, grow], W=[arow], out=arow.ap, in0=mrow[:, D:2 * D],
             scalar=1.0, in1=grow.ap, op0=ALU.add, op1=ALU.mult)
        k.dma("sp", self.MODS[l, :, 0, :], arow.ap, R=[arow], W=[self.bMODS])
        k.dma("sp", self.MODS[l, :, 1, :], mrow[:, 0:D], R=[mrow], W=[self.bMODS])
        k.dma("sp", self.MODS[l, :, 2, :], mrow[:, 2 * D:3 * D], R=[mrow], W=[self.bMODS])

    def bcast_row(self, dram_ap_1d, n):
        a = dram_ap_1d
        return bass.AP(a.tensor, a.offset, [[0, 128], [1, n]])

    def norm_phase(self, j, l, final):
        k = self
        self.new_phase()
        AFa, ABa = self.AF_, self.AB_
        GBk = min(4, j.NB)
        src = j.x_in if l == 0 else j.XR
        if final:
            modA = k.tt(AFa, D)
            k.dma("sp", modA.ap, self.fgb, W=[modA])
        else:
            modA, modS = k.tt(AFa, D), k.tt(AFa, D)
            k.dma("sp", modA.ap, self.bcast_row(self.MODS[l, j.g, 0, :], D), R=[self.bMODS], W=[modA])
            k.dma("sp", modS.ap, self.bcast_row(self.MODS[l, j.g, 1, :], D), R=[self.bMODS], W=[modS])
        xg = [k.tt(AFa, GBk * D) for _ in range(2)]
        t1 = [k.tt(AFa, D) for _ in range(2)]
        ss = [k.tt(AFa, GBk) for _ in range(2)]
        rs = [k.tt(AFa, GBk) for _ in range(2)]
        junk = k.tt(ABa, D)
        hb = [k.tt(ABa, D) for _ in range(2)]
        htg = [k.tt(ABa, KT * GBk * 128) for _ in range(2)]
        for gi in range(j.NB // GBk):
            x, s_, r_, ht = xg[gi % 2], ss[gi % 2], rs[gi % 2], htg[gi % 2]
            tok0 = gi * GBk * 128
            ntk = GBk * 128
            x3 = v3(x.ap, GBk, D)
            k.dma("sp", x3, src[tok0:tok0 + ntk, :].rearrange("(b p) d -> p b d", p=128),
                  R=[] if l == 0 else [j.bXR], W=[x])
            for b in range(GBk):
                k.op("act", "activation", R=[x], W=[junk, s_], out=junk.ap, in_=x3[:, b, :], func=AF.Square,
                     accum_out=s_[:, b:b + 1])
            k.op("act", "activation", R=[s_], W=[r_], out=r_.ap, in_=s_.ap, func=AF.Ln, scale=1.0 / D,
                 bias=k.eps.ap)
            k.op("act", "activation", R=[r_], W=[r_], out=r_.ap, in_=r_.ap, func=AF.Exp, scale=-0.5)
            for b in range(GBk):
                tt1 = t1[b % 2]
                k.op("dve", "scalar_tensor_tensor", R=[x, r_, modA], W=[tt1], out=tt1.ap, in0=x3[:, b, :],
                     scalar=r_[:, b:b + 1], in1=modA.ap, op0=ALU.mult, op1=ALU.mult)
                if final:
                    k.dma("sp", j.y_out[tok0 + b * 128: tok0 + (b + 1) * 128, :], tt1.ap, R=[tt1])
                    continue
                h = hb[b % 2]
                k.op("pool", "tensor_tensor", R=[tt1, modS], W=[h], out=h.ap, in0=tt1.ap, in1=modS.ap, op=ALU.add)
                bb = k.bbank()
                for kt in range(KT):
                    k.op("pe", "transpose", R=[h, k.identb], W=[bb], out=bb[:, kt * 128:(kt + 1) * 128],
                         in_=h[:, kt * 128:(kt + 1) * 128], identity=k.identb.ap)
                k.op("act", "activation", R=[bb], W=[ht], out=v3(ht.ap, KT, ntk)[:, :, b * 128:(b + 1) * 128],
                     in_=v3(bb.ap, KT, 128), func=AF.Copy)
            if not final:
                k.dma("sp", j.HT.rearrange("k p n -> p k n")[:, :, tok0:tok0 + ntk], v3(ht.ap, KT, ntk),
                      R=[ht], W=[j.bHT])

    def outproj_phase(self, j, l):
        k = self
        self.new_phase()
        AFa, ABa = self.AF_, self.AB_
        GBk = min(4, j.NB)
        src = j.x_in if l == 0 else j.XR
        modG = k.tt(AFa, D)
        k.dma("sp", modG.ap, self.bcast_row(self.MODS[l, j.g, 2, :], D), R=[self.bMODS], W=[modG])
        wo = k.tt(ABa, 16 * D)
        wo3 = v3(wo.ap, 16, D)
        for q in range(4):
            k.dma("pool", wo3[:, q * 4:(q + 1) * 4, :], self.wout[l][:, q * 4:(q + 1) * 4, :], W=[wo])
        xg = [k.tt(AFa, GBk * D) for _ in range(2)]
        tmp = [k.tt(AFa, 512) for _ in range(2)]
        ytg = [k.tt(ABa, 16 * GBk * 128) for _ in range(2)]
        for gi in range(j.NB // GBk):
            x, yt = xg[gi % 2], ytg[gi % 2]
            tok0 = gi * GBk * 128
            ntk = GBk * 128
            x3 = v3(x.ap, GBk, D)
            yt3 = v3(yt.ap, 16, ntk)
            k.dma("sp", x3, src[tok0:tok0 + ntk, :].rearrange("(b p) d -> p b d", p=128),
                  R=[] if l == 0 else [j.bXR], W=[x])
            k.dma("sp", yt3, j.YT.rearrange("k p n -> p k n")[:, :, tok0:tok0 + ntk], R=[j.bYT], W=[yt])
            for b in range(GBk):
                for hf in range(2):
                    bk = k.bank()
                    for kt in range(16):
                        k.op("pe", "matmul", R=[yt, wo], W=[bk], out=bk.ap, lhsT=yt3[:, kt, b * 128:(b + 1) * 128],
                             rhs=wo3[:, kt, hf * 512:(hf + 1) * 512], start=(kt == 0), stop=(kt == 15))
                    tm = tmp[hf]
                    k.op("dve", "tensor_tensor", R=[bk, modG], W=[tm], out=tm.ap, in0=bk.ap,
                         in1=modG[:, hf * 512:(hf + 1) * 512], op=ALU.mult)
                    k.op("pool", "tensor_tensor", R=[tm, x], W=[x], out=x3[:, b, hf * 512:(hf + 1) * 512],
                         in0=tm.ap, in1=x3[:, b, hf * 512:(hf + 1) * 512], op=ALU.add)
            k.dma("sp", j.XR[tok0:tok0 + ntk, :].rearrange("(b p) d -> p b d", p=128), x3, R=[x], W=[j.bXR])

    def stream_ht(self, j, fn):
        k = self
        nblk = (j.N + 511) // 512
        hts = [k.tt(self.AB_, KT * 512) for _ in range(2)]
        for b5 in range(nblk):
            tok0 = b5 * 512
            nt = min(512, j.N - tok0)
            ht = hts[b5 % 2]
            h3 = v3(ht.ap, KT, 512)[:, :, 0:nt]
            k.dma("sp", h3, j.HT.rearrange("k p n -> p k n")[:, :, tok0:tok0 + nt], R=[j.bHT], W=[ht])
            fn(b5, tok0, nt, h3, ht)

    def mm_fm(self, bk, out_ap, W, W3, c0, ncol, h3, ht, nt):
        for kt in range(KT):
            self.op("pe", "matmul", R=[W, ht], W=[bk], out=out_ap, lhsT=W3[:, kt, c0:c0 + ncol],
                    rhs=h3[:, kt, 0:nt], start=(kt == 0), stop=(kt == KT - 1))

    def mm_tm(self, bk, out_ap, W, W3, c0, ncol, h3, ht, t0):
        for kt in range(KT):
            self.op("pe", "matmul", R=[W, ht], W=[bk], out=out_ap, lhsT=h3[:, kt, t0:t0 + 128],
                    rhs=W3[:, kt, c0:c0 + ncol], start=(kt == 0), stop=(kt == KT - 1))

    def load_w(self, dram_ap, ncol):
        W = self.tt(self.AB_, KT * ncol)
        W3 = v3(W.ap, KT, ncol)
        self.dma("pool", W3, dram_ap, W=[W])
        return W, W3

    def segs(self, j, d):
        SB = min(self.SEGB, j.NB)
        lst = [(b0, min(SB, j.NB - b0)) for b0 in range(0, j.NB, SB)]
        return lst if d == 0 else lst[::-1]

    def yt_store(self, j, mixer, hp, tok0, ntk, yt_tt, yt_ap):
        kt = mixer * 4 + hp
        self.dma("sp", j.YT[kt][:, tok0:tok0 + ntk], yt_ap, R=[yt_tt], W=[j.bYT])
        if self.dbg:
            pass

    def lru_unit(self, j, l, hp):
        k = self
        self.new_phase()
        AFa, ABa = self.AF_, self.AB_
        N = j.N
        W, W3 = self.load_w(self.wB[l, hp], 256)
        lp = k.tt(AFa, 12)
        k.dma("sp", lp.ap, self.lruP[l, hp], W=[lp])
        gwf = k.tt(AFa, 512)
        k.dma("sp", v3(gwf.ap, 4, 128), self.lruG[l, hp], W=[gwf])
        gw = k.tt(ABa, 512)
        k.op("act", "activation", R=[gwf], W=[gw], out=gw.ap, in_=gwf.ap, func=AF.Copy)
        gw3 = v3(gw.ap, 4, 128)
        nsp = k.tt(AFa, 2)
        k.op("act", "activation", R=[lp], W=[nsp], out=nsp.ap, in_=lp[:, 9:11], func=AF.Exp, scale=-1.0)
        k.op("act", "activation", R=[nsp], W=[nsp], out=nsp.ap, in_=nsp.ap, func=AF.Ln, bias=1.0)
        k.op("dve", "tensor_scalar", R=[nsp], W=[nsp], out=nsp.ap, in0=nsp.ap, scalar1=-8.0, scalar2=None,
             op0=ALU.mult)
        XB = k.tt(AFa, N + 3)
        XC = k.tt(AFa, N)
        HF = k.tt(AFa, N)
        XCb = k.tt(ABa, N)
        SG = k.tt(ABa, N)
        k.op("dve", "memset", W=[XB], ap=XB[:, 0:1], constant=0.0)
        k.op("dve", "memset", W=[XB], ap=XB[:, N + 1:N + 3], constant=0.0)

        def blk(b5, tok0, nt, h3, ht):
            bk = k.bank()
            k.mm_fm(bk, bk[:, 0:nt], W, W3, 0, 128, h3, ht, nt)
            k.op("act", "activation", R=[bk], W=[XB], out=XB[:, 1 + tok0:1 + tok0 + nt], in_=bk[:, 0:nt], func=AF.Copy)
            bk2 = k.bank()
            k.mm_fm(bk2, bk2[:, 0:nt], W, W3, 128, 128, h3, ht, nt)
            k.op("act", "activation", R=[bk2], W=[SG], out=SG[:, tok0:tok0 + nt], in_=bk2[:, 0:nt], func=AF.Silu)
        self.stream_ht(j, blk)
        k.op("dve", "tensor_scalar", R=[XB, lp], W=[XC], out=XC.ap, in0=XB[:, 0:N], scalar1=lp[:, 0:1],
             scalar2=lp[:, 4:5], op0=ALU.mult, op1=ALU.add)
        for t in range(1, 4):
            k.op("dve", "scalar_tensor_tensor", R=[XB, lp, XC], W=[XC], out=XC.ap, in0=XB[:, t:t + N],
                 scalar=lp[:, t:t + 1], in1=XC.ap, op0=ALU.mult, op1=ALU.add)
        k.op("pool", "tensor_copy", R=[XC], W=[XCb], out=XCb.ap, in_=XC.ap)
        SEG = min(self.SEGB, j.NB) * 128
        tm = {n: [k.tt(AFa, SEG) for _ in range(2)] for n in ("sr", "si", "a", "u", "bt", "h")}
        carry = k.tt(AFa, 2)
        for d in range(2):
            if j.kind == "s":
                k.dma("sp", carry[:, d:d + 1], self.st_in["h"][l, d, hp * 128:(hp + 1) * 128].unsqueeze(1), W=[carry])
            else:
                k.op("dve", "memset", W=[carry], ap=carry[:, d:d + 1], constant=0.0)
        ytb = [k.tt(ABa, SEG) for _ in range(2)]
        for d in range(2):
            cur = (carry, carry[:, d:d + 1])
            for si_, (b0, nb) in enumerate(self.segs(j, d)):
                s0, n = b0 * 128, nb * 128
                T = {nme: tm[nme][si_ % 2] for nme in tm}
                bkr, bki = k.bank(), k.bank()
                k.op("pe", "matmul", R=[gw, XCb], W=[bkr], out=bkr[:, 0:n], lhsT=gw3[:, d * 2 + 0, :],
                     rhs=XCb[:, s0:s0 + n], start=True, stop=True)
                k.op("pe", "matmul", R=[gw, XCb], W=[bki], out=bki[:, 0:n], lhsT=gw3[:, d * 2 + 1, :],
                     rhs=XCb[:, s0:s0 + n], start=True, stop=True)
                k.op("act", "activation", R=[bkr, lp], W=[T["sr"]], out=T["sr"][:, 0:n], in_=bkr[:, 0:n],
                     func=AF.Sigmoid, bias=lp[:, 5 + d * 2:6 + d * 2])
                k.op("act", "activation", R=[bki, lp], W=[T["si"]], out=T["si"][:, 0:n], in_=bki[:, 0:n],
                     func=AF.Sigmoid, bias=lp[:, 6 + d * 2:7 + d * 2])
                k.op("act", "activation", R=[T["sr"], nsp], W=[T["a"]], out=T["a"][:, 0:n], in_=T["sr"][:, 0:n],
                     func=AF.Exp, scale=nsp[:, d:d + 1])
                k.op("dve", "scalar_tensor_tensor", R=[T["a"]], W=[T["u"]], out=T["u"][:, 0:n], in0=T["a"][:, 0:n],
                     scalar=0.99999994, in1=T["a"][:, 0:n], op0=ALU.min, op1=ALU.mult)
                k.op("act", "activation", R=[T["u"]], W=[T["bt"]], out=T["bt"][:, 0:n], in_=T["u"][:, 0:n],
                     func=AF.Ln, scale=-1.0, bias=1.0)
                k.op("act", "activation", R=[T["bt"]], W=[T["bt"]], out=T["bt"][:, 0:n], in_=T["bt"][:, 0:n],
                     func=AF.Exp, scale=0.5)
                k.op("dve", "tensor_tensor", R=[T["si"], XC], W=[T["si"]], out=T["si"][:, 0:n], in0=T["si"][:, 0:n],
                     in1=XC[:, s0:s0 + n], op=ALU.mult)
                k.op("pool", "tensor_tensor", R=[T["si"], T["bt"]], W=[T["bt"]], out=T["bt"][:, 0:n],
                     in0=T["si"][:, 0:n], in1=T["bt"][:, 0:n], op=ALU.mult)
                a_ap, b_ap, h_ap = T["a"][:, 0:n], T["bt"][:, 0:n], T["h"][:, 0:n]
                if d == 1:
                    a_ap, b_ap, h_ap = flip(a_ap), flip(b_ap), flip(h_ap)
                k.op("dve", "tensor_tensor_scan", R=[T["a"], T["bt"], cur[0]], W=[T["h"]], out=h_ap, data0=a_ap,
                     data1=b_ap, initial=cur[1], op0=ALU.mult, op1=ALU.add)
                cur = (T["h"], T["h"][:, n - 1:n] if d == 0 else T["h"][:, 0:1])
                if d == 0:
                    k.op("pool", "tensor_copy", R=[T["h"]], W=[HF], out=HF[:, s0:s0 + n], in_=T["h"][:, 0:n])
                else:
                    yt = ytb[si_ % 2]
                    k.op("dve", "tensor_tensor", R=[T["h"], HF], W=[T["u"]], out=T["u"][:, 0:n], in0=T["h"][:, 0:n],
                         in1=HF[:, s0:s0 + n], op=ALU.add)
                    k.op("pool", "tensor_tensor", R=[T["u"], SG], W=[yt], out=yt[:, 0:n], in0=T["u"][:, 0:n],
                         in1=SG[:, s0:s0 + n], op=ALU.mult)
                    k.yt_store(j, 1, hp, s0, n, yt, yt[:, 0:n])
            if j.kind == "p":
                k.dma("sp", self.so["h"][j.idx, l, d, hp * 128:(hp + 1) * 128].unsqueeze(1), cur[1], R=[cur[0]])

    def tail(self, j, mixer, hp, s0, nb, X, SGap, SG, tmps, post=None):
        k = self
        n = nb * 128
        sq, ss, yb, yt = tmps
        k.op("act", "activation", R=[X], W=[sq], out=sq[:, 0:n], in_=X[:, 0:n], func=AF.Square)
        k.op("dve", "tensor_reduce", R=[sq], W=[ss], out=ss[:, 0:nb * 2], in_=v3(sq[:, 0:n], nb * 2, 64),
             axis=AX.X, op=ALU.add)
        k.op("act", "activation", R=[ss], W=[ss], out=ss[:, 0:nb * 2], in_=ss[:, 0:nb * 2], func=AF.Ln,
             scale=1.0 / 64, bias=k.eps.ap)
        k.op("act", "activation", R=[ss], W=[ss], out=ss[:, 0:nb * 2], in_=ss[:, 0:nb * 2], func=AF.Exp, scale=-0.5)
        if SGap is not None:
            k.op("dve", "tensor_tensor", R=[X, ss], W=[sq], out=v3(sq[:, 0:n], nb * 2, 64),
                 in0=v3(X[:, 0:n], nb * 2, 64), in1=bc_last(ss[:, 0:nb * 2], 64), op=ALU.mult)
            k.op("pool", "tensor_tensor", R=[sq, SG], W=[yb], out=yb[:, 0:n], in0=sq[:, 0:n], in1=SGap, op=ALU.mult)
        else:
            k.op("dve", "tensor_tensor", R=[X, ss], W=[yb], out=v3(yb[:, 0:n], nb * 2, 64),
                 in0=v3(X[:, 0:n], nb * 2, 64), in1=bc_last(ss[:, 0:nb * 2], 64), op=ALU.mult)
        bb = k.bbank()
        for b in range(nb):
            k.op("pe", "transpose", R=[yb, k.identb], W=[bb], out=bb[:, b * 128:(b + 1) * 128],
                 in_=yb[:, b * 128:(b + 1) * 128], identity=k.identb.ap)
        if post is None:
            k.op("act", "activation", R=[bb], W=[yt], out=yt[:, 0:n], in_=bb[:, 0:n], func=AF.Copy)
        else:
            post(bb, yt, n)
        k.yt_store(j, mixer, hp, s0, n, yt, yt[:, 0:n])

    def tail_tmps(self, SEG, nbuf=2):
        k = self
        return [(k.tt(self.AF_, SEG), k.tt(self.AF_, 16), k.tt(self.AB_, SEG), k.tt(self.AB_, SEG)) for _ in range(nbuf)]

    def ret_unit(self, j, l, hp):
        k = self
        self.new_phase()
        AFa, ABa = self.AF_, self.AB_
        N, NB = j.N, j.NB
        rope = j.kind == "s"
        W, W3 = self.load_w(self.wC[l, hp], 768)
        rp = k.tt(AFa, 6)
        k.dma("sp", rp.ap, self.retP[l, hp], W=[rp])
        lg = k.tt(AFa, 6)
        k.op("act", "activation", R=[rp], W=[lg], out=lg.ap, in_=rp.ap, func=AF.Exp, scale=-1.0)
        k.op("act", "activation", R=[lg], W=[lg], out=lg.ap, in_=lg.ap, func=AF.Ln, bias=1.0)
        k.op("dve", "tensor_scalar", R=[lg], W=[lg], out=lg.ap, in0=lg.ap, scalar1=-1.0, scalar2=None, op0=ALU.mult)
        DT = [k.tt(AFa, 256) for _ in range(2)]
        XI = [k.tt(AFa, 128) for _ in range(2)]
        WK = [k.tt(AFa, 128) for _ in range(2)]
        dec = k.tt(AFa, 2)
        diff3, mask3, pos3 = v3(k.diff.ap, 2, 128), v3(k.masks.ap, 4, 128), v3(k.pos.ap, 4, 128)
        for d in range(2):
            for h in range(2):
                dt = v3(DT[d].ap, 2, 128)[:, h, :]
                k.op("act", "activation", R=[k.diff, lg], W=[DT[d]], out=dt, in_=diff3[:, d, :], func=AF.Exp,
                     scale=lg[:, 2 + d * 2 + h:3 + d * 2 + h])
                k.op("dve", "tensor_tensor", R=[DT[d], k.masks], W=[DT[d]], out=dt, in0=dt, in1=mask3[:, d, :],
                     op=ALU.mult)
            k.op("act", "activation", R=[k.pos, lg], W=[XI[d]], out=XI[d].ap, in_=pos3[:, d, :], func=AF.Exp,
                 scale=lg[:, d:d + 1])
            k.op("act", "activation", R=[k.pos, lg], W=[WK[d]], out=WK[d].ap, in_=pos3[:, 2 + d, :], func=AF.Exp,
                 scale=lg[:, d:d + 1])
        k.op("act", "activation", R=[lg], W=[dec], out=dec.ap, in_=lg[:, 0:2], func=AF.Exp, scale=128.0)
        QT, KT_ = k.tt(ABa, N), k.tt(ABa, N)
        V, SG = k.tt(ABa, N), k.tt(ABa, N)
        YF = k.tt(AFa, N)
        V3_, SG3 = v3(V.ap, NB, 128), v3(SG.ap, NB, 128)
        if rope:
            cs = [k.tt(AFa, 512) for _ in range(2)]
            sn = [k.tt(AFa, 512) for _ in range(2)]
            rt = [k.tt(AFa, 512) for _ in range(4)]

        def blk(b5, tok0, nt, h3, ht):
            if rope:
                c_, s_ = cs[b5 % 2], sn[b5 % 2]
                k.dma("sp", c_[:, 0:nt], self.rope[0][:, tok0:tok0 + nt], W=[c_])
                k.dma("sp", s_[:, 0:nt], self.rope[1][:, tok0:tok0 + nt], W=[s_])
            for qi, (dst, scl) in enumerate(((QT, 1.0), (KT_, 0.125))):
                bka = k.bank()
                k.mm_fm(bka, bka[:, 0:nt], W, W3, qi * 256, 128, h3, ht, nt)
                if not rope:
                    k.op("act", "activation", R=[bka], W=[dst], out=dst[:, tok0:tok0 + nt], in_=bka[:, 0:nt],
                         func=AF.Copy, scale=scl)
                    continue
                bkb = k.bank()
                k.mm_fm(bkb, bkb[:, 0:nt], W, W3, qi * 256 + 128, 128, h3, ht, nt)
                t1, t2 = rt[qi * 2], rt[qi * 2 + 1]
                k.op("dve", "scalar_tensor_tensor", R=[bka, c_], W=[t1], out=t1[:, 0:nt], in0=bka[:, 0:nt], scalar=scl,
                     in1=c_[:, 0:nt], op0=ALU.mult, op1=ALU.mult)
                k.op("dve", "scalar_tensor_tensor", R=[bkb, s_], W=[t2], out=t2[:, 0:nt], in0=bkb[:, 0:nt], scalar=scl,
                     in1=s_[:, 0:nt], op0=ALU.mult, op1=ALU.mult)
                k.op("pool", "tensor_tensor", R=[t1, t2], W=[dst], out=dst[:, tok0:tok0 + nt], in0=t1[:, 0:nt],
                     in1=t2[:, 0:nt], op=ALU.add)
            for jj in range(0, nt // 128, 2):
                nj = min(2, nt // 128 - jj)
                bk = k.bank()
                for q in range(nj):
                    k.mm_tm(bk, bk[:, q * 256:(q + 1) * 256], W, W3, 512, 256, h3, ht, (jj + q) * 128)
                blk0 = tok0 // 128 + jj
                pv = v3(bk[:, 0:nj * 256], nj, 256)
                k.op("act", "activation", R=[bk], W=[V], out=V3_[:, blk0:blk0 + nj, :], in_=pv[:, :, 0:128], func=AF.Copy)
                k.op("act", "activation", R=[bk], W=[SG], out=SG3[:, blk0:blk0 + nj, :], in_=pv[:, :, 128:256],
                     func=AF.Silu)
        self.stream_ht(j, blk)
        SB = min(self.SEGB, NB)
        SEG = SB * 128
        qs_ = [k.tt(ABa, SEG) for _ in range(2)]
        ks_ = [k.tt(ABa, SEG) for _ in range(2)]
        ktk = [k.tt(ABa, SEG) for _ in range(2)]
        RS = [k.tt(ABa, SB * 64) for _ in range(2)]
        Sm = [[k.tt(ABa, SEG) for _ in range(2)] for _ in range(2)]
        ysum = [k.tt(AFa, SEG) for _ in range(2)]
        ttm = self.tail_tmps(SEG)
        Rst = k.tt(AFa, 64)
        for d in range(2):
            if j.kind == "s":
                k.dma("sp", Rst.ap, self.st_in["r"][l, d, 2 * hp:2 * hp + 2].rearrange("h a v -> (h a) v"), W=[Rst])
            else:
                k.op("dve", "memset", W=[Rst], ap=Rst.ap, constant=0.0)
            for si_, (b0, nb) in enumerate(self.segs(j, d)):
                s0, n = b0 * 128, nb * 128
                pq, pk, pt, prs = qs_[si_ % 2], ks_[si_ % 2], ktk[si_ % 2], RS[si_ % 2]
                k.op("dve", "tensor_tensor", R=[QT, XI[d]], W=[pq], out=v3(pq[:, 0:n], nb, 128),
                     in0=v3(QT[:, s0:s0 + n], nb, 128), in1=bc_mid(XI[d].ap, nb), op=ALU.mult)
                k.op("pool", "tensor_tensor", R=[KT_, WK[d]], W=[pk], out=v3(pk[:, 0:n], nb, 128),
                     in0=v3(KT_[:, s0:s0 + n], nb, 128), in1=bc_mid(WK[d].ap, nb), op=ALU.mult)
                bb = k.bbank()
                for b in range(nb):
                    k.op("pe", "transpose", R=[pk, k.identb], W=[bb], out=bb[:, b * 128:(b + 1) * 128],
                         in_=pk[:, b * 128:(b + 1) * 128], identity=k.identb.ap)
                k.op("act", "activation", R=[bb], W=[pt], out=pt[:, 0:n], in_=bb[:, 0:n], func=AF.Copy)
                bdr = k.bank()
                for b in range(nb):
                    for h in range(2):
                        k.op("pe", "matmul", R=[pt, V], W=[bdr], out=bdr[h * 64:(h + 1) * 64, b * 64:(b + 1) * 64],
                             lhsT=pt[:, b * 128 + h * 64:b * 128 + (h + 1) * 64], rhs=V3_[:, b0 + b, h * 64:(h + 1) * 64],
                             start=True, stop=True)
                order = range(nb) if d == 0 else range(nb - 1, -1, -1)
                for b in order:
                    k.op("pool", "tensor_copy", R=[Rst], W=[prs], out=prs[:, b * 64:(b + 1) * 64], in_=Rst.ap)
                    k.op("dve", "scalar_tensor_tensor", R=[Rst, dec, bdr], W=[Rst], out=Rst.ap, in0=Rst.ap,
                         scalar=dec[:, d:d + 1], in1=bdr[:, b * 64:(b + 1) * 64], op0=ALU.mult, op1=ALU.add)
                for h in range(2):
                    bst = k.bank()
                    for b in range(nb):
                        tk = slice(s0 + b * 128, s0 + (b + 1) * 128)
                        k.op("pe", "matmul", R=[KT_, QT], W=[bst], out=bst[:, b * 128:(b + 1) * 128],
                             lhsT=KT_[h * 64:(h + 1) * 64, tk], rhs=QT[h * 64:(h + 1) * 64, tk], start=True, stop=True)
                    sm = Sm[si_ % 2][h]
                    k.op("dve", "tensor_tensor", R=[bst, DT[d]], W=[sm], out=v3(sm[:, 0:n], nb, 128),
                         in0=v3(bst[:, 0:n], nb, 128), in1=bc_mid(v3(DT[d].ap, 2, 128)[:, h, :], nb), op=ALU.mult)
                bo = k.bank()
                for b in range(nb):
                    for h in range(2):
                        sm = Sm[si_ % 2][h]
                        oo = bo[:, b * 128 + h * 64:b * 128 + (h + 1) * 64]
                        k.op("pe", "matmul", R=[sm, V], W=[bo], out=oo, lhsT=sm[:, b * 128:(b + 1) * 128],
                             rhs=V3_[:, b0 + b, h * 64:(h + 1) * 64], start=True, stop=False)
                        k.op("pe", "matmul", R=[pq, prs], W=[bo], out=oo, lhsT=pq[h * 64:(h + 1) * 64, b * 128:(b + 1) * 128],
                             rhs=prs[h * 64:(h + 1) * 64, b * 64:(b + 1) * 64], start=False, stop=True)
                if d == 0:
                    k.op("act", "activation", R=[bo], W=[YF], out=YF[:, s0:s0 + n], in_=bo[:, 0:n], func=AF.Copy)
                else:
                    ys_ = ysum[si_ % 2]
                    k.op("dve", "tensor_tensor", R=[bo, YF], W=[ys_], out=ys_[:, 0:n], in0=bo[:, 0:n],
                         in1=YF[:, s0:s0 + n], op=ALU.add)
                    self.tail(j, 2, hp, s0, nb, ys_, SG[:, s0:s0 + n], SG, ttm[si_ % 2])
            if j.kind == "p":
                k.dma("sp", self.so["r"][j.idx, l, d, 2 * hp:2 * hp + 2].rearrange("h a v -> (h a) v"), Rst.ap, R=[Rst])

    def mlstm_prepass(self, j, l):
        k = self
        self.new_phase()
        AFa, ABa = self.AF_, self.AB_
        N, NB = j.N, j.NB
        W, W3 = self.load_w(self.wAg[l], 32)
        gbt = k.tt(AFa, 4, parts=8)
        ngb = k.tt(AFa, 4, parts=8)
        k.dma("sp", gbt.ap, self.gbA[l], W=[gbt])
        k.op("dve", "tensor_scalar", R=[gbt], W=[ngb], out=ngb.ap, in0=gbt.ap, scalar1=-1.0, scalar2=None, op0=ALU.mult)
        GS = [k.tt(AFa, NB, parts=8) for _ in range(2)]
        BL = [k.tt(AFa, NB, parts=8) for _ in range(2)]
        rm3 = v3(k.rmask.ap, 2, 512)
        tsp = [k.tt(AFa, 512, parts=8) for _ in range(2)]
        tb = [k.tt(AFa, 512, parts=8) for _ in range(2)]
        tg = [k.tt(AFa, 512, parts=8) for _ in range(2)]
        tpm = [k.tt(AFa, 512, parts=8) for _ in range(2)]

        def blk(b5, tok0, nt, h3, ht):
            nch = nt // 128
            c0 = tok0 // 128
            bks = []
            for gi in range(4):
                bk = k.bank()
                k.mm_fm(bk, bk[0:8, 0:nt], W, W3, gi * 8, 8, h3, ht, nt)
                bks.append(bk)
            for d in range(2):
                ig, fg = bks[2 * d], bks[2 * d + 1]
                sp_, b_, g_, pm_ = tsp[d], tb[d], tg[d], tpm[d]
                fl = (lambda a: a) if d == 0 else flip
                k.op("act", "activation", R=[fg, ngb], W=[sp_], out=sp_[:, 0:nt], in_=fg[0:8, 0:nt], func=AF.Exp,
                     scale=-1.0, bias=ngb[:, 2 * d + 1:2 * d + 2])
                k.op("act", "activation", R=[sp_], W=[sp_], out=sp_[:, 0:nt], in_=sp_[:, 0:nt], func=AF.Ln, bias=1.0)
                k.op("dve", "tensor_tensor_scan", R=[sp_, k.rmask], W=[b_], out=fl(b_[:, 0:nt]), data0=rm3[0:8, 0, 0:nt],
                     data1=fl(sp_[:, 0:nt]), initial=0.0, op0=ALU.mult, op1=ALU.subtract)
                k.op("dve", "scalar_tensor_tensor", R=[ig, gbt, b_], W=[g_], out=g_[:, 0:nt], in0=ig[0:8, 0:nt],
                     scalar=gbt[:, 2 * d:2 * d + 1], in1=b_[:, 0:nt], op0=ALU.add, op1=ALU.subtract)
                k.op("dve", "tensor_tensor_scan", R=[g_, k.rmask], W=[pm_], out=fl(pm_[:, 0:nt]), data0=rm3[0:8, 1, 0:nt],
                     data1=fl(g_[:, 0:nt]), initial=0.0, op0=ALU.add, op1=ALU.max)
                e = 127 if d == 0 else 0
                k.op("pool", "tensor_copy", R=[pm_], W=[GS[d]], out=GS[d][:, c0:c0 + nch],
                     in_=v3(pm_[:, 0:nt], nch, 128)[:, :, e])
                k.op("pool", "tensor_copy", R=[b_], W=[BL[d]], out=BL[d][:, c0:c0 + nch],
                     in_=v3(b_[:, 0:nt], nch, 128)[:, :, e])
                k.dma("sp", j.GB[d, 0, :, tok0:tok0 + nt], g_[:, 0:nt], R=[g_], W=[j.bGB])
                k.dma("sp", j.GB[d, 1, :, tok0:tok0 + nt], b_[:, 0:nt], R=[b_], W=[j.bGB])
        self.stream_ht(j, blk)
        ML = [k.tt(AFa, NB, parts=8) for _ in range(2)]
        for d in range(2):
            fl = (lambda a: a) if d == 0 else flip
            m0 = k.tt(AFa, 1, parts=8)
            if j.kind == "s":
                k.dma("sp", m0.ap, self.st_in["m"][l, d, :].unsqueeze(1), W=[m0])
            else:
                k.op("dve", "memset", W=[m0], ap=m0.ap, constant=0.0)
            Mall, MP = k.tt(AFa, NB, parts=8), k.tt(AFa, NB, parts=8)
            k.op("dve", "tensor_tensor_scan", R=[GS[d], BL[d], m0], W=[Mall], out=fl(Mall.ap), data0=fl(GS[d].ap),
                 data1=fl(BL[d].ap), initial=m0.ap, op0=ALU.max, op1=ALU.add)
            if d == 0:
                k.op("pool", "tensor_copy", R=[m0], W=[MP], out=MP[:, 0:1], in_=m0.ap)
                if NB > 1:
                    k.op("pool", "tensor_copy", R=[Mall], W=[MP], out=MP[:, 1:NB], in_=Mall[:, 0:NB - 1])
                k.op("pool", "tensor_copy", R=[Mall], W=[j.mfin[d]], out=j.mfin[d].ap, in_=Mall[:, NB - 1:NB])
            else:
                k.op("pool", "tensor_copy", R=[m0], W=[MP], out=MP[:, NB - 1:NB], in_=m0.ap)
                if NB > 1:
                    k.op("pool", "tensor_copy", R=[Mall], W=[MP], out=MP[:, 0:NB - 1], in_=Mall[:, 1:NB])
                k.op("pool", "tensor_copy", R=[Mall], W=[j.mfin[d]], out=j.mfin[d].ap, in_=Mall[:, 0:1])
            k.op("dve", "tensor_tensor", R=[MP, GS[d]], W=[ML[d]], out=ML[d].ap, in0=MP.ap, in1=GS[d].ap, op=ALU.max)
            k.op("dve", "tensor_tensor", R=[MP, ML[d]], W=[MP], out=MP.ap, in0=MP.ap, in1=ML[d].ap, op=ALU.subtract)
            k.op("act", "activation", R=[MP], W=[j.Fch[d]], out=j.Fch[d].ap, in_=MP.ap, func=AF.Exp)
            if j.kind == "p":
                k.dma("sp", self.so["m"][j.idx, l, d, :].unsqueeze(1), j.mfin[d].ap, R=[j.mfin[d]])
        ta = [k.tt(AFa, 512, parts=8) for _ in range(2)]
        te = [k.tt(AFa, 512, parts=8) for _ in range(2)]
        for b5 in range((N + 511) // 512):
            tok0 = b5 * 512
            nt = min(512, N - tok0)
            nch, c0 = nt // 128, tok0 // 128
            for d in range(2):
                a_, e_ = ta[d], te[d]
                k.dma("sp", a_[:, 0:nt], j.GB[d, 0, :, tok0:tok0 + nt], R=[j.bGB], W=[a_])
                k.dma("sp", e_[:, 0:nt], j.GB[d, 1, :, tok0:tok0 + nt], R=[j.bGB], W=[e_])
                mlb = bc_last(ML[d][:, c0:c0 + nch], 128)
                k.op("dve", "tensor_tensor", R=[a_, ML[d]], W=[a_], out=v3(a_[:, 0:nt], nch, 128),
                     in0=v3(a_[:, 0:nt], nch, 128), in1=mlb, op=ALU.subtract)
                k.op("act", "activation", R=[a_], W=[a_], out=a_[:, 0:nt], in_=a_[:, 0:nt], func=AF.Exp)
                k.op("dve", "tensor_tensor", R=[e_, ML[d]], W=[e_], out=v3(e_[:, 0:nt], nch, 128),
                     in0=v3(e_[:, 0:nt], nch, 128), in1=mlb, op=ALU.add)
                k.op("act", "activation", R=[e_], W=[e_], out=e_[:, 0:nt], in_=e_[:, 0:nt], func=AF.Exp, scale=-1.0)
                bk = k.bank()
                for c in range(nch):
                    k.op("pe", "transpose", R=[a_, k.ident], W=[bk], out=bk[:, c * 8:(c + 1) * 8],
                         in_=a_[:, c * 128:(c + 1) * 128], identity=k.ident[0:8, 0:8])
                    k.op("pe", "transpose", R=[e_, k.ident], W=[bk], out=bk[:, 64 + c * 8:64 + (c + 1) * 8],
                         in_=e_[:, c * 128:(c + 1) * 128], identity=k.ident[0:8, 0:8])
                k.op("act", "activation", R=[bk], W=[j.Atok[d]], out=j.Atok[d][:, c0 * 8:(c0 + nch) * 8],
                     in_=bk[:, 0:nch * 8], func=AF.Copy)
                k.op("act", "activation", R=[bk], W=[j.Etok[d]], out=j.Etok[d][:, c0 * 8:(c0 + nch) * 8],
                     in_=bk[:, 64:64 + nch * 8], func=AF.Copy)

    def mlstm_unit(self, j, l, hp):
        k = self
        self.new_phase()
        AFa, ABa = self.AF_, self.AB_
        N, NB = j.N, j.NB
        W, W3 = self.load_w(self.wA[l, hp], 640)
        QT, KT_ = k.tt(ABa, N), k.tt(ABa, N)
        Ktok = k.tt(ABa, N)
        VA0 = k.tt(ABa, NB * 130)
        SO, SG = k.tt(ABa, N), k.tt(ABa, N)
        HF = k.tt(AFa, N)
        VA04 = VA0.ap.rearrange("p (b h c) -> p b h c", b=NB, h=2, c=65)
        SO3, SG3 = v3(SO.ap, NB, 128), v3(SG.ap, NB, 128)
        k.op("dve", "memset", W=[VA0], ap=v3(VA0.ap, NB * 2, 65)[:, :, 64:65], constant=1.0)

        def blk(b5, tok0, nt, h3, ht):
            for qi, (dst, scl) in enumerate(((QT, 1.0), (KT_, 0.125))):
                bka = k.bank()
                k.mm_fm(bka, bka[:, 0:nt], W, W3, qi * 128, 128, h3, ht, nt)
                k.op("act", "activation", R=[bka], W=[dst], out=dst[:, tok0:tok0 + nt], in_=bka[:, 0:nt],
                     func=AF.Copy, scale=scl)
            bb = k.bbank()
            for b in range(nt // 128):
                k.op("pe", "transpose", R=[KT_, k.identb], W=[bb], out=bb[:, b * 128:(b + 1) * 128],
                     in_=KT_[:, tok0 + b * 128:tok0 + (b + 1) * 128], identity=k.identb.ap)
            k.op("act", "activation", R=[bb], W=[Ktok], out=Ktok[:, tok0:tok0 + nt], in_=bb[:, 0:nt], func=AF.Copy)
            for jj in range(nt // 128):
                bk = k.bank()
                k.mm_tm(bk, bk[:, 0:384], W, W3, 256, 384, h3, ht, jj * 128)
                bi = tok0 // 128 + jj
                k.op("act", "activation", R=[bk], W=[VA0], out=VA04[:, bi, :, 0:64], in_=v3(bk[:, 0:128], 2, 64),
                     func=AF.Copy)
                k.op("act", "activation", R=[bk], W=[SO], out=SO3[:, bi, :], in_=bk[:, 128:256], func=AF.Sigmoid)
                k.op("act", "activation", R=[bk], W=[SG], out=SG3[:, bi, :], in_=bk[:, 256:384], func=AF.Silu)
        self.stream_ht(j, blk)
        SB = min(self.SEGB, NB)
        SEG = SB * 128
        VA = [k.tt(ABa, SB * 130) for _ in range(2)]
        CS = [k.tt(ABa, SB * 65) for _ in range(2)]
        Sm = [[k.tt(ABa, SEG) for _ in range(2)] for _ in range(2)]
        den = [k.tt(AFa, 8) for _ in range(2)]
        hd = [k.tt(AFa, 256) for _ in range(2)]
        X = [k.tt(AFa, SEG) for _ in range(2)]
        ttm = self.tail_tmps(SEG)
        C = k.tt(AFa, 65)
        Fbc = k.tt(AFa, NB)
        mask3 = v3(k.masksb.ap, 4, 128)
        for d in range(2):
            bkf = k.bank()
            k.op("pe", "matmul", R=[k.sel, j.Fch[d]], W=[bkf], out=bkf[:, 0:NB], lhsT=v3(k.sel.ap, 4, 128)[:, hp, :],
                 rhs=j.Fch[d].ap, start=True, stop=True)
            k.op("act", "activation", R=[bkf], W=[Fbc], out=Fbc.ap, in_=bkf[:, 0:NB], func=AF.Copy)
            if j.kind == "s":
                k.dma("sp", C[:, 0:64], self.st_in["c"][l, d, 2 * hp:2 * hp + 2].rearrange("h a v -> (h a) v"), W=[C])
                k.dma("sp", C[:, 64:65], self.st_in["n"][l, d, 2 * hp:2 * hp + 2].rearrange("h (a o) -> (h a) o", o=1),
                      W=[C])
            else:
                k.op("dve", "memset", W=[C], ap=C.ap, constant=0.0)
            A3 = v3(j.Atok[d].ap, NB, 8)
            E3 = v3(j.Etok[d].ap, NB, 8)
            for si_, (b0, nb) in enumerate(self.segs(j, d)):
                s0, n = b0 * 128, nb * 128
                va, cs = VA[si_ % 2], CS[si_ % 2]
                va4 = va[:, 0:nb * 130].rearrange("p (b h c) -> p b h c", b=nb, h=2, c=65)
                k.op("dve", "tensor_tensor", R=[VA0, j.Atok[d]], W=[va], out=va4, in0=VA04[:, b0:b0 + nb, :, :],
                     in1=bc_last(A3[:, b0:b0 + nb, 2 * hp:2 * hp + 2], 65), op=ALU.mult)
                bdc = k.bank()
                for b in range(nb):
                    for h in range(2):
                        k.op("pe", "matmul", R=[Ktok, va], W=[bdc], out=bdc[h * 64:(h + 1) * 64, b * 65:(b + 1) * 65],
                             lhsT=Ktok[:, (b0 + b) * 128 + h * 64:(b0 + b) * 128 + (h + 1) * 64], rhs=va4[:, b, h, :],
                             start=True, stop=True)
                order = range(nb) if d == 0 else range(nb - 1, -1, -1)
                for b in order:
                    c = b0 + b
                    k.op("dve", "tensor_scalar", R=[C, Fbc], W=[cs], out=cs[:, b * 65:(b + 1) * 65], in0=C.ap,
                         scalar1=Fbc[:, c:c + 1], scalar2=None, op0=ALU.mult)
                    k.op("dve", "scalar_tensor_tensor", R=[C, Fbc, bdc], W=[C], out=C.ap, in0=C.ap, scalar=Fbc[:, c:c + 1],
                         in1=bdc[:, b * 65:(b + 1) * 65], op0=ALU.mult, op1=ALU.add)
                for h in range(2):
                    bst = k.bank()
                    for b in range(nb):
                        tk = slice(s0 + b * 128, s0 + (b + 1) * 128)
                        k.op("pe", "matmul", R=[KT_, QT], W=[bst], out=bst[:, b * 128:(b + 1) * 128],
                             lhsT=KT_[h * 64:(h + 1) * 64, tk], rhs=QT[h * 64:(h + 1) * 64, tk], start=True, stop=True)
                    sm = Sm[si_ % 2][h]
                    k.op("dve", "tensor_tensor", R=[bst, k.masksb], W=[sm], out=v3(sm[:, 0:n], nb, 128),
                         in0=v3(bst[:, 0:n], nb, 128), in1=bc_mid(mask3[:, d, :], nb), op=ALU.mult)
                xx = X[si_ % 2]
                for p0 in range(0, nb, 2):
                    n2 = min(2, nb - p0)
                    bo = k.bank()
                    for b in range(p0, p0 + n2):
                        for h in range(2):
                            sm = Sm[si_ % 2][h]
                            off = (b - p0) * 130 + h * 65
                            oo = bo[:, off:off + 65]
                            k.op("pe", "matmul", R=[sm, va], W=[bo], out=oo, lhsT=sm[:, b * 128:(b + 1) * 128],
                                 rhs=va4[:, b, h, :], start=True, stop=False)
                            k.op("pe", "matmul", R=[QT, cs], W=[bo], out=oo,
                                 lhsT=QT[h * 64:(h + 1) * 64, s0 + b * 128:s0 + (b + 1) * 128],
                                 rhs=cs[h * 64:(h + 1) * 64, b * 65:(b + 1) * 65], start=False, stop=True)
                    bo4 = bo[:, 0:n2 * 130].rearrange("p (b h c) -> p b h c", b=n2, h=2, c=65)
                    dn = den[(p0 // 2) % 2]
                    dn3 = v3(dn[:, 0:n2 * 2], n2, 2)
                    k.op("act", "activation", R=[bo], W=[dn], out=dn3, in_=bo4[:, :, :, 64], func=AF.Abs)
                    k.op("dve", "tensor_tensor", R=[dn, j.Etok[d]], W=[dn], out=dn3, in0=dn3,
                         in1=E3[:, b0 + p0:b0 + p0 + n2, 2 * hp:2 * hp + 2], op=ALU.max)
                    k.op("dve", "reciprocal", R=[dn], W=[dn], out=dn[:, 0:n2 * 2], in_=dn[:, 0:n2 * 2])
                    t0 = s0 + p0 * 128
                    if d == 0:
                        k.op("dve", "tensor_tensor", R=[bo, dn], W=[HF],
                             out=HF[:, t0:t0 + n2 * 128].rearrange("p (b h c) -> p b h c", b=n2, h=2, c=64),
                             in0=bo4[:, :, :, 0:64], in1=bc_last(dn3, 64), op=ALU.mult)
                    else:
                        hh = hd[(p0 // 2) % 2]
                        k.op("dve", "tensor_tensor", R=[bo, dn], W=[hh],
                             out=hh[:, 0:n2 * 128].rearrange("p (b h c) -> p b h c", b=n2, h=2, c=64),
                             in0=bo4[:, :, :, 0:64], in1=bc_last(dn3, 64), op=ALU.mult)
                        k.op("pool", "tensor_tensor", R=[hh, HF], W=[hh], out=hh[:, 0:n2 * 128], in0=hh[:, 0:n2 * 128],
                             in1=HF[:, t0:t0 + n2 * 128], op=ALU.add)
                        k.op("pool", "tensor_tensor", R=[hh, SO], W=[xx], out=xx[:, p0 * 128:(p0 + n2) * 128],
                             in0=hh[:, 0:n2 * 128], in1=SO[:, t0:t0 + n2 * 128], op=ALU.mult)
                if d == 1:
                    self.tail(j, 0, hp, s0, nb, xx, SG[:, s0:s0 + n], SG, ttm[si_ % 2])
            if j.kind == "p":
                k.dma("sp", self.so["c"][j.idx, l, d, 2 * hp:2 * hp + 2].rearrange("h a v -> (h a) v"), C[:, 0:64], R=[C])
                k.dma("sp", self.so["n"][j.idx, l, d, 2 * hp:2 * hp + 2].rearrange("h (a o) -> (h a) o", o=1),
                      C[:, 64:65], R=[C])

    def shiftmix(self, j, S, c0_ap, cd_ap, co, s0, n, out):
        k = self
        N = j.N
        k.op("dve", "tensor_scalar", R=[S, co], W=[out], out=out[:, 0:n], in0=S[:, s0:s0 + n], scalar1=c0_ap,
             scalar2=None, op0=ALU.mult)

        def acc(o_ap, i_ap, dirn):
            k.op("dve", "scalar_tensor_tensor", R=[S, co, out], W=[out], out=o_ap, in0=i_ap,
                 scalar=cd_ap[:, dirn:dirn + 1], in1=o_ap, op0=ALU.mult, op1=ALU.add)
        if j.kind == "s":
            o3 = v3(out[:, 0:n], n // 64, 64)
            s3 = v3(S[:, s0:s0 + n], n // 64, 64)
            acc(o3[:, :, 1:64], s3[:, :, 0:63], 0)
            acc(o3[:, :, 0:63], s3[:, :, 1:64], 1)
            i0 = 0 if s0 >= 64 else 64
            if n > i0:
                acc(out[:, i0:n], S[:, s0 + i0 - 64:s0 + n - 64], 2)
            i1 = n if s0 + n + 64 <= N else n - 64
            if i1 > 0:
                acc(out[:, 0:i1], S[:, s0 + 64:s0 + i1 + 64], 3)
        else:
            i0 = 0 if s0 > 0 else 1
            acc(out[:, i0:n], S[:, s0 + i0 - 1:s0 + n - 1], 0)
            i1 = n if s0 + n < N else n - 1
            acc(out[:, 0:i1], S[:, s0 + 1:s0 + i1 + 1], 1)

    def rwkv_prepass(self, j, l):
        k = self
        self.new_phase()
        AFa, ABa = self.AF_, self.AB_
        N, NB = j.N, j.NB
        if not hasattr(j, "LW"):
            j.LW = self.scratch(f"LW_{j.name}", [128, N], BF16)
            j.bLW = TT(None, self.P.buf())
        W, W3 = self.load_w(self.wDl[l], 128)
        mu = k.tt(AFa, 2)
        shl = k.tt(AFa, 4)
        k.dma("sp", mu.ap, self.rwLP[l], W=[mu])
        k.dma("sp", shl.ap, self.c_shl[j.kind], W=[shl])
        co = k.tt(AFa, 8)
        k.op("dve", "tensor_scalar", R=[mu], W=[co], out=co[:, 0:1], in0=mu[:, 0:1], scalar1=-1.0, scalar2=1.0,
             op0=ALU.mult, op1=ALU.add)
        k.op("dve", "tensor_scalar", R=[shl, mu], W=[co], out=co[:, 1:5], in0=shl.ap, scalar1=mu[:, 0:1], scalar2=None,
             op0=ALU.mult)
        S = k.tt(ABa, N)

        def blk(b5, tok0, nt, h3, ht):
            bk = k.bank()
            k.mm_fm(bk, bk[:, 0:nt], W, W3, 0, 128, h3, ht, nt)
            k.op("act", "activation", R=[bk], W=[S], out=S[:, tok0:tok0 + nt], in_=bk[:, 0:nt], func=AF.Copy)
        self.stream_ht(j, blk)
        SEG = min(self.SEGB, NB) * 128
        xo = [k.tt(AFa, SEG) for _ in range(2)]
        lo = [k.tt(ABa, SEG) for _ in range(2)]
        for si_, (b0, nb) in enumerate(self.segs(j, 0)):
            s0, n = b0 * 128, nb * 128
            x, lw = xo[si_ % 2], lo[si_ % 2]
            self.shiftmix(j, S, co[:, 0:1], co[:, 1:5], co, s0, n, x)
            k.op("act", "activation", R=[x], W=[lw], out=lw[0:64, 0:n], in_=x[0:64, 0:n], func=AF.Tanh)
            k.op("act", "activation", R=[x], W=[lw], out=lw[64:128, 0:n], in_=x[64:128, 0:n], func=AF.Copy)
            k.dma("sp", j.LW[:, s0:s0 + n], lw[:, 0:n], R=[lw], W=[j.bLW])

    def rwkv_unit(self, j, l, hp):
        k = self
        self.new_phase()
        AFa, ABa = self.AF_, self.AB_
        N, NB = j.N, j.NB
        CE = 0.6065306597126334
        rw = k.tt(AFa, 12)
        k.dma("sp", rw.ap, self.rwP[l, hp], W=[rw])
        shm = k.tt(AFa, 12)
        k.dma("sp", v3(shm.ap, 3, 4), self.c_shm[j.kind][hp], W=[shm])
        co = k.tt(AFa, 16)
        k.op("dve", "tensor_scalar", R=[rw], W=[co], out=co[:, 0:3], in0=rw[:, 0:3], scalar1=-1.0, scalar2=1.0,
             op0=ALU.mult, op1=ALU.add)
        k.op("dve", "tensor_tensor", R=[shm, rw], W=[co], out=v3(co[:, 4:16], 3, 4), in0=v3(shm.ap, 3, 4),
             in1=bc_last(rw[:, 0:3], 4), op=ALU.mult)
        w2 = k.tt(ABa, 256)
        for d in range(2):
            k.dma("pool", w2[:, d * 128:(d + 1) * 128], self.rwW2[l, d][:, hp * 128:(hp + 1) * 128], W=[w2])
        Rr, Kk, KK, BON, SGT, Vtok = (k.tt(ABa, N) for _ in range(6))
        YF = k.tt(AFa, N)
        Sst = k.tt(AFa, 64)
        mark_f, mark_b = AFa.pos, ABa.pos
        Sx = [k.tt(ABa, N) for _ in range(3)]
        W, W3 = self.load_w(self.wD[l, hp], 512)

        def blk(b5, tok0, nt, h3, ht):
            for c in range(3):
                bk = k.bank()
                k.mm_fm(bk, bk[:, 0:nt], W, W3, c * 128, 128, h3, ht, nt)
                k.op("act", "activation", R=[bk], W=[Sx[c]], out=Sx[c][:, tok0:tok0 + nt], in_=bk[:, 0:nt], func=AF.Copy)
            bk = k.bank()
            k.mm_fm(bk, bk[:, 0:nt], W, W3, 384, 128, h3, ht, nt)
            k.op("act", "activation", R=[bk], W=[SGT], out=SGT[:, tok0:tok0 + nt], in_=bk[:, 0:nt], func=AF.Silu)
        self.stream_ht(j, blk)
        SB = min(self.SEGB, NB)
        SEG = SB * 128
        X = [[k.tt(AFa, SEG) for _ in range(3)] for _ in range(2)]
        t1 = [k.tt(AFa, SEG) for _ in range(2)]
        t2 = [k.tt(AFa, SEG) for _ in range(2)]
        vb = [k.tt(ABa, SEG)] * 2
        for si_, (b0, nb) in enumerate(self.segs(j, 0)):
            s0, n = b0 * 128, nb * 128
            Xr, Xk, Xv = X[si_ % 2]
            a1, a2 = t1[si_ % 2], t2[si_ % 2]
            for c, xx in enumerate((Xr, Xk, Xv)):
                self.shiftmix(j, Sx[c], co[:, c:c + 1], v3(co[:, 4:16], 3, 4)[:, c, :], co, s0, n, xx)
            k.op("pool", "tensor_copy", R=[Xr], W=[Rr], out=Rr[:, s0:s0 + n], in_=Xr[:, 0:n])
            k.op("pool", "tensor_copy", R=[Xk], W=[Kk], out=Kk[:, s0:s0 + n], in_=Xk[:, 0:n])
            v_ = vb[si_ % 2]
            k.op("act", "activation", R=[Xk, rw], W=[v_], out=v_[:, 0:n], in_=Xk[:, 0:n], func=AF.Square, scale=rw[:, 3:4])
            bk = k.bank()
            k.op("pe", "matmul", R=[k.bonesb, v_], W=[bk], out=bk[:, 0:n], lhsT=k.bonesb.ap, rhs=v_[:, 0:n], start=True, stop=True)
            k.op("dve", "tensor_scalar", R=[bk], W=[a1], out=a1[:, 0:n], in0=bk[:, 0:n], scalar1=1e-24, scalar2=None, op0=ALU.max)
            k.op("act", "activation", R=[a1], W=[a1], out=a1[:, 0:n], in_=a1[:, 0:n], func=AF.Ln)
            k.op("act", "activation", R=[a1], W=[a1], out=a1[:, 0:n], in_=a1[:, 0:n], func=AF.Exp, scale=-0.5)
            k.op("dve", "scalar_tensor_tensor", R=[Xk, rw, a1], W=[KK], out=KK[:, s0:s0 + n], in0=Xk[:, 0:n],
                 scalar=rw[:, 3:4], in1=a1[:, 0:n], op0=ALU.mult, op1=ALU.mult)
            k.op("dve", "scalar_tensor_tensor", R=[Xr, rw, Xk], W=[v_], out=v_[:, 0:n], in0=Xr[:, 0:n],
                 scalar=rw[:, 5:6], in1=Xk[:, 0:n], op0=ALU.mult, op1=ALU.mult)
            bk2 = k.bank()
            k.op("pe", "matmul", R=[k.bonesb, v_], W=[bk2], out=bk2[:, 0:n], lhsT=k.bonesb.ap, rhs=v_[:, 0:n], start=True, stop=True)
            k.op("dve", "tensor_tensor", R=[bk2, Xv], W=[BON], out=BON[:, s0:s0 + n], in0=bk2[:, 0:n], in1=Xv[:, 0:n], op=ALU.mult)
            k.op("pool", "tensor_copy", R=[Xv], W=[v_], out=v_[:, 0:n], in_=Xv[:, 0:n])
            bb = k.bbank()
            for b in range(nb):
                k.op("pe", "transpose", R=[v_, k.identb], W=[bb], out=bb[:, b * 128:(b + 1) * 128],
                     in_=v_[:, b * 128:(b + 1) * 128], identity=k.identb.ap)
            k.op("act", "activation", R=[bb], W=[Vtok], out=Vtok[:, s0:s0 + n], in_=bb[:, 0:n], func=AF.Copy)
        RS_ = self.cfg.get("rw_stop", 9)
        if RS_ <= 1:
            return
        self.P.barrier()
        AFa.pos, ABa.pos = mark_f, mark_b
        F_ = {nm: k.tt(AFa, SEG) for nm in ("sgw", "a", "G", "Gm", "EG", "EnG", "EGm", "EGL", "kt", "bb")}
        Bt = {nm: k.tt(ABa, SEG) for nm in ("lw", "rT", "aT", "bT", "kT", "BpT", "Atok", "Bptok", "MT",
                                            "AhT", "Ubf")}
        BrbS, BrkS = k.tt(ABa, SB * 256), k.tt(ABa, SB * 256)
        Kp32 = k.tt(AFa, SEG)
        SS = k.tt(ABa, SB * 64)
        NG = (SB + 1) // 2
        G_ = [{nm: [k.tt(ABa, 512) for _ in range(1 if nm in ("T", "TT") else 2)] for nm in ("X", "XT", "T", "TT", "X0", "I")}
              for _ in range(NG)]
        AakS = [k.tt(ABa, 512) for _ in range(NG)]
        Zs = [k.tt(ABa, 512) for _ in range(NG)]
        N32 = [k.tt(AFa, 512) for _ in range(NG)]
        ysum = k.tt(AFa, SEG)
        ttm = self.tail_tmps(SEG, 1) * 2
        ptmp = k.tt(ABa, SEG)
        rm3 = v3(k.rmask.ap, 2, 512)
        mask3 = v3(k.masksb.ap, 4, 128)
        sti = k.tt(AFa, 128)
        for d in range(2):
            fl = (lambda a: a) if d == 0 else flip
            eidx = 127 if d == 0 else 0
            mSU, mSL, mU = (2, 3, 0) if d == 0 else (3, 2, 1)
            if j.kind == "s":
                k.dma("sp", v3(sti[0:64, :], 2, 64), self.st_in["s"][l, d, 2 * hp:2 * hp + 2].rearrange("h i j -> i h j"), W=[sti])
                bk = k.bank()
                k.op("pe", "transpose", R=[sti, k.ident], W=[bk], out=bk[:, 0:64], in_=sti[0:64, :], identity=k.ident[0:64, 0:64])
                k.op("act", "activation", R=[bk], W=[Sst], out=Sst.ap, in_=bk[:, 0:64], func=AF.Copy)
            else:
                k.op("dve", "memset", W=[Sst], ap=Sst.ap, constant=0.0)
            for si_, (b0, nb) in enumerate(self.segs(j, d)):
                s0, n = b0 * 128, nb * 128
                lw = Bt["lw"]
                k.dma("sp", lw[:, 0:n], j.LW[:, s0:s0 + n], R=[j.bLW], W=[lw])
                bzw, bza = k.bank(), k.bank()
                k.op("pe", "matmul", R=[w2, lw], W=[bzw], out=bzw[:, 0:n], lhsT=w2[0:64, d * 128:(d + 1) * 128],
                     rhs=lw[0:64, 0:n], start=True, stop=True)
                k.op("pe", "matmul", R=[w2, lw], W=[bza], out=bza[:, 0:n], lhsT=w2[64:128, d * 128:(d + 1) * 128],
                     rhs=lw[64:128, 0:n], start=True, stop=True)
                f = F_
                k.op("act", "activation", R=[bzw, rw], W=[f["sgw"]], out=f["sgw"][:, 0:n], in_=bzw[:, 0:n], func=AF.Sigmoid,
                     bias=rw[:, 6 + d:7 + d])
                k.op("act", "activation", R=[bza, rw], W=[f["a"]], out=f["a"][:, 0:n], in_=bza[:, 0:n], func=AF.Sigmoid,
                     bias=rw[:, 8 + d:9 + d])
                k.op("dve", "tensor_tensor_scan", R=[f["sgw"], k.rmask], W=[f["G"]], out=fl(f["G"][:, 0:n]),
                     data0=rm3[:, 0, 0:n], data1=fl(f["sgw"][:, 0:n]), initial=0.0, op0=ALU.mult, op1=ALU.add)
                k.op("act", "activation", R=[f["G"]], W=[f["EG"]], out=f["EG"][:, 0:n], in_=f["G"][:, 0:n], func=AF.Exp, scale=-CE)
                k.op("act", "activation", R=[f["G"]], W=[f["EnG"]], out=f["EnG"][:, 0:n], in_=f["G"][:, 0:n], func=AF.Exp, scale=CE)
                k.op("dve", "tensor_tensor", R=[f["G"], f["sgw"]], W=[f["Gm"]], out=f["Gm"][:, 0:n], in0=f["G"][:, 0:n],
                     in1=f["sgw"][:, 0:n], op=ALU.subtract)
                k.op("act", "activation", R=[f["Gm"]], W=[f["EGm"]], out=f["EGm"][:, 0:n], in_=f["Gm"][:, 0:n], func=AF.Exp, scale=-CE)
                G3 = v3(f["G"][:, 0:n], nb, 128)
                k.op("dve", "tensor_tensor", R=[f["G"]], W=[f["Gm"]], out=v3(f["Gm"][:, 0:n], nb, 128),
                     in0=bc_last(G3[:, :, eidx], 128), in1=G3, op=ALU.subtract)
                k.op("act", "activation", R=[f["Gm"]], W=[f["EGL"]], out=f["EGL"][:, 0:n], in_=f["Gm"][:, 0:n], func=AF.Exp, scale=-CE)
                k.op("dve", "tensor_scalar", R=[f["a"], rw], W=[f["kt"]], out=f["kt"][:, 0:n], in0=f["a"][:, 0:n], scalar1=-1.0,
                     scalar2=rw[:, 4:5], op0=ALU.add, op1=ALU.mult)
                k.op("dve", "scalar_tensor_tensor", R=[f["kt"], Kk], W=[f["kt"]], out=f["kt"][:, 0:n], in0=f["kt"][:, 0:n],
                     scalar=1.0, in1=Kk[:, s0:s0 + n], op0=ALU.add, op1=ALU.mult)
                k.op("pool", "tensor_tensor", R=[KK, f["a"]], W=[f["bb"]], out=f["bb"][:, 0:n], in0=KK[:, s0:s0 + n],
                     in1=f["a"][:, 0:n], op=ALU.mult)
                k.op("pool", "tensor_tensor", R=[Rr, f["EG"]], W=[Bt["rT"]], out=Bt["rT"][:, 0:n], in0=Rr[:, s0:s0 + n],
                     in1=f["EG"][:, 0:n], op=ALU.mult)
                k.op("dve", "scalar_tensor_tensor", R=[KK, f["EGm"]], W=[Bt["aT"]], out=Bt["aT"][:, 0:n], in0=KK[:, s0:s0 + n],
                     scalar=-1.0, in1=f["EGm"][:, 0:n], op0=ALU.mult, op1=ALU.mult)
                k.op("pool", "tensor_tensor", R=[f["bb"], f["EnG"]], W=[Bt["bT"]], out=Bt["bT"][:, 0:n], in0=f["bb"][:, 0:n],
                     in1=f["EnG"][:, 0:n], op=ALU.mult)
                k.op("dve", "tensor_tensor", R=[f["kt"], f["EnG"]], W=[Bt["kT"]], out=Bt["kT"][:, 0:n], in0=f["kt"][:, 0:n],
                     in1=f["EnG"][:, 0:n], op=ALU.mult)
                k.op("pool", "tensor_tensor", R=[f["bb"], f["EGL"]], W=[Bt["BpT"]], out=Bt["BpT"][:, 0:n], in0=f["bb"][:, 0:n],
                     in1=f["EGL"][:, 0:n], op=ALU.mult)
                k.op("dve", "tensor_tensor", R=[f["kt"], f["EGL"]], W=[f["Gm"]], out=f["Gm"][:, 0:n], in0=f["kt"][:, 0:n],
                     in1=f["EGL"][:, 0:n], op=ALU.mult)
                bkp = k.bank()
                for b in range(nb):
                    k.op("pe", "transpose", R=[f["Gm"], k.ident], W=[bkp], out=bkp[:, b * 128:(b + 1) * 128],
                         in_=f["Gm"][:, b * 128:(b + 1) * 128], identity=k.ident.ap)
                k.op("act", "activation", R=[bkp], W=[Kp32], out=Kp32[:, 0:n], in_=bkp[:, 0:n], func=AF.Copy)
                for src, dst in (("aT", "Atok"), ("BpT", "Bptok")):
                    bb = k.bbank()
                    for b in range(nb):
                        k.op("pe", "transpose", R=[Bt[src], k.identb], W=[bb], out=bb[:, b * 128:(b + 1) * 128],
                             in_=Bt[src][:, b * 128:(b + 1) * 128], identity=k.identb.ap)
                    k.op("act", "activation", R=[bb], W=[Bt[dst]], out=Bt[dst][:, 0:n], in_=bb[:, 0:n], func=AF.Copy)
                if RS_ <= 2:
                    continue
                groups = [(g0, min(2, nb - g0)) for g0 in range(0, nb, 2)]

                def gen_group(gi, g0, ng):
                    T = G_[gi]
                    ni = ng * 2
                    w_ = ni * 128

                    def prod(lname, rname, dstTT, dst_ap_fn, midx):
                        for h in range(2):
                            bk = k.bank()
                            for bl in range(ng):
                                tk = slice((g0 + bl) * 128, (g0 + bl + 1) * 128)
                                k.op("pe", "matmul", R=[Bt[lname], Bt[rname]], W=[bk], out=bk[:, bl * 128:(bl + 1) * 128],
                                     lhsT=Bt[lname][h * 64:(h + 1) * 64, tk], rhs=Bt[rname][h * 64:(h + 1) * 64, tk],
                                     start=True, stop=True)
                            k.op("dve", "tensor_tensor", R=[bk, k.masksb], W=[dstTT], out=dst_ap_fn(h),
                                 in0=v3(bk[:, 0:ng * 128], ng, 128), in1=bc_mid(mask3[:, midx, :], ng), op=ALU.mult)
                    slot3 = lambda t: (lambda h: v3(t[:, h * ng * 128:(h + 1) * ng * 128], ng, 128))
                    X0, X0T = T["X0"][0], T["X0"][1]
                    prod("bT", "aT", X0T, slot3(X0T), mSU)
                    prod("aT", "bT", X0, slot3(X0), mSL)
                    yield
                    prod("aT", "kT", AakS[gi], slot3(AakS[gi]), mSL)
                    brb4 = v3(BrbS[:, 0:nb * 256], nb * 2, 128)
                    brk4 = v3(BrkS[:, 0:nb * 256], nb * 2, 128)
                    prod("bT", "rT", BrbS, lambda h: v3(BrbS[:, 0:nb * 256], nb, 256)[:, g0:g0 + ng, h * 128:(h + 1) * 128], mU)
                    prod("kT", "rT", BrkS, lambda h: v3(BrkS[:, 0:nb * 256], nb, 256)[:, g0:g0 + ng, h * 128:(h + 1) * 128], mU)
                    yield
                    if RS_ <= 3:
                        return
                    idb = bc_mid(k.identb.ap, ni)
                    hm = v3(k.hmaskb.ap, 4, 128)
                    IX, IXT = T["I"]

                    def msk(src, dst, mi):
                        k.op("pool", "tensor_tensor", R=[src, k.hmaskb], W=[dst], out=v3(dst[:, 0:w_], ni, 128),
                             in0=v3(src[:, 0:w_], ni, 128), in1=bc_mid(hm[:, mi, :], ni), op=ALU.mult)

                    def addid(eng, src, dst, srcTT=None):
                        k.op(eng, "tensor_tensor", R=[src, k.identb], W=[dst], out=v3(dst[:, 0:w_], ni, 128),
                             in0=v3(src[:, 0:w_], ni, 128), in1=idb, op=ALU.add)

                    def mm4(lt, rt):
                        bk = k.bank()
                        for it in range(ni):
                            sl = slice(it * 128, (it + 1) * 128)
                            k.op("pe", "matmul", R=[lt, rt], W=[bk], out=bk[:, sl], lhsT=lt[:, sl], rhs=rt[:, sl], start=True, stop=True)
                        return bk
                    msk(X0, T["X"][0], 0)
                    msk(X0T, T["XT"][0], 0)
                    addid("pool", T["X"][0], T["T"][0])
                    addid("pool", T["XT"][0], T["TT"][0])
                    cur, tc = 0, 0
                    if RS_ <= 3.2:
                        return
                    for lvl in range(1, 4):
                        nxt = 1 - cur
                        bx = mm4(T["XT"][cur], T["X"][cur])
                        bxt = mm4(T["X"][cur], T["XT"][cur])
                        addid("dve", bx, IX)
                        addid("dve", bxt, IXT)
                        if lvl < 3:
                            k.op("act", "activation", R=[bx], W=[T["X"][nxt]], out=T["X"][nxt][:, 0:w_], in_=bx[:, 0:w_], func=AF.Copy)
                            k.op("act", "activation", R=[bxt], W=[T["XT"][nxt]], out=T["XT"][nxt][:, 0:w_], in_=bxt[:, 0:w_], func=AF.Copy)
                        if RS_ <= 3.3:
                            return
                        yield
                        btt = mm4(IX, T["TT"][tc])
                        bt = mm4(IXT, T["T"][tc])
                        k.op("act", "activation", R=[btt], W=[T["TT"][tc]], out=T["TT"][tc][:, 0:w_], in_=btt[:, 0:w_], func=AF.Copy)
                        k.op("dve", "tensor_copy", R=[bt], W=[T["T"][tc]], out=T["T"][tc][:, 0:w_], in_=bt[:, 0:w_])
                        cur = nxt
                        if RS_ <= 3.4:
                            return
                        yield
                    if RS_ <= 3.5:
                        return
                    for li, mi in enumerate((1, 2, 3)):
                        lastl = li == 2
                        Ao, AoT, Q1, Q2 = T["X"][0], T["XT"][0], T["X"][1], T["XT"][1]
                        msk(X0, Ao, mi)
                        bq1 = mm4(Ao, T["TT"][tc])
                        k.op("act", "activation", R=[bq1], W=[Q1], out=Q1[:, 0:w_], in_=bq1[:, 0:w_], func=AF.Copy)
                        msk(X0T, AoT, mi)
                        bq2 = mm4(AoT, T["T"][tc])
                        k.op("dve", "tensor_copy", R=[bq2], W=[Q2], out=Q2[:, 0:w_], in_=bq2[:, 0:w_])
                        yield
                        btt = mm4(T["T"][tc], Q1)
                        bt = mm4(T["TT"][tc], Q2)
                        k.op("dve", "tensor_tensor", R=[btt, T["TT"][tc]], W=[T["TT"][tc]], out=T["TT"][tc][:, 0:w_],
                             in0=btt[:, 0:w_], in1=T["TT"][tc][:, 0:w_], op=ALU.add)
                        k.op("dve", "tensor_tensor", R=[bt, T["T"][tc]], W=[T["T"][tc]], out=T["T"][tc][:, 0:w_],
                             in0=bt[:, 0:w_], in1=T["T"][tc][:, 0:w_], op=ALU.add)
                        yield
                    T0, TT0 = T["T"][tc], T["TT"][tc]
                    Rm, TTh, TTl, s32 = T["I"][0], T["I"][1], T["X"][0], N32[gi]
                    self.dbgaps = dict(Rm=Rm.ap, TTh=TTh.ap, TTl=TTl.ap, s32=s32.ap, T0=T0.ap, TT0=TT0.ap, X0T=X0T.ap, X0=X0.ap)
                    bA = mm4(X0, TT0)
                    k.op("dve", "tensor_tensor", R=[bA, TT0], W=[s32], out=s32[:, 0:w_], in0=bA[:, 0:w_], in1=TT0[:, 0:w_], op=ALU.subtract)
                    addid("dve", s32, Rm)
                    yield
                    bD = mm4(T0, Rm)
                    k.op("dve", "scalar_tensor_tensor", R=[bD, TT0], W=[s32], out=s32[:, 0:w_], in0=bD[:, 0:w_], scalar=float(self.cfg.get("nwt", 1.0)), in1=TT0[:, 0:w_], op0=ALU.mult, op1=ALU.add)
                    k.op("act", "activation", R=[s32], W=[TTh], out=TTh[:, 0:w_], in_=s32[:, 0:w_], func=AF.Copy)
                    k.op("dve", "tensor_tensor", R=[s32, TTh], W=[TTl], out=TTl[:, 0:w_], in0=s32[:, 0:w_], in1=TTh[:, 0:w_], op=ALU.subtract)
                    yield
                    cur = tc
                    TT_ = None
                    ba = k.bank()
                    for bl in range(ng):
                        for h in range(2):
                            it = h * ng + bl
                            for pi_, tpart in enumerate((TTh, TTl)):
                                k.op("pe", "matmul", R=[Bt["Atok"], tpart], W=[ba], out=ba[h * 64:(h + 1) * 64, bl * 128:(bl + 1) * 128],
                                     lhsT=Bt["Atok"][:, (g0 + bl) * 128 + h * 64:(g0 + bl) * 128 + (h + 1) * 64],
                                     rhs=tpart[:, it * 128:(it + 1) * 128], start=(pi_ == 0), stop=(pi_ == 1))
                    k.op("act", "activation", R=[ba], W=[Bt["AhT"]], out=Bt["AhT"][:, g0 * 128:(g0 + ng) * 128], in_=ba[:, 0:ng * 128],
                         func=AF.Copy)
                    bz = k.bank()
                    for it in range(ni):
                        sl = slice(it * 128, (it + 1) * 128)
                        for pi_, tpart in enumerate((TTh, TTl)):
                            k.op("pe", "matmul", R=[tpart, AakS[gi]], W=[bz], out=bz[:, sl], lhsT=tpart[:, sl], rhs=AakS[gi][:, sl],
                                 start=(pi_ == 0), stop=(pi_ == 1))
                    k.op("act", "activation", R=[bz], W=[Zs[gi]], out=Zs[gi][:, 0:w_], in_=bz[:, 0:w_], func=AF.Copy)
                    yield
                    bm = k.bank()
                    for bl in range(ng):
                        for h in range(2):
                            it = h * ng + bl
                            k.op("pe", "matmul", R=[Zs[gi], Bt["Bptok"]], W=[bm], out=bm[:, (bl * 2 + h) * 64:(bl * 2 + h + 1) * 64],
                                 lhsT=Zs[gi][:, it * 128:(it + 1) * 128],
                                 rhs=Bt["Bptok"][:, (g0 + bl) * 128 + h * 64:(g0 + bl) * 128 + (h + 1) * 64], start=True, stop=True)
                    k.op("dve", "tensor_tensor", R=[bm, Kp32], W=[Bt["MT"]], out=Bt["MT"][:, g0 * 128:(g0 + ng) * 128],
                         in0=bm[:, 0:ng * 128], in1=Kp32[:, g0 * 128:(g0 + ng) * 128], op=ALU.add)
                    bmy = k.bank()
                    for bl in range(ng):
                        for h in range(2):
                            it = h * ng + bl
                            sl = slice(((g0 + bl) * 2 + h) * 128, ((g0 + bl) * 2 + h + 1) * 128)
                            k.op("pe", "matmul", R=[Zs[gi], BrbS], W=[bmy], out=bmy[:, (bl * 2 + h) * 128:(bl * 2 + h + 1) * 128],
                                 lhsT=Zs[gi][:, it * 128:(it + 1) * 128], rhs=BrbS[:, sl], start=True, stop=True)
                    k.op("dve", "tensor_tensor", R=[bmy, BrkS], W=[BrkS], out=BrkS[:, g0 * 256:(g0 + ng) * 256],
                         in0=bmy[:, 0:ng * 256], in1=BrkS[:, g0 * 256:(g0 + ng) * 256], op=ALU.add)
                    yield
                gens = [gen_group(gi, g0, ng) for gi, (g0, ng) in enumerate(groups)]
                while gens:
                    for g in list(gens):
                        try:
                            next(g)
                        except StopIteration:
                            gens.remove(g)
                if RS_ <= 4:
                    continue
                order = range(nb) if d == 0 else range(nb - 1, -1, -1)
                Ub4 = Bt["Ubf"][:, 0:n].rearrange("p (b h c) -> p b h c", b=nb, h=2, c=64)
                for b in order:
                    k.op("pool", "tensor_copy", R=[Sst], W=[SS], out=SS[:, b * 64:(b + 1) * 64], in_=Sst.ap)
                    for h in range(2):
                        bu = k.bank()
                        k.op("pe", "matmul", R=[Bt["AhT"], SS], W=[bu], out=bu[:, 0:64], lhsT=Bt["AhT"][h * 64:(h + 1) * 64, b * 128:(b + 1) * 128],
                             rhs=SS[h * 64:(h + 1) * 64, b * 64:(b + 1) * 64], start=True, stop=True)
                        k.op("act", "activation", R=[bu], W=[Bt["Ubf"]], out=Ub4[:, b, h, :], in_=bu[:, 0:64], func=AF.Copy)
                    bd = k.bank()
                    for h in range(2):
                        oo = bd[h * 64:(h + 1) * 64, 0:64]
                        vs = Vtok[:, (b0 + b) * 128 + h * 64:(b0 + b) * 128 + (h + 1) * 64]
                        k.op("pe", "matmul", R=[Bt["Bptok"], Bt["Ubf"]], W=[bd], out=oo, lhsT=Bt["Bptok"][:, b * 128 + h * 64:b * 128 + (h + 1) * 64],
                             rhs=Ub4[:, b, h, :], start=True, stop=False)
                        k.op("pe", "matmul", R=[Bt["MT"], Vtok], W=[bd], out=oo, lhsT=Bt["MT"][:, b * 128 + h * 64:b * 128 + (h + 1) * 64],
                             rhs=vs, start=False, stop=True)
                    gcol = b * 128 + eidx
                    k.op("dve", "scalar_tensor_tensor", R=[Sst, f["EG"], bd], W=[Sst], out=Sst.ap, in0=Sst.ap, scalar=f["EG"][:, gcol:gcol + 1],
                         in1=bd[:, 0:64], op0=ALU.mult, op1=ALU.add)
                if RS_ <= 5:
                    continue
                by = k.bank()
                for b in range(nb):
                    for h in range(2):
                        oo = by[:, b * 128 + h * 64:b * 128 + (h + 1) * 64]
                        vs = Vtok[:, (b0 + b) * 128 + h * 64:(b0 + b) * 128 + (h + 1) * 64]
                        sl = slice((b * 2 + h) * 128, (b * 2 + h + 1) * 128)
                        k.op("pe", "matmul", R=[Bt["rT"], SS], W=[by], out=oo, lhsT=Bt["rT"][h * 64:(h + 1) * 64, b * 128:(b + 1) * 128],
                             rhs=SS[h * 64:(h + 1) * 64, b * 64:(b + 1) * 64], start=True, stop=False)
                        k.op("pe", "matmul", R=[BrbS, Bt["Ubf"]], W=[by], out=oo, lhsT=BrbS[:, sl], rhs=Ub4[:, b, h, :], start=False, stop=False)
                        k.op("pe", "matmul", R=[BrkS, Vtok], W=[by], out=oo, lhsT=BrkS[:, sl], rhs=vs, start=False, stop=True)
                if d == 0:
                    k.op("act", "activation", R=[by], W=[YF], out=YF[:, s0:s0 + n], in_=by[:, 0:n], func=AF.Copy)
                else:
                    k.op("dve", "tensor_tensor", R=[by, YF], W=[ysum], out=ysum[:, 0:n], in0=by[:, 0:n], in1=YF[:, s0:s0 + n], op=ALU.add)

                    def post(bb, yt, n_, s0=s0):
                        k.op("dve", "tensor_tensor", R=[bb, BON], W=[ptmp], out=ptmp[:, 0:n_], in0=bb[:, 0:n_], in1=BON[:, s0:s0 + n_], op=ALU.add)
                        k.op("pool", "tensor_tensor", R=[ptmp, SGT], W=[yt], out=yt[:, 0:n_], in0=ptmp[:, 0:n_], in1=SGT[:, s0:s0 + n_], op=ALU.mult)
                    self.tail(j, 3, hp, s0, nb, ysum, None, None, ttm[si_ % 2], post=post)
            if j.kind == "p" and RS_ > 6:
                bk = k.bank()
                k.op("pe", "transpose", R=[Sst, k.ident], W=[bk], out=bk[0:64, 0:128], in_=Sst.ap, identity=k.ident.ap)
                k.op("act", "activation", R=[bk], W=[sti], out=sti[0:64, :], in_=bk[0:64, 0:128], func=AF.Copy)
                k.dma("sp", self.so["s"][j.idx, l, d, 2 * hp:2 * hp + 2].rearrange("h i j -> i h j"), v3(sti[0:64, :], 2, 64), R=[sti])


def _kt(w):
    C = w.shape[1]
    return np.ascontiguousarray(w.reshape(KT, 128, C).transpose(1, 0, 2))


def prep_shared(inp, cfg):
    L = cfg["DEPTH"]
    f = np.float32
    w_in = np.asarray(inp["w_in"], f)
    out = {}
    wA = np.zeros((L, 4, 128, KT, 640), f)
    wAg = np.zeros((L, 128, KT, 32), f)
    wB = np.zeros((L, 4, 128, KT, 256), f)
    wC = np.zeros((L, 4, 128, KT, 768), f)
    wD = np.zeros((L, 4, 128, KT, 512), f)
    wDl = np.zeros((L, 128, KT, 128), f)
    sw = np.concatenate([np.arange(32, 64), np.arange(0, 32), np.arange(96, 128), np.arange(64, 96)])
    for l in range(L):
        w = w_in[l]
        wAg[l] = _kt(w[:, OFF_A + 2560:OFF_A + 2592])
        wDl[l] = _kt(w[:, OFF_D + 1536:OFF_D + 1664])
        for hp in range(4):
            sl = lambda base, comp: w[:, base + comp * 512 + hp * 128: base + comp * 512 + (hp + 1) * 128]
            wA[l, hp] = _kt(np.concatenate([sl(OFF_A, c) for c in range(5)], 1))
            wB[l, hp] = _kt(np.concatenate([sl(OFF_B, 0), sl(OFF_B, 1)], 1))
            q, k_, v, g = (sl(OFF_C, c) for c in range(4))
            wC[l, hp] = _kt(np.concatenate([q, q[:, sw], k_, k_[:, sw], v, g], 1))
            gD = w[:, OFF_D + 1664 + hp * 128: OFF_D + 1664 + (hp + 1) * 128]
            wD[l, hp] = _kt(np.concatenate([sl(OFF_D, 0), sl(OFF_D, 1), sl(OFF_D, 2), gD], 1))
    out.update(wA=wA, wAg=wAg, wB=wB, wC=wC, wD=wD, wDl=wDl)
    w_out = np.asarray(inp["w_out"], f)
    out["wout"] = np.ascontiguousarray(w_out.reshape(L, 16, 128, D).transpose(0, 2, 1, 3))
    w_mod = np.asarray(inp["w_mod"], f)
    out["wmod"] = np.ascontiguousarray(w_mod.reshape(L, KT, 128, 3 * D).transpose(0, 2, 1, 3))
    out["ng2"] = np.ascontiguousarray(np.repeat(np.asarray(inp["norm_g"], f)[:, None, :], 2, 1))
    out["bmod2"] = np.ascontiguousarray(np.repeat(np.asarray(inp["b_mod"], f)[:, None, :], 2, 1))
    out["fgb"] = np.ascontiguousarray(np.repeat(np.asarray(inp["final_g"], f)[None, :], 128, 0))
    out["gbA"] = np.ascontiguousarray(np.asarray(inp["mlstm_gate_b"], f).reshape(L, 4, 8).transpose(0, 2, 1))
    lruP = np.zeros((L, 4, 128, 12), f)
    lruG = np.zeros((L, 4, 128, 4, 128), f)
    retP = np.zeros((L, 4, 128, 6), f)
    rwP = np.zeros((L, 4, 128, 12), f)
    cw, cb = np.asarray(inp["lru_conv_w"], f), np.asarray(inp["lru_conv_b"], f)
    gw, gb = np.asarray(inp["lru_gate_w"], f), np.asarray(inp["lru_gate_b"], f)
    lam, th = np.asarray(inp["lru_lambda"], f), np.asarray(inp["ret_theta"], f)
    mu = np.asarray(inp["rwkv_mu"], f)
    kk_, ka_, rk_ = (np.asarray(inp[n], f) for n in ("rwkv_kk", "rwkv_ka", "rwkv_rk"))
    w0, a0 = np.asarray(inp["rwkv_w0"], f), np.asarray(inp["rwkv_a0"], f)
    for l in range(L):
        for hp in range(4):
            ch = slice(hp * 128, (hp + 1) * 128)
            for t in range(4):
                lruP[l, hp, :, t] = cw[l, t, ch]
            lruP[l, hp, :, 4] = cb[l, ch]
            for d in range(2):
                for g_ in range(2):
                    lruP[l, hp, :, 5 + d * 2 + g_] = gb[l, d, g_, ch]
                    for hl in range(2):
                        lruG[l, hp, hl * 64:(hl + 1) * 64, d * 2 + g_, hl * 64:(hl + 1) * 64] = gw[l, d, g_, 2 * hp + hl]
                lruP[l, hp, :, 9 + d] = lam[l, d, ch]
                retP[l, hp, 0:64, d] = th[l, d, 2 * hp]
                retP[l, hp, 64:128, d] = th[l, d, 2 * hp + 1]
                for hl in range(2):
                    retP[l, hp, :, 2 + d * 2 + hl] = th[l, d, 2 * hp + hl]
                rwP[l, hp, :, 6 + d] = w0[l, d, ch]
                rwP[l, hp, :, 8 + d] = a0[l, d, ch]
            for c in range(3):
                rwP[l, hp, :, c] = mu[l, c * 512 + hp * 128: c * 512 + (hp + 1) * 128]
            rwP[l, hp, :, 3] = kk_[l, ch]
            rwP[l, hp, :, 4] = ka_[l, ch]
            rwP[l, hp, :, 5] = rk_[l, ch]
    out.update(lruP=lruP, lruG=lruG, retP=retP, rwP=rwP)
    rwLP = np.zeros((L, 128, 2), f)
    rwLP[:, :, 0] = mu[:, 1536:1664]
    out["rwLP"] = rwLP
    out["rwW2"] = np.ascontiguousarray(np.concatenate([np.asarray(inp["rwkv_w2"], f), np.asarray(inp["rwkv_a2"], f)], 2))
    p = np.arange(128)[:, None]
    fr = np.arange(128)[None, :]
    out["c_ident"] = np.eye(128, dtype=f)
    out["c_masks"] = np.stack([(fr >= p), (fr <= p), (fr > p), (fr < p)], 1).astype(f)
    out["c_diff"] = np.stack([np.maximum(fr - p, 0), np.maximum(p - fr, 0)], 1).astype(f)
    pos = np.broadcast_to(fr, (128, 128))
    out["c_pos"] = np.stack([pos + 1, 128 - pos, 127 - pos, pos], 1).astype(f)
    sel = np.zeros((8, 4, 128), f)
    for hp in range(4):
        sel[2 * hp, hp, 0:64] = 1
        sel[2 * hp + 1, hp, 64:128] = 1
    out["c_sel"] = sel
    sel2 = np.zeros((2, 2, 128), f)
    sel2[0, 0] = 1
    sel2[1, 1] = 1
    out["c_sel2"] = sel2
    bo = np.zeros((128, 128), f)
    bo[0:64, 0:64] = 1
    bo[64:, 64:] = 1
    out["c_bones"] = bo
    t5 = np.arange(512)
    rm = np.zeros((128, 2, 512), f)
    rm[:, 0, :] = (t5 % 128 != 0)
    rm[:, 1, :] = np.where(t5 % 128 == 0, -1e30, 0.0)
    out["c_rmask"] = rm
    bdm = lambda sz: ((fr // sz) == (p // sz))
    out["c_hmask"] = np.stack([bdm(16), bdm(32) & ~bdm(16), bdm(64) & ~bdm(32), bdm(128) & ~bdm(64)], 1).astype(f)
    for kind in ("p", "s"):
        m = np.zeros((4, 128, 3, 4), f)
        ml = np.zeros((128, 4), f)

        def dirof(c):
            return (c // 416) if kind == "s" else (0 if c < 832 else 1)
        for hp in range(4):
            for comp in range(3):
                for pp in range(128):
                    m[hp, pp, comp, dirof(comp * 512 + hp * 128 + pp)] = 1
        for pp in range(128):
            ml[pp, dirof(1536 + pp)] = 1
        out[f"c_shm_{kind}"] = m
        out[f"c_shl_{kind}"] = ml
    NS = cfg["NS"]
    if NS:
        rows = NS // 64
        row_idx = np.repeat(np.arange(rows, dtype=f), 64)
        col_idx = np.tile(np.arange(64, dtype=f), rows)
        nfreq = 16
        freqs = (100.0 ** (-np.arange(nfreq, dtype=f) / nfreq)).astype(f)
        ang = np.concatenate([row_idx[:, None] * freqs, col_idx[:, None] * freqs], -1).astype(f)
        cs, sn = np.cos(ang).astype(f), np.sin(ang).astype(f)
        rope = np.zeros((2, 128, NS), f)
        for pp in range(128):
            dd = pp % 64
            rope[0, pp] = cs[:, dd % 32]
            rope[1, pp] = (-sn[:, dd % 32]) if dd < 32 else sn[:, dd % 32]
        out["rope"] = rope
    return out


def prep_core(inp, cfg, core, shared):
    f = np.float32
    NPS, NS = cfg["NPS"], cfg["NS"]
    m = dict(shared)
    cvec = np.zeros((2, D), f)
    cvec[0] = np.asarray(inp["c_ctx"], f)
    if NPS:
        m["xp"] = np.ascontiguousarray(np.asarray(inp["x_prompt"], f)[core * NPS:(core + 1) * NPS])
    if NS:
        nb = np.asarray(inp["x_sample"]).shape[0]
        b = (core * nb) // cfg.get("NCORES", 8)
        m["xs"] = np.ascontiguousarray(np.asarray(inp["x_sample"], f)[b])
        cvec[1] = np.asarray(inp["c"], f)[b]
        m["st_c"] = np.ascontiguousarray(np.asarray(inp["state_mlstm_c"], f)[b])
        m["st_n"] = np.ascontiguousarray(np.asarray(inp["state_mlstm_n"], f)[b])
        m["st_m"] = np.ascontiguousarray(np.asarray(inp["state_mlstm_m"], f)[b])
        m["st_h"] = np.ascontiguousarray(np.asarray(inp["state_lru_h"], f)[b])
        m["st_r"] = np.ascontiguousarray(np.asarray(inp["state_ret_r"], f)[b])
        m["st_s"] = np.ascontiguousarray(np.asarray(inp["state_rwkv_s"], f)[b])
    m["cvT"] = np.ascontiguousarray(cvec.reshape(2, KT, 128).transpose(2, 1, 0))
    return m


FULL_CFG = dict(DEPTH=4, NP=256, NPS=2, NS=4096, NCORES=8)
_CACHE = {}


def kernel(**inputs):
    cfg = FULL_CFG
    if "nc" not in _CACHE:
        kb = KB(cfg)
        _CACHE["nc"] = kb.build()
        _CACHE["kb"] = kb
    nc, kb = _CACHE["nc"], _CACHE["kb"]
    shared = prep_shared(inputs, cfg)
    in_maps = []
    for core in range(8):
        m = prep_core(inputs, cfg, core, shared)
        in_maps.append({k_: m[k_] for k_ in kb.din})
    res = run_bass_kernel_spmd(nc, in_maps, core_ids=list(range(8)))
    r = res.results
    L = cfg["DEPTH"]
    y_prompt = np.concatenate([r[i]["yp"] for i in range(8)], 0)
    y_sample = np.stack([r[0]["ys"], r[4]["ys"]], 0)
    outs = [y_prompt, y_sample]
    for nm in ("o_c", "o_n", "o_m", "o_h", "o_r", "o_s"):
        outs.append(np.concatenate([r[i][nm] for i in range(8)], 0))
    return tuple(np.asarray(o, np.float32) for o in outs)
```

```python
import contextlib
import numpy as np
import concourse.bass as bass
import concourse.mybir as mybir
from concourse.bass_utils import run_bass_kernel_spmd

F32 = mybir.dt.float32
BF16 = mybir.dt.bfloat16
AF = mybir.ActivationFunctionType
ALU = mybir.AluOpType
AX = mybir.AxisListType

D = 1024
KT = 8
NH = 8
HD = 64
DB = 512
EPS = 1e-6
P_IN = 7840
OFF_A, OFF_B, OFF_C, OFF_D = 0, 2592, 3616, 5664

ENGS = ("pe", "dve", "act", "pool", "sp")


class Buf:
    __slots__ = ("name", "w", "rs", "psum")

    def __init__(self, name):
        self.name = name
        self.w = None
        self.rs = []
        self.psum = False


class Op:
    __slots__ = ("eng", "meth", "kw", "deps", "dma", "sig", "sem", "val", "idx")


class Prog:
    def __init__(self, nc, n_dma_sems=8):
        self.nc = nc
        self.ops = []
        self.stack = contextlib.ExitStack()
        self.n_dma_sems = n_dma_sems
        self.nbuf = 0
        self.last = {e: None for e in ENGS}
        self.dmas_since_barrier = []

    def sb(self, name, shape, dt):
        return self.stack.enter_context(self.nc.sbuf_tensor(name, list(shape), dt))

    def ps(self, name, shape, dt):
        return self.stack.enter_context(self.nc.psum_tensor(name, list(shape), dt))

    def buf(self, name=None):
        self.nbuf += 1
        return Buf(name or f"b{self.nbuf}")

    def bufs(self, n):
        return [self.buf() for _ in range(n)]

    def _rec(self, eng, meth, kw, R, W, dma=False, extra_deps=()):
        deps = set(extra_deps)
        for b in R:
            if b.w is not None:
                deps.add(b.w)
            if b.psum:
                for r in b.rs:
                    if self.ops[r].eng != eng:
                        deps.add(r)
        for b in W:
            if b.w is not None:
                deps.add(b.w)
            deps.update(b.rs)
        i = len(self.ops)
        op = Op()
        op.eng, op.meth, op.kw, op.dma = eng, meth, kw, dma
        op.deps = sorted(deps)
        op.sig = bool(dma)
        op.sem = None
        op.val = 0
        op.idx = i
        self.ops.append(op)
        for d in deps:
            self.ops[d].sig = True
        for b in R:
            b.rs.append(i)
        for b in W:
            b.w = i
            b.rs = []
        self.last[eng] = i
        if dma:
            self.dmas_since_barrier.append(i)
        return op

    def op(self, eng, meth, R=(), W=(), **kw):
        return self._rec(eng, meth, kw, R, W)

    def dma(self, q, out, in_, R=(), W=(), **kw):
        kw = dict(kw)
        kw["out"] = out
        kw["in_"] = in_
        return self._rec(q, "dma_start", kw, R, W, dma=True)

    def barrier(self):
        lasts = [v for v in self.last.values() if v is not None] + list(self.dmas_since_barrier)
        self.dmas_since_barrier = []
        for e in ("pe", "dve", "act", "pool", "sp"):
            self._rec(e, "nop", {}, (), (), extra_deps=lasts)

    def emit(self):
        nc, st = self.nc, self.stack
        sems = {e: st.enter_context(nc.semaphore(f"s_{e}")) for e in ("pe", "dve", "act", "pool", "sp")}
        dsems = {q: [st.enter_context(nc.semaphore(f"d_{q}{i}")) for i in range(self.n_dma_sems)]
                 for q in ("sp", "act", "pool")}
        cnt = {e: 0 for e in sems}
        dcnt = {q: 0 for q in dsems}
        dval = {q: [0] * self.n_dma_sems for q in dsems}
        prev_on_sem = {}
        for op in self.ops:
            if op.dma:
                k = dcnt[op.eng] % self.n_dma_sems
                dcnt[op.eng] += 1
                dval[op.eng][k] += 16
                op.sem = ("d", op.eng, k)
                op.val = dval[op.eng][k]
                p = prev_on_sem.get(op.sem)
                if p is not None and p not in op.deps:
                    op.deps = sorted(set(op.deps) | {p})
                prev_on_sem[op.sem] = op.idx
            elif op.sig:
                cnt[op.eng] += 1
                op.sem = ("c", op.eng)
                op.val = cnt[op.eng]

        def semobj(key):
            return sems[key[1]] if key[0] == "c" else dsems[key[1]][key[2]]

        clocks = [None] * len(self.ops)
        known = {e: {} for e in ENGS}
        streams = {e: [] for e in ENGS}
        for op in self.ops:
            kn = known[op.eng]
            wm = {}
            for d in op.deps:
                dop = self.ops[d]
                if kn.get(dop.sem, 0) >= dop.val:
                    continue
                wm[dop.sem] = max(wm.get(dop.sem, 0), dop.val)
                for s, v in clocks[d].items():
                    if kn.get(s, 0) < v:
                        kn[s] = v
            ck = dict(kn)
            if op.sem is not None:
                ck[op.sem] = max(ck.get(op.sem, 0), op.val)
            clocks[op.idx] = ck
            streams[op.eng].append((op, wm))
        self.n_instr = {e: len(streams[e]) for e in ENGS}
        finals = {}
        for q in dsems:
            for k in range(self.n_dma_sems):
                if dval[q][k] > 0:
                    finals[("d", q, k)] = dval[q][k]
        block = st.enter_context(nc.Block())

        def run(engobj, ename):
            for op, wm in streams[ename]:
                for s, v in wm.items():
                    engobj.wait_ge(semobj(s), v)
                if op.meth == "nop":
                    ins = engobj.nop()
                else:
                    ins = getattr(engobj, op.meth)(**op.kw)
                if op.sem is not None:
                    ins.then_inc(semobj(op.sem), 16 if op.dma else 1)
            if ename == "sp":
                for s, v in finals.items():
                    engobj.wait_ge(semobj(s), v)

        block.tensor(lambda e: run(e, "pe"))
        block.vector(lambda e: run(e, "dve"))
        block.scalar(lambda e: run(e, "act"))
        block.gpsimd(lambda e: run(e, "pool"))
        block.sync(lambda e: run(e, "sp"))

    def close(self):
        self.stack.close()


def flip(a, dims=(-1,)):
    ap = [list(x) for x in a.ap]
    off = a.offset
    for d in dims:
        s, n = ap[d]
        off = off + (n - 1) * s
        ap[d] = [-s, n]
    return bass.AP(a.tensor, off, ap)


class Arena:
    def __init__(self, P, name, words, dt):
        self.t = P.sb(name, [128, words], dt)
        self.words = words
        self.pos = 0
        self.P = P
        self.hi = 0

    def reset(self):
        self.pos = 0

    def alloc(self, n, parts=128):
        n2 = (n + 7) // 8 * 8
        assert self.pos + n2 <= self.words, f"arena overflow {self.pos}+{n2}>{self.words}"
        a = self.t[0:parts, self.pos:self.pos + n]
        self.pos += n2
        self.hi = max(self.hi, self.pos)
        return a


class TT:
    __slots__ = ("ap", "b")

    def __init__(self, ap, b):
        self.ap, self.b = ap, b

    def __getitem__(self, k):
        return self.ap[k]


def v3(ap, a, b):
    return ap.rearrange("p (a b) -> p a b", a=a, b=b)


def bc_mid(ap, n):
    p, m = ap.shape
    return ap.unsqueeze(1).broadcast_to([p, n, m])


def bc_last(ap, n):
    shp = list(ap.shape)
    return ap.unsqueeze(len(shp)).broadcast_to(shp + [n])


class Job:
    pass


class KB:
    def __init__(self, cfg):
        self.cfg = cfg
        self.L = cfg["DEPTH"]
        self.mixers = cfg.get("mixers", "ABCD")
        nc = bass.Bass("TRN2", target_bir_lowering=False)
        self.nc = nc
        self.P = Prog(nc)
        self.din = {}
        self.dout = {}
        self.in_shapes = {}

    def inp(self, name, shape):
        t = self.nc.dram_tensor(name, list(shape), F32, kind="ExternalInput")
        self.din[name] = t
        self.in_shapes[name] = tuple(shape)
        return t.ap()

    def outp(self, name, shape):
        t = self.nc.dram_tensor(name, list(shape), F32, kind="ExternalOutput")
        self.dout[name] = t
        return t.ap()

    def scratch(self, name, shape, dt):
        return self.nc.dram_tensor(name, list(shape), dt, kind="Internal").ap()

    def tt(self, arena, n, parts=128):
        return TT(arena.alloc(n, parts), self.P.buf())

    def bank(self):
        i = self.bank_i
        self.bank_i = (i + 1) % len(self.banks)
        return self.banks[i]

    def bbank(self):
        i = self.bbank_i
        self.bbank_i = (i + 1) % len(self.bbanks)
        return self.bbanks[i]

    def op(self, eng, meth, R=(), W=(), **kw):
        return self.P.op(eng, meth, R=[x.b for x in R], W=[x.b for x in W], **kw)

    def dma(self, q, out, in_, R=(), W=(), **kw):
        return self.P.dma(q, out, in_, R=[x.b for x in R], W=[x.b for x in W], **kw)

    def build(self):
        cfg, P, nc, L = self.cfg, self.P, self.nc, self.L
        NP, NPS, NS = cfg["NP"], cfg["NPS"], cfg["NS"]
        self.SEGB = cfg.get("SEGB", 4)
        I = self.inp
        xp = I("xp", [NPS, NP, D]) if NPS else None
        xs = I("xs", [NS, D]) if NS else None
        cvT = I("cvT", [128, KT, 2])
        self.wA = I("wA", [L, 4, 128, KT, 640])
        self.wAg = I("wAg", [L, 128, KT, 32])
        self.wB = I("wB", [L, 4, 128, KT, 256])
        self.wC = I("wC", [L, 4, 128, KT, 768])
        self.wD = I("wD", [L, 4, 128, KT, 512])
        self.wDl = I("wDl", [L, 128, KT, 128])
        self.wout = I("wout", [L, 128, 16, D])
        self.wmod = I("wmod", [L, 128, KT, 3 * D])
        self.ng2 = I("ng2", [L, 2, D])
        self.bmod2 = I("bmod2", [L, 2, 3 * D])
        self.fgb = I("fgb", [128, D])
        self.gbA = I("gbA", [L, 8, 4])
        self.lruP = I("lruP", [L, 4, 128, 12])
        self.lruG = I("lruG", [L, 4, 128, 4, 128])
        self.retP = I("retP", [L, 4, 128, 6])
        self.rwP = I("rwP", [L, 4, 128, 12])
        self.rwLP = I("rwLP", [L, 128, 2])
        self.rwW2 = I("rwW2", [L, 2, 128, DB])
        c_ident = I("c_ident", [128, 128])
        c_masks = I("c_masks", [128, 4, 128])
        c_diff = I("c_diff", [128, 2, 128])
        c_pos = I("c_pos", [128, 4, 128])
        c_sel = I("c_sel", [8, 4, 128])
        c_sel2 = I("c_sel2", [2, 2, 128])
        c_bones = I("c_bones", [128, 128])
        c_hmask = I("c_hmask", [128, 4, 128])
        c_rmask = I("c_rmask", [128, 2, 512])
        self.c_shm = {}
        self.c_shm["p"] = I("c_shm_p", [4, 128, 3, 4])
        self.c_shm["s"] = I("c_shm_s", [4, 128, 3, 4])
        self.c_shl = {"p": I("c_shl_p", [128, 4]), "s": I("c_shl_s", [128, 4])}
        if NS:
            self.rope = I("rope", [2, 128, NS])
            self.st_in = dict(
                c=I("st_c", [L, 2, NH, HD, HD]), n=I("st_n", [L, 2, NH, HD]), m=I("st_m", [L, 2, NH]),
                h=I("st_h", [L, 2, DB]), r=I("st_r", [L, 2, NH, HD, HD]), s=I("st_s", [L, 2, NH, HD, HD]))
        O = self.outp
        if NPS:
            yp = O("yp", [NPS, NP, D])
            self.so = dict(c=O("o_c", [NPS, L, 2, NH, HD, HD]), n=O("o_n", [NPS, L, 2, NH, HD]),
                           m=O("o_m", [NPS, L, 2, NH]), h=O("o_h", [NPS, L, 2, DB]),
                           r=O("o_r", [NPS, L, 2, NH, HD, HD]), s=O("o_s", [NPS, L, 2, NH, HD, HD]))
        if NS:
            ys = O("ys", [NS, D])
        self.dbg = cfg.get("dbg", False)
        jobs = []
        for i in range(NPS):
            j = Job()
            j.name, j.N, j.kind, j.g, j.idx = f"p{i}", NP, "p", 0, i
            j.x_in, j.y_out = xp[i], yp[i]
            jobs.append(j)
        if NS:
            j = Job()
            j.name, j.N, j.kind, j.g, j.idx = "s", NS, "s", 1, 0
            j.x_in, j.y_out = xs, ys
            jobs.append(j)
        for j in jobs:
            j.NB = j.N // 128
            j.HT = self.scratch(f"HT_{j.name}", [KT, 128, j.N], BF16)
            j.YT = self.scratch(f"YT_{j.name}", [16, 128, j.N], BF16)
            j.XR = self.scratch(f"XR_{j.name}", [j.N, D], F32)
            j.GB = self.scratch(f"GB_{j.name}", [2, 2, 8, j.N], F32)
            j.bHT, j.bYT, j.bXR, j.bGB = TT(None, P.buf()), TT(None, P.buf()), TT(None, P.buf()), TT(None, P.buf())
            if self.dbg:
                j.dbgY = O(f"dbgY_{j.name}", [16, 128, j.N])
        self.jobs = jobs
        NMAX = max(j.N for j in jobs)
        NBMAX = NMAX // 128
        self.banks = [TT(P.ps(f"bk{i}", [128, 512], F32)[:], P.buf()) for i in range(6)]
        self.bbanks = [TT(P.ps(f"bb{i}", [128, 1024], BF16)[:], P.buf()) for i in range(2)]
        for t_ in self.banks + self.bbanks:
            t_.b.psum = True
        self.bank_i = 0
        self.bbank_i = 0
        cw = cfg.get("CARENA", 5000)
        CA = Arena(P, "carena", cw, F32)
        self.CA = CA
        CB = Arena(P, "cbarena", 1536, BF16)
        k = self
        k.ident = k.tt(CA, 128)
        k.masks = k.tt(CA, 512)
        k.diff = k.tt(CA, 256)
        k.pos = k.tt(CA, 512)
        k.sel = k.tt(CA, 512, parts=8)
        k.bones = k.tt(CA, 128)
        k.sel2 = k.tt(CA, 256, parts=2)
        k.rmask = k.tt(CA, 1024)
        k.identb = k.tt(CB, 128)
        k.bonesb = k.tt(CB, 128)
        k.masksb = k.tt(CB, 512)
        k.hmaskb = k.tt(CB, 512)
        k.dma("pool", v3(k.hmaskb.ap, 4, 128), c_hmask, W=[k.hmaskb])
        k.dma("sp", k.ident.ap, c_ident, W=[k.ident])
        k.dma("sp", v3(k.masks.ap, 4, 128), c_masks, W=[k.masks])
        k.dma("sp", v3(k.diff.ap, 2, 128), c_diff, W=[k.diff])
        k.dma("sp", v3(k.pos.ap, 4, 128), c_pos, W=[k.pos])
        k.dma("sp", v3(k.sel.ap, 4, 128), c_sel, W=[k.sel])
        k.dma("sp", k.bones.ap, c_bones, W=[k.bones])
        k.dma("sp", v3(k.sel2.ap, 2, 128), c_sel2, W=[k.sel2])
        k.dma("sp", v3(k.rmask.ap, 2, 512), c_rmask, W=[k.rmask])
        k.op("act", "activation", R=[k.ident], W=[k.identb], out=k.identb.ap, in_=k.ident.ap, func=AF.Copy)
        k.op("act", "activation", R=[k.bones], W=[k.bonesb], out=k.bonesb.ap, in_=k.bones.ap, func=AF.Copy)
        k.op("act", "activation", R=[k.masks], W=[k.masksb], out=k.masksb.ap, in_=k.masks.ap, func=AF.Copy)
        k.eps = k.tt(CA, 1)
        k.op("dve", "memset", W=[k.eps], ap=k.eps.ap, constant=EPS)
        k.cv = k.tt(CA, KT * 2)
        k.scT = k.tt(CB, KT * 2)
        k.dma("sp", v3(k.cv.ap, KT, 2), cvT, W=[k.cv])
        k.op("act", "activation", R=[k.cv], W=[k.scT], out=k.scT.ap, in_=k.cv.ap, func=AF.Silu)
        self.MODS = self.scratch("MODS", [L, 2, 3, D], F32)
        self.bMODS = TT(None, P.buf())
        for j in jobs:
            j.Atok = [k.tt(CA, j.NB * 8) for _ in range(2)]
            j.Etok = [k.tt(CA, j.NB * 8) for _ in range(2)]
            j.Fch = [k.tt(CA, j.NB, parts=8) for _ in range(2)]
            j.mfin = [k.tt(CA, 1, parts=8) for _ in range(2)]
        self.AF_ = Arena(P, "arena_f", cfg.get("AF", 19000), F32)
        self.AB_ = Arena(P, "arena_b", cfg.get("AB", 50000), BF16)

        if cfg.get("zero_yt"):
            z = k.tt(self.AB_, 512)
            k.op("dve", "memset", W=[z], ap=z.ap, constant=0.0)
            for j in jobs:
                for kt in range(16):
                    for t0 in range(0, j.N, 512):
                        n = min(512, j.N - t0)
                        k.dma("sp", j.YT[kt][:, t0:t0 + n], z[:, 0:n], R=[z], W=[j.bYT])
        for l in range(L):
            self.modvec(l)
        for l in range(L):
            for j in jobs:
                self.norm_phase(j, l, final=False)
            for j in jobs:
                if "A" in self.mixers:
                    self.mlstm_prepass(j, l)
                    for hp in range(4):
                        self.mlstm_unit(j, l, hp)
                if "B" in self.mixers:
                    for hp in range(4):
                        self.lru_unit(j, l, hp)
                if "C" in self.mixers:
                    for hp in range(4):
                        self.ret_unit(j, l, hp)
                if "D" in self.mixers:
                    self.rwkv_prepass(j, l)
                    for hp in range(4 if cfg.get("rw_stop", 9) > 0 else 0):
                        self.rwkv_unit(j, l, hp)
            for j in jobs:
                self.outproj_phase(j, l)
        for j in jobs:
            self.norm_phase(j, L, final=True)
        P.emit()
        P.close()
        return nc

    def new_phase(self):
        self.P.barrier()
        self.AF_.reset()
        self.AB_.reset()

    def modvec(self, l):
        k, P = self, self.P
        self.new_phase()
        AFa, ABa = self.AF_, self.AB_
        mrow = k.tt(AFa, 3 * D, parts=2)
        brow = k.tt(AFa, 3 * D, parts=2)
        grow = k.tt(AFa, D, parts=2)
        k.dma("sp", brow.ap, self.bmod2[l], W=[brow])
        k.dma("sp", grow.ap, self.ng2[l], W=[grow])
        wbufs = [k.tt(ABa, KT * 512) for _ in range(2)]
        for cb in range(6):
            wb = wbufs[cb % 2]
            k.dma("pool", v3(wb.ap, KT, 512), self.wmod[l][:, :, cb * 512:(cb + 1) * 512], W=[wb])
            bk = k.bank()
            for kt in range(KT):
                k.op("pe", "matmul", R=[wb, k.scT], W=[bk], out=bk[0:2, :],
                     lhsT=v3(k.scT.ap, KT, 2)[:, kt, :], rhs=v3(wb.ap, KT, 512)[:, kt, :],
                     start=(kt == 0), stop=(kt == KT - 1))
            k.op("dve", "tensor_tensor", R=[bk, brow], W=[mrow], out=mrow[:, cb * 512:(cb + 1) * 512],
                 in0=bk[0:2, :], in1=brow[:, cb * 512:(cb + 1) * 512], op=ALU.add)
        arow = k.tt(AFa, D, parts=2)
        k.op("dve", "scalar_tensor_tensor", R=[mrow, grow], W=[arow], out=arow.ap, in0=mrow[:, D:2 * D],
             scalar=1.0, in1=grow.ap, op0=ALU.add, op1=ALU.mult)
        k.dma("sp", self.MODS[l, :, 0, :], arow.ap, R=[arow], W=[self.bMODS])
        k.dma("sp", self.MODS[l, :, 1, :], mrow[:, 0:D], R=[mrow], W=[self.bMODS])
        k.dma("sp", self.MODS[l, :, 2, :], mrow[:, 2 * D:3 * D], R=[mrow], W=[self.bMODS])

    def bcast_row(self, dram_ap_1d, n):
        a = dram_ap_1d
        return bass.AP(a.tensor, a.offset, [[0, 128], [1, n]])

    def norm_phase(self, j, l, final):
        k = self
        self.new_phase()
        AFa, ABa = self.AF_, self.AB_
        GBk = min(4, j.NB)
        src = j.x_in if l == 0 else j.XR
        if final:
            modA = k.tt(AFa, D)
            k.dma("sp", modA.ap, self.fgb, W=[modA])
        else:
            modA, modS = k.tt(AFa, D), k.tt(AFa, D)
            k.dma("sp", modA.ap, self.bcast_row(self.MODS[l, j.g, 0, :], D), R=[self.bMODS], W=[modA])
            k.dma("sp", modS.ap, self.bcast_row(self.MODS[l, j.g, 1, :], D), R=[self.bMODS], W=[modS])
        xg = [k.tt(AFa, GBk * D) for _ in range(2)]
        t1 = [k.tt(AFa, D) for _ in range(2)]
        ss = [k.tt(AFa, GBk) for _ in range(2)]
        rs = [k.tt(AFa, GBk) for _ in range(2)]
        junk = k.tt(ABa, D)
        hb = [k.tt(ABa, D) for _ in range(2)]
        htg = [k.tt(ABa, KT * GBk * 128) for _ in range(2)]
        for gi in range(j.NB // GBk):
            x, s_, r_, ht = xg[gi % 2], ss[gi % 2], rs[gi % 2], htg[gi % 2]
            tok0 = gi * GBk * 128
            ntk = GBk * 128
            x3 = v3(x.ap, GBk, D)
            k.dma("sp", x3, src[tok0:tok0 + ntk, :].rearrange("(b p) d -> p b d", p=128),
                  R=[] if l == 0 else [j.bXR], W=[x])
            for b in range(GBk):
                k.op("act", "activation", R=[x], W=[junk, s_], out=junk.ap, in_=x3[:, b, :], func=AF.Square,
                     accum_out=s_[:, b:b + 1])
            k.op("act", "activation", R=[s_], W=[r_], out=r_.ap, in_=s_.ap, func=AF.Ln, scale=1.0 / D,
                 bias=k.eps.ap)
            k.op("act", "activation", R=[r_], W=[r_], out=r_.ap, in_=r_.ap, func=AF.Exp, scale=-0.5)
            for b in range(GBk):
                tt1 = t1[b % 2]
                k.op("dve", "scalar_tensor_tensor", R=[x, r_, modA], W=[tt1], out=tt1.ap, in0=x3[:, b, :],
                     scalar=r_[:, b:b + 1], in1=modA.ap, op0=ALU.mult, op1=ALU.mult)
                if final:
                    k.dma("sp", j.y_out[tok0 + b * 128: tok0 + (b + 1) * 128, :], tt1.ap, R=[tt1])
                    continue
                h = hb[b % 2]
                k.op("pool", "tensor_tensor", R=[tt1, modS], W=[h], out=h.ap, in0=tt1.ap, in1=modS.ap, op=ALU.add)
                bb = k.bbank()
                for kt in range(KT):
                    k.op("pe", "transpose", R=[h, k.identb], W=[bb], out=bb[:, kt * 128:(kt + 1) * 128],
                         in_=h[:, kt * 128:(kt + 1) * 128], identity=k.identb.ap)
                k.op("act", "activation", R=[bb], W=[ht], out=v3(ht.ap, KT, ntk)[:, :, b * 128:(b + 1) * 128],
                     in_=v3(bb.ap, KT, 128), func=AF.Copy)
            if not final:
                k.dma("sp", j.HT.rearrange("k p n -> p k n")[:, :, tok0:tok0 + ntk], v3(ht.ap, KT, ntk),
                      R=[ht], W=[j.bHT])

    def outproj_phase(self, j, l):
        k = self
        self.new_phase()
        AFa, ABa = self.AF_, self.AB_
        GBk = min(4, j.NB)
        src = j.x_in if l == 0 else j.XR
        modG = k.tt(AFa, D)
        k.dma("sp", modG.ap, self.bcast_row(self.MODS[l, j.g, 2, :], D), R=[self.bMODS], W=[modG])
        wo = k.tt(ABa, 16 * D)
        wo3 = v3(wo.ap, 16, D)
        for q in range(4):
            k.dma("pool", wo3[:, q * 4:(q + 1) * 4, :], self.wout[l][:, q * 4:(q + 1) * 4, :], W=[wo])
        xg = [k.tt(AFa, GBk * D) for _ in range(2)]
        tmp = [k.tt(AFa, 512) for _ in range(2)]
        ytg = [k.tt(ABa, 16 * GBk * 128) for _ in range(2)]
        for gi in range(j.NB // GBk):
            x, yt = xg[gi % 2], ytg[gi % 2]
            tok0 = gi * GBk * 128
            ntk = GBk * 128
            x3 = v3(x.ap, GBk, D)
            yt3 = v3(yt.ap, 16, ntk)
            k.dma("sp", x3, src[tok0:tok0 + ntk, :].rearrange("(b p) d -> p b d", p=128),
                  R=[] if l == 0 else [j.bXR], W=[x])
            k.dma("sp", yt3, j.YT.rearrange("k p n -> p k n")[:, :, tok0:tok0 + ntk], R=[j.bYT], W=[yt])
            for b in range(GBk):
                for hf in range(2):
                    bk = k.bank()
                    for kt in range(16):
                        k.op("pe", "matmul", R=[yt, wo], W=[bk], out=bk.ap, lhsT=yt3[:, kt, b * 128:(b + 1) * 128],
                             rhs=wo3[:, kt, hf * 512:(hf + 1) * 512], start=(kt == 0), stop=(kt == 15))
                    tm = tmp[hf]
                    k.op("dve", "tensor_tensor", R=[bk, modG], W=[tm], out=tm.ap, in0=bk.ap,
                         in1=modG[:, hf * 512:(hf + 1) * 512], op=ALU.mult)
                    k.op("pool", "tensor_tensor", R=[tm, x], W=[x], out=x3[:, b, hf * 512:(hf + 1) * 512],
                         in0=tm.ap, in1=x3[:, b, hf * 512:(hf + 1) * 512], op=ALU.add)
            k.dma("sp", j.XR[tok0:tok0 + ntk, :].rearrange("(b p) d -> p b d", p=128), x3, R=[x], W=[j.bXR])

    def stream_ht(self, j, fn):
        k = self
        nblk = (j.N + 511) // 512
        hts = [k.tt(self.AB_, KT * 512) for _ in range(2)]
        for b5 in range(nblk):
            tok0 = b5 * 512
            nt = min(512, j.N - tok0)
            ht = hts[b5 % 2]
            h3 = v3(ht.ap, KT, 512)[:, :, 0:nt]
            k.dma("sp", h3, j.HT.rearrange("k p n -> p k n")[:, :, tok0:tok0 + nt], R=[j.bHT], W=[ht])
            fn(b5, tok0, nt, h3, ht)

    def mm_fm(self, bk, out_ap, W, W3, c0, ncol, h3, ht, nt):
        for kt in range(KT):
            self.op("pe", "matmul", R=[W, ht], W=[bk], out=out_ap, lhsT=W3[:, kt, c0:c0 + ncol],
                    rhs=h3[:, kt, 0:nt], start=(kt == 0), stop=(kt == KT - 1))

    def mm_tm(self, bk, out_ap, W, W3, c0, ncol, h3, ht, t0):
        for kt in range(KT):
            self.op("pe", "matmul", R=[W, ht], W=[bk], out=out_ap, lhsT=h3[:, kt, t0:t0 + 128],
                    rhs=W3[:, kt, c0:c0 + ncol], start=(kt == 0), stop=(kt == KT - 1))

    def load_w(self, dram_ap, ncol):
        W = self.tt(self.AB_, KT * ncol)
        W3 = v3(W.ap, KT, ncol)
        self.dma("pool", W3, dram_ap, W=[W])
        return W, W3

    def segs(self, j, d):
        SB = min(self.SEGB, j.NB)
        lst = [(b0, min(SB, j.NB - b0)) for b0 in range(0, j.NB, SB)]
        return lst if d == 0 else lst[::-1]

    def yt_store(self, j, mixer, hp, tok0, ntk, yt_tt, yt_ap):
        kt = mixer * 4 + hp
        self.dma("sp", j.YT[kt][:, tok0:tok0 + ntk], yt_ap, R=[yt_tt], W=[j.bYT])
        if self.dbg:
            pass

    def lru_unit(self, j, l, hp):
        k = self
        self.new_phase()
        AFa, ABa = self.AF_, self.AB_
        N = j.N
        W, W3 = self.load_w(self.wB[l, hp], 256)
        lp = k.tt(AFa, 12)
        k.dma("sp", lp.ap, self.lruP[l, hp], W=[lp])
        gwf = k.tt(AFa, 512)
        k.dma("sp", v3(gwf.ap, 4, 128), self.lruG[l, hp], W=[gwf])
        gw = k.tt(ABa, 512)
        k.op("act", "activation", R=[gwf], W=[gw], out=gw.ap, in_=gwf.ap, func=AF.Copy)
        gw3 = v3(gw.ap, 4, 128)
        nsp = k.tt(AFa, 2)
        k.op("act", "activation", R=[lp], W=[nsp], out=nsp.ap, in_=lp[:, 9:11], func=AF.Exp, scale=-1.0)
        k.op("act", "activation", R=[nsp], W=[nsp], out=nsp.ap, in_=nsp.ap, func=AF.Ln, bias=1.0)
        k.op("dve", "tensor_scalar", R=[nsp], W=[nsp], out=nsp.ap, in0=nsp.ap, scalar1=-8.0, scalar2=None,
             op0=ALU.mult)
        XB = k.tt(AFa, N + 3)
        XC = k.tt(AFa, N)
        HF = k.tt(AFa, N)
        XCb = k.tt(ABa, N)
        SG = k.tt(ABa, N)
        k.op("dve", "memset", W=[XB], ap=XB[:, 0:1], constant=0.0)
        k.op("dve", "memset", W=[XB], ap=XB[:, N + 1:N + 3], constant=0.0)

        def blk(b5, tok0, nt, h3, ht):
            bk = k.bank()
            k.mm_fm(bk, bk[:, 0:nt], W, W3, 0, 128, h3, ht, nt)
            k.op("act", "activation", R=[bk], W=[XB], out=XB[:, 1 + tok0:1 + tok0 + nt], in_=bk[:, 0:nt], func=AF.Copy)
            bk2 = k.bank()
            k.mm_fm(bk2, bk2[:, 0:nt], W, W3, 128, 128, h3, ht, nt)
            k.op("act", "activation", R=[bk2], W=[SG], out=SG[:, tok0:tok0 + nt], in_=bk2[:, 0:nt], func=AF.Silu)
        self.stream_ht(j, blk)
        k.op("dve", "tensor_scalar", R=[XB, lp], W=[XC], out=XC.ap, in0=XB[:, 0:N], scalar1=lp[:, 0:1],
             scalar2=lp[:, 4:5], op0=ALU.mult, op1=ALU.add)
        for t in range(1, 4):
            k.op("dve", "scalar_tensor_tensor", R=[XB, lp, XC], W=[XC], out=XC.ap, in0=XB[:, t:t + N],
                 scalar=lp[:, t:t + 1], in1=XC.ap, op0=ALU.mult, op1=ALU.add)
        k.op("pool", "tensor_copy", R=[XC], W=[XCb], out=XCb.ap, in_=XC.ap)
        SEG = min(self.SEGB, j.NB) * 128
        tm = {n: [k.tt(AFa, SEG) for _ in range(2)] for n in ("sr", "si", "a", "u", "bt", "h")}
        carry = k.tt(AFa, 2)
        for d in range(2):
            if j.kind == "s":
                k.dma("sp", carry[:, d:d + 1], self.st_in["h"][l, d, hp * 128:(hp + 1) * 128].unsqueeze(1), W=[carry])
            else:
                k.op("dve", "memset", W=[carry], ap=carry[:, d:d + 1], constant=0.0)
        ytb = [k.tt(ABa, SEG) for _ in range(2)]
        for d in range(2):
            cur = (carry, carry[:, d:d + 1])
            for si_, (b0, nb) in enumerate(self.segs(j, d)):
                s0, n = b0 * 128, nb * 128
                T = {nme: tm[nme][si_ % 2] for nme in tm}
                bkr, bki = k.bank(), k.bank()
                k.op("pe", "matmul", R=[gw, XCb], W=[bkr], out=bkr[:, 0:n], lhsT=gw3[:, d * 2 + 0, :],
                     rhs=XCb[:, s0:s0 + n], start=True, stop=True)
                k.op("pe", "matmul", R=[gw, XCb], W=[bki], out=bki[:, 0:n], lhsT=gw3[:, d * 2 + 1, :],
                     rhs=XCb[:, s0:s0 + n], start=True, stop=True)
                k.op("act", "activation", R=[bkr, lp], W=[T["sr"]], out=T["sr"][:, 0:n], in_=bkr[:, 0:n],
                     func=AF.Sigmoid, bias=lp[:, 5 + d * 2:6 + d * 2])
                k.op("act", "activation", R=[bki, lp], W=[T["si"]], out=T["si"][:, 0:n], in_=bki[:, 0:n],
                     func=AF.Sigmoid, bias=lp[:, 6 + d * 2:7 + d * 2])
                k.op("act", "activation", R=[T["sr"], nsp], W=[T["a"]], out=T["a"][:, 0:n], in_=T["sr"][:, 0:n],
                     func=AF.Exp, scale=nsp[:, d:d + 1])
                k.op("dve", "scalar_tensor_tensor", R=[T["a"]], W=[T["u"]], out=T["u"][:, 0:n], in0=T["a"][:, 0:n],
                     scalar=0.99999994, in1=T["a"][:, 0:n], op0=ALU.min, op1=ALU.mult)
                k.op("act", "activation", R=[T["u"]], W=[T["bt"]], out=T["bt"][:, 0:n], in_=T["u"][:, 0:n],
                     func=AF.Ln, scale=-1.0, bias=1.0)
                k.op("act", "activation", R=[T["bt"]], W=[T["bt"]], out=T["bt"][:, 0:n], in_=T["bt"][:, 0:n],
                     func=AF.Exp, scale=0.5)
                k.op("dve", "tensor_tensor", R=[T["si"], XC], W=[T["si"]], out=T["si"][:, 0:n], in0=T["si"][:, 0:n],
                     in1=XC[:, s0:s0 + n], op=ALU.mult)
                k.op("pool", "tensor_tensor", R=[T["si"], T["bt"]], W=[T["bt"]], out=T["bt"][:, 0:n],
                     in0=T["si"][:, 0:n], in1=T["bt"][:, 0:n], op=ALU.mult)
                a_ap, b_ap, h_ap = T["a"][:, 0:n], T["bt"][:, 0:n], T["h"][:, 0:n]
                if d == 1:
                    a_ap, b_ap, h_ap = flip(a_ap), flip(b_ap), flip(h_ap)
                k.op("dve", "tensor_tensor_scan", R=[T["a"], T["bt"], cur[0]], W=[T["h"]], out=h_ap, data0=a_ap,
                     data1=b_ap, initial=cur[1], op0=ALU.mult, op1=ALU.add)
                cur = (T["h"], T["h"][:, n - 1:n] if d == 0 else T["h"][:, 0:1])
                if d == 0:
                    k.op("pool", "tensor_copy", R=[T["h"]], W=[HF], out=HF[:, s0:s0 + n], in_=T["h"][:, 0:n])
                else:
                    yt = ytb[si_ % 2]
                    k.op("dve", "tensor_tensor", R=[T["h"], HF], W=[T["u"]], out=T["u"][:, 0:n], in0=T["h"][:, 0:n],
                         in1=HF[:, s0:s0 + n], op=ALU.add)
                    k.op("pool", "tensor_tensor", R=[T["u"], SG], W=[yt], out=yt[:, 0:n], in0=T["u"][:, 0:n],
                         in1=SG[:, s0:s0 + n], op=ALU.mult)
                    k.yt_store(j, 1, hp, s0, n, yt, yt[:, 0:n])
            if j.kind == "p":
                k.dma("sp", self.so["h"][j.idx, l, d, hp * 128:(hp + 1) * 128].unsqueeze(1), cur[1], R=[cur[0]])

    def tail(self, j, mixer, hp, s0, nb, X, SGap, SG, tmps, post=None):
        k = self
        n = nb * 128
        sq, ss, yb, yt = tmps
        k.op("act", "activation", R=[X], W=[sq], out=sq[:, 0:n], in_=X[:, 0:n], func=AF.Square)
        k.op("dve", "tensor_reduce", R=[sq], W=[ss], out=ss[:, 0:nb * 2], in_=v3(sq[:, 0:n], nb * 2, 64),
             axis=AX.X, op=ALU.add)
        k.op("act", "activation", R=[ss], W=[ss], out=ss[:, 0:nb * 2], in_=ss[:, 0:nb * 2], func=AF.Ln,
             scale=1.0 / 64, bias=k.eps.ap)
        k.op("act", "activation", R=[ss], W=[ss], out=ss[:, 0:nb * 2], in_=ss[:, 0:nb * 2], func=AF.Exp, scale=-0.5)
        if SGap is not None:
            k.op("dve", "tensor_tensor", R=[X, ss], W=[sq], out=v3(sq[:, 0:n], nb * 2, 64),
                 in0=v3(X[:, 0:n], nb * 2, 64), in1=bc_last(ss[:, 0:nb * 2], 64), op=ALU.mult)
            k.op("pool", "tensor_tensor", R=[sq, SG], W=[yb], out=yb[:, 0:n], in0=sq[:, 0:n], in1=SGap, op=ALU.mult)
        else:
            k.op("dve", "tensor_tensor", R=[X, ss], W=[yb], out=v3(yb[:, 0:n], nb * 2, 64),
                 in0=v3(X[:, 0:n], nb * 2, 64), in1=bc_last(ss[:, 0:nb * 2], 64), op=ALU.mult)
        bb = k.bbank()
        for b in range(nb):
            k.op("pe", "transpose", R=[yb, k.identb], W=[bb], out=bb[:, b * 128:(b + 1) * 128],
                 in_=yb[:, b * 128:(b + 1) * 128], identity=k.identb.ap)
        if post is None:
            k.op("act", "activation", R=[bb], W=[yt], out=yt[:, 0:n], in_=bb[:, 0:n], func=AF.Copy)
        else:
            post(bb, yt, n)
        k.yt_store(j, mixer, hp, s0, n, yt, yt[:, 0:n])

    def tail_tmps(self, SEG, nbuf=2):
        k = self
        return [(k.tt(self.AF_, SEG), k.tt(self.AF_, 16), k.tt(self.AB_, SEG), k.tt(self.AB_, SEG)) for _ in range(nbuf)]

    def ret_unit(self, j, l, hp):
        k = self
        self.new_phase()
        AFa, ABa = self.AF_, self.AB_
        N, NB = j.N, j.NB
        rope = j.kind == "s"
        W, W3 = self.load_w(self.wC[l, hp], 768)
        rp = k.tt(AFa, 6)
        k.dma("sp", rp.ap, self.retP[l, hp], W=[rp])
        lg = k.tt(AFa, 6)
        k.op("act", "activation", R=[rp], W=[lg], out=lg.ap, in_=rp.ap, func=AF.Exp, scale=-1.0)
        k.op("act", "activation", R=[lg], W=[lg], out=lg.ap, in_=lg.ap, func=AF.Ln, bias=1.0)
        k.op("dve", "tensor_scalar", R=[lg], W=[lg], out=lg.ap, in0=lg.ap, scalar1=-1.0, scalar2=None, op0=ALU.mult)
        DT = [k.tt(AFa, 256) for _ in range(2)]
        XI = [k.tt(AFa, 128) for _ in range(2)]
        WK = [k.tt(AFa, 128) for _ in range(2)]
        dec = k.tt(AFa, 2)
        diff3, mask3, pos3 = v3(k.diff.ap, 2, 128), v3(k.masks.ap, 4, 128), v3(k.pos.ap, 4, 128)
        for d in range(2):
            for h in range(2):
                dt = v3(DT[d].ap, 2, 128)[:, h, :]
                k.op("act", "activation", R=[k.diff, lg], W=[DT[d]], out=dt, in_=diff3[:, d, :], func=AF.Exp,
                     scale=lg[:, 2 + d * 2 + h:3 + d * 2 + h])
                k.op("dve", "tensor_tensor", R=[DT[d], k.masks], W=[DT[d]], out=dt, in0=dt, in1=mask3[:, d, :],
                     op=ALU.mult)
            k.op("act", "activation", R=[k.pos, lg], W=[XI[d]], out=XI[d].ap, in_=pos3[:, d, :], func=AF.Exp,
                 scale=lg[:, d:d + 1])
            k.op("act", "activation", R=[k.pos, lg], W=[WK[d]], out=WK[d].ap, in_=pos3[:, 2 + d, :], func=AF.Exp,
                 scale=lg[:, d:d + 1])
        k.op("act", "activation", R=[lg], W=[dec], out=dec.ap, in_=lg[:, 0:2], func=AF.Exp, scale=128.0)
        QT, KT_ = k.tt(ABa, N), k.tt(ABa, N)
        V, SG = k.tt(ABa, N), k.tt(ABa, N)
        YF = k.tt(AFa, N)
        V3_, SG3 = v3(V.ap, NB, 128), v3(SG.ap, NB, 128)
        if rope:
            cs = [k.tt(AFa, 512) for _ in range(2)]
            sn = [k.tt(AFa, 512) for _ in range(2)]
            rt = [k.tt(AFa, 512) for _ in range(4)]

        def blk(b5, tok0, nt, h3, ht):
            if rope:
                c_, s_ = cs[b5 % 2], sn[b5 % 2]
                k.dma("sp", c_[:, 0:nt], self.rope[0][:, tok0:tok0 + nt], W=[c_])
                k.dma("sp", s_[:, 0:nt], self.rope[1][:, tok0:tok0 + nt], W=[s_])
            for qi, (dst, scl) in enumerate(((QT, 1.0), (KT_, 0.125))):
                bka = k.bank()
                k.mm_fm(bka, bka[:, 0:nt], W, W3, qi * 256, 128, h3, ht, nt)
                if not rope:
                    k.op("act", "activation", R=[bka], W=[dst], out=dst[:, tok0:tok0 + nt], in_=bka[:, 0:nt],
                         func=AF.Copy, scale=scl)
                    continue
                bkb = k.bank()
                k.mm_fm(bkb, bkb[:, 0:nt], W, W3, qi * 256 + 128, 128, h3, ht, nt)
                t1, t2 = rt[qi * 2], rt[qi * 2 + 1]
                k.op("dve", "scalar_tensor_tensor", R=[bka, c_], W=[t1], out=t1[:, 0:nt], in0=bka[:, 0:nt], scalar=scl,
                     in1=c_[:, 0:nt], op0=ALU.mult, op1=ALU.mult)
                k.op("dve", "scalar_tensor_tensor", R=[bkb, s_], W=[t2], out=t2[:, 0:nt], in0=bkb[:, 0:nt], scalar=scl,
                     in1=s_[:, 0:nt], op0=ALU.mult, op1=ALU.mult)
                k.op("pool", "tensor_tensor", R=[t1, t2], W=[dst], out=dst[:, tok0:tok0 + nt], in0=t1[:, 0:nt],
                     in1=t2[:, 0:nt], op=ALU.add)
            for jj in range(0, nt // 128, 2):
                nj = min(2, nt // 128 - jj)
                bk = k.bank()
                for q in range(nj):
                    k.mm_tm(bk, bk[:, q * 256:(q + 1) * 256], W, W3, 512, 256, h3, ht, (jj + q) * 128)
                blk0 = tok0 // 128 + jj
                pv = v3(bk[:, 0:nj * 256], nj, 256)
                k.op("act", "activation", R=[bk], W=[V], out=V3_[:, blk0:blk0 + nj, :], in_=pv[:, :, 0:128], func=AF.Copy)
                k.op("act", "activation", R=[bk], W=[SG], out=SG3[:, blk0:blk0 + nj, :], in_=pv[:, :, 128:256],
                     func=AF.Silu)
        self.stream_ht(j, blk)
        SB = min(self.SEGB, NB)
        SEG = SB * 128
        qs_ = [k.tt(ABa, SEG) for _ in range(2)]
        ks_ = [k.tt(ABa, SEG) for _ in range(2)]
        ktk = [k.tt(ABa, SEG) for _ in range(2)]
        RS = [k.tt(ABa, SB * 64) for _ in range(2)]
        Sm = [[k.tt(ABa, SEG) for _ in range(2)] for _ in range(2)]
        ysum = [k.tt(AFa, SEG) for _ in range(2)]
        ttm = self.tail_tmps(SEG)
        Rst = k.tt(AFa, 64)
        for d in range(2):
            if j.kind == "s":
                k.dma("sp", Rst.ap, self.st_in["r"][l, d, 2 * hp:2 * hp + 2].rearrange("h a v -> (h a) v"), W=[Rst])
            else:
                k.op("dve", "memset", W=[Rst], ap=Rst.ap, constant=0.0)
            for si_, (b0, nb) in enumerate(self.segs(j, d)):
                s0, n = b0 * 128, nb * 128
                pq, pk, pt, prs = qs_[si_ % 2], ks_[si_ % 2], ktk[si_ % 2], RS[si_ % 2]
                k.op("dve", "tensor_tensor", R=[QT, XI[d]], W=[pq], out=v3(pq[:, 0:n], nb, 128),
                     in0=v3(QT[:, s0:s0 + n], nb, 128), in1=bc_mid(XI[d].ap, nb), op=ALU.mult)
                k.op("pool", "tensor_tensor", R=[KT_, WK[d]], W=[pk], out=v3(pk[:, 0:n], nb, 128),
                     in0=v3(KT_[:, s0:s0 + n], nb, 128), in1=bc_mid(WK[d].ap, nb), op=ALU.mult)
                bb = k.bbank()
                for b in range(nb):
                    k.op("pe", "transpose", R=[pk, k.identb], W=[bb], out=bb[:, b * 128:(b + 1) * 128],
                         in_=pk[:, b * 128:(b + 1) * 128], identity=k.identb.ap)
                k.op("act", "activation", R=[bb], W=[pt], out=pt[:, 0:n], in_=bb[:, 0:n], func=AF.Copy)
                bdr = k.bank()
                for b in range(nb):
                    for h in range(2):
                        k.op("pe", "matmul", R=[pt, V], W=[bdr], out=bdr[h * 64:(h + 1) * 64, b * 64:(b + 1) * 64],
                             lhsT=pt[:, b * 128 + h * 64:b * 128 + (h + 1) * 64], rhs=V3_[:, b0 + b, h * 64:(h + 1) * 64],
                             start=True, stop=True)
                order = range(nb) if d == 0 else range(nb - 1, -1, -1)
                for b in order:
                    k.op("pool", "tensor_copy", R=[Rst], W=[prs], out=prs[:, b * 64:(b + 1) * 64], in_=Rst.ap)
                    k.op("dve", "scalar_tensor_tensor", R=[Rst, dec, bdr], W=[Rst], out=Rst.ap, in0=Rst.ap,
                         scalar=dec[:, d:d + 1], in1=bdr[:, b * 64:(b + 1) * 64], op0=ALU.mult, op1=ALU.add)
                for h in range(2):
                    bst = k.bank()
                    for b in range(nb):
                        tk = slice(s0 + b * 128, s0 + (b + 1) * 128)
                        k.op("pe", "matmul", R=[KT_, QT], W=[bst], out=bst[:, b * 128:(b + 1) * 128],
                             lhsT=KT_[h * 64:(h + 1) * 64, tk], rhs=QT[h * 64:(h + 1) * 64, tk], start=True, stop=True)
                    sm = Sm[si_ % 2][h]
                    k.op("dve", "tensor_tensor", R=[bst, DT[d]], W=[sm], out=v3(sm[:, 0:n], nb, 128),
                         in0=v3(bst[:, 0:n], nb, 128), in1=bc_mid(v3(DT[d].ap, 2, 128)[:, h, :], nb), op=ALU.mult)
                bo = k.bank()
                for b in range(nb):
                    for h in range(2):
                        sm = Sm[si_ % 2][h]
                        oo = bo[:, b * 128 + h * 64:b * 128 + (h + 1) * 64]
                        k.op("pe", "matmul", R=[sm, V], W=[bo], out=oo, lhsT=sm[:, b * 128:(b + 1) * 128],
                             rhs=V3_[:, b0 + b, h * 64:(h + 1) * 64], start=True, stop=False)
                        k.op("pe", "matmul", R=[pq, prs], W=[bo], out=oo, lhsT=pq[h * 64:(h + 1) * 64, b * 128:(b + 1) * 128],
                             rhs=prs[h * 64:(h + 1) * 64, b * 64:(b + 1) * 64], start=False, stop=True)
                if d == 0:
                    k.op("act", "activation", R=[bo], W=[YF], out=YF[:, s0:s0 + n], in_=bo[:, 0:n], func=AF.Copy)
                else:
                    ys_ = ysum[si_ % 2]
                    k.op("dve", "tensor_tensor", R=[bo, YF], W=[ys_], out=ys_[:, 0:n], in0=bo[:, 0:n],
                         in1=YF[:, s0:s0 + n], op=ALU.add)
                    self.tail(j, 2, hp, s0, nb, ys_, SG[:, s0:s0 + n], SG, ttm[si_ % 2])
            if j.kind == "p":
                k.dma("sp", self.so["r"][j.idx, l, d, 2 * hp:2 * hp + 2].rearrange("h a v -> (h a) v"), Rst.ap, R=[Rst])

    def mlstm_prepass(self, j, l):
        k = self
        self.new_phase()
        AFa, ABa = self.AF_, self.AB_
        N, NB = j.N, j.NB
        W, W3 = self.load_w(self.wAg[l], 32)
        gbt = k.tt(AFa, 4, parts=8)
        ngb = k.tt(AFa, 4, parts=8)
        k.dma("sp", gbt.ap, self.gbA[l], W=[gbt])
        k.op("dve", "tensor_scalar", R=[gbt], W=[ngb], out=ngb.ap, in0=gbt.ap, scalar1=-1.0, scalar2=None, op0=ALU.mult)
        GS = [k.tt(AFa, NB, parts=8) for _ in range(2)]
        BL = [k.tt(AFa, NB, parts=8) for _ in range(2)]
        rm3 = v3(k.rmask.ap, 2, 512)
        tsp = [k.tt(AFa, 512, parts=8) for _ in range(2)]
        tb = [k.tt(AFa, 512, parts=8) for _ in range(2)]
        tg = [k.tt(AFa, 512, parts=8) for _ in range(2)]
        tpm = [k.tt(AFa, 512, parts=8) for _ in range(2)]

        def blk(b5, tok0, nt, h3, ht):
            nch = nt // 128
            c0 = tok0 // 128
            bks = []
            for gi in range(4):
                bk = k.bank()
                k.mm_fm(bk, bk[0:8, 0:nt], W, W3, gi * 8, 8, h3, ht, nt)
                bks.append(bk)
            for d in range(2):
                ig, fg = bks[2 * d], bks[2 * d + 1]
                sp_, b_, g_, pm_ = tsp[d], tb[d], tg[d], tpm[d]
                fl = (lambda a: a) if d == 0 else flip
                k.op("act", "activation", R=[fg, ngb], W=[sp_], out=sp_[:, 0:nt], in_=fg[0:8, 0:nt], func=AF.Exp,
                     scale=-1.0, bias=ngb[:, 2 * d + 1:2 * d + 2])
                k.op("act", "activation", R=[sp_], W=[sp_], out=sp_[:, 0:nt], in_=sp_[:, 0:nt], func=AF.Ln, bias=1.0)
                k.op("dve", "tensor_tensor_scan", R=[sp_, k.rmask], W=[b_], out=fl(b_[:, 0:nt]), data0=rm3[0:8, 0, 0:nt],
                     data1=fl(sp_[:, 0:nt]), initial=0.0, op0=ALU.mult, op1=ALU.subtract)
                k.op("dve", "scalar_tensor_tensor", R=[ig, gbt, b_], W=[g_], out=g_[:, 0:nt], in0=ig[0:8, 0:nt],
                     scalar=gbt[:, 2 * d:2 * d + 1], in1=b_[:, 0:nt], op0=ALU.add, op1=ALU.subtract)
                k.op("dve", "tensor_tensor_scan", R=[g_, k.rmask], W=[pm_], out=fl(pm_[:, 0:nt]), data0=rm3[0:8, 1, 0:nt],
                     data1=fl(g_[:, 0:nt]), initial=0.0, op0=ALU.add, op1=ALU.max)
                e = 127 if d == 0 else 0
                k.op("pool", "tensor_copy", R=[pm_], W=[GS[d]], out=GS[d][:, c0:c0 + nch],
                     in_=v3(pm_[:, 0:nt], nch, 128)[:, :, e])
                k.op("pool", "tensor_copy", R=[b_], W=[BL[d]], out=BL[d][:, c0:c0 + nch],
                     in_=v3(b_[:, 0:nt], nch, 128)[:, :, e])
                k.dma("sp", j.GB[d, 0, :, tok0:tok0 + nt], g_[:, 0:nt], R=[g_], W=[j.bGB])
                k.dma("sp", j.GB[d, 1, :, tok0:tok0 + nt], b_[:, 0:nt], R=[b_], W=[j.bGB])
        self.stream_ht(j, blk)
        ML = [k.tt(AFa, NB, parts=8) for _ in range(2)]
        for d in range(2):
            fl = (lambda a: a) if d == 0 else flip
            m0 = k.tt(AFa, 1, parts=8)
            if j.kind == "s":
                k.dma("sp", m0.ap, self.st_in["m"][l, d, :].unsqueeze(1), W=[m0])
            else:
                k.op("dve", "memset", W=[m0], ap=m0.ap, constant=0.0)
            Mall, MP = k.tt(AFa, NB, parts=8), k.tt(AFa, NB, parts=8)
            k.op("dve", "tensor_tensor_scan", R=[GS[d], BL[d], m0], W=[Mall], out=fl(Mall.ap), data0=fl(GS[d].ap),
                 data1=fl(BL[d].ap), initial=m0.ap, op0=ALU.max, op1=ALU.add)
            if d == 0:
                k.op("pool", "tensor_copy", R=[m0], W=[MP], out=MP[:, 0:1], in_=m0.ap)
                if NB > 1:
                    k.op("pool", "tensor_copy", R=[Mall], W=[MP], out=MP[:, 1:NB], in_=Mall[:, 0:NB - 1])
                k.op("pool", "tensor_copy", R=[Mall], W=[j.mfin[d]], out=j.mfin[d].ap, in_=Mall[:, NB - 1:NB])
            else:
                k.op("pool", "tensor_copy", R=[m0], W=[MP], out=MP[:, NB - 1:NB], in_=m0.ap)
                if NB > 1:
                    k.op("pool", "tensor_copy", R=[Mall], W=[MP], out=MP[:, 0:NB - 1], in_=Mall[:, 1:NB])
                k.op("pool", "tensor_copy", R=[Mall], W=[j.mfin[d]], out=j.mfin[d].ap, in_=Mall[:, 0:1])
            k.op("dve", "tensor_tensor", R=[MP, GS[d]], W=[ML[d]], out=ML[d].ap, in0=MP.ap, in1=GS[d].ap, op=ALU.max)
            k.op("dve", "tensor_tensor", R=[MP, ML[d]], W=[MP], out=MP.ap, in0=MP.ap, in1=ML[d].ap, op=ALU.subtract)
            k.op("act", "activation", R=[MP], W=[j.Fch[d]], out=j.Fch[d].ap, in_=MP.ap, func=AF.Exp)
            if j.kind == "p":
                k.dma("sp", self.so["m"][j.idx, l, d, :].unsqueeze(1), j.mfin[d].ap, R=[j.mfin[d]])
        ta = [k.tt(AFa, 512, parts=8) for _ in range(2)]
        te = [k.tt(AFa, 512, parts=8) for _ in range(2)]
        for b5 in range((N + 511) // 512):
            tok0 = b5 * 512
            nt = min(512, N - tok0)
            nch, c0 = nt // 128, tok0 // 128
            for d in range(2):
                a_, e_ = ta[d], te[d]
                k.dma("sp", a_[:, 0:nt], j.GB[d, 0, :, tok0:tok0 + nt], R=[j.bGB], W=[a_])
                k.dma("sp", e_[:, 0:nt], j.GB[d, 1, :, tok0:tok0 + nt], R=[j.bGB], W=[e_])
                mlb = bc_last(ML[d][:, c0:c0 + nch], 128)
                k.op("dve", "tensor_tensor", R=[a_, ML[d]], W=[a_], out=v3(a_[:, 0:nt], nch, 128),
                     in0=v3(a_[:, 0:nt], nch, 128), in1=mlb, op=ALU.subtract)
                k.op("act", "activation", R=[a_], W=[a_], out=a_[:, 0:nt], in_=a_[:, 0:nt], func=AF.Exp)
                k.op("dve", "tensor_tensor", R=[e_, ML[d]], W=[e_], out=v3(e_[:, 0:nt], nch, 128),
                     in0=v3(e_[:, 0:nt], nch, 128), in1=mlb, op=ALU.add)
                k.op("act", "activation", R=[e_], W=[e_], out=e_[:, 0:nt], in_=e_[:, 0:nt], func=AF.Exp, scale=-1.0)
                bk = k.bank()
                for c in range(nch):
                    k.op("pe", "transpose", R=[a_, k.ident], W=[bk], out=bk[:, c * 8:(c + 1) * 8],
                         in_=a_[:, c * 128:(c + 1) * 128], identity=k.ident[0:8, 0:8])
                    k.op("pe", "transpose", R=[e_, k.ident], W=[bk], out=bk[:, 64 + c * 8:64 + (c + 1) * 8],
                         in_=e_[:, c * 128:(c + 1) * 128], identity=k.ident[0:8, 0:8])
                k.op("act", "activation", R=[bk], W=[j.Atok[d]], out=j.Atok[d][:, c0 * 8:(c0 + nch) * 8],
                     in_=bk[:, 0:nch * 8], func=AF.Copy)
                k.op("act", "activation", R=[bk], W=[j.Etok[d]], out=j.Etok[d][:, c0 * 8:(c0 + nch) * 8],
                     in_=bk[:, 64:64 + nch * 8], func=AF.Copy)

    def mlstm_unit(self, j, l, hp):
        k = self
        self.new_phase()
        AFa, ABa = self.AF_, self.AB_
        N, NB = j.N, j.NB
        W, W3 = self.load_w(self.wA[l, hp], 640)
        QT, KT_ = k.tt(ABa, N), k.tt(ABa, N)
        Ktok = k.tt(ABa, N)
        VA0 = k.tt(ABa, NB * 130)
        SO, SG = k.tt(ABa, N), k.tt(ABa, N)
        HF = k.tt(AFa, N)
        VA04 = VA0.ap.rearrange("p (b h c) -> p b h c", b=NB, h=2, c=65)
        SO3, SG3 = v3(SO.ap, NB, 128), v3(SG.ap, NB, 128)
        k.op("dve", "memset", W=[VA0], ap=v3(VA0.ap, NB * 2, 65)[:, :, 64:65], constant=1.0)

        def blk(b5, tok0, nt, h3, ht):
            for qi, (dst, scl) in enumerate(((QT, 1.0), (KT_, 0.125))):
                bka = k.bank()
                k.mm_fm(bka, bka[:, 0:nt], W, W3, qi * 128, 128, h3, ht, nt)
                k.op("act", "activation", R=[bka], W=[dst], out=dst[:, tok0:tok0 + nt], in_=bka[:, 0:nt],
                     func=AF.Copy, scale=scl)
            bb = k.bbank()
            for b in range(nt // 128):
                k.op("pe", "transpose", R=[KT_, k.identb], W=[bb], out=bb[:, b * 128:(b + 1) * 128],
                     in_=KT_[:, tok0 + b * 128:tok0 + (b + 1) * 128], identity=k.identb.ap)
            k.op("act", "activation", R=[bb], W=[Ktok], out=Ktok[:, tok0:tok0 + nt], in_=bb[:, 0:nt], func=AF.Copy)
            for jj in range(nt // 128):
                bk = k.bank()
                k.mm_tm(bk, bk[:, 0:384], W, W3, 256, 384, h3, ht, jj * 128)
                bi = tok0 // 128 + jj
                k.op("act", "activation", R=[bk], W=[VA0], out=VA04[:, bi, :, 0:64], in_=v3(bk[:, 0:128], 2, 64),
                     func=AF.Copy)
                k.op("act", "activation", R=[bk], W=[SO], out=SO3[:, bi, :], in_=bk[:, 128:256], func=AF.Sigmoid)
                k.op("act", "activation", R=[bk], W=[SG], out=SG3[:, bi, :], in_=bk[:, 256:384], func=AF.Silu)
        self.stream_ht(j, blk)
        SB = min(self.SEGB, NB)
        SEG = SB * 128
        VA = [k.tt(ABa, SB * 130) for _ in range(2)]
        CS = [k.tt(ABa, SB * 65) for _ in range(2)]
        Sm = [[k.tt(ABa, SEG) for _ in range(2)] for _ in range(2)]
        den = [k.tt(AFa, 8) for _ in range(2)]
        hd = [k.tt(AFa, 256) for _ in range(2)]
        X = [k.tt(AFa, SEG) for _ in range(2)]
        ttm = self.tail_tmps(SEG)
        C = k.tt(AFa, 65)
        Fbc = k.tt(AFa, NB)
        mask3 = v3(k.masksb.ap, 4, 128)
        for d in range(2):
            bkf = k.bank()
            k.op("pe", "matmul", R=[k.sel, j.Fch[d]], W=[bkf], out=bkf[:, 0:NB], lhsT=v3(k.sel.ap, 4, 128)[:, hp, :],
                 rhs=j.Fch[d].ap, start=True, stop=True)
            k.op("act", "activation", R=[bkf], W=[Fbc], out=Fbc.ap, in_=bkf[:, 0:NB], func=AF.Copy)
            if j.kind == "s":
                k.dma("sp", C[:, 0:64], self.st_in["c"][l, d, 2 * hp:2 * hp + 2].rearrange("h a v -> (h a) v"), W=[C])
                k.dma("sp", C[:, 64:65], self.st_in["n"][l, d, 2 * hp:2 * hp + 2].rearrange("h (a o) -> (h a) o", o=1),
                      W=[C])
            else:
                k.op("dve", "memset", W=[C], ap=C.ap, constant=0.0)
            A3 = v3(j.Atok[d].ap, NB, 8)
            E3 = v3(j.Etok[d].ap, NB, 8)
            for si_, (b0, nb) in enumerate(self.segs(j, d)):
                s0, n = b0 * 128, nb * 128
                va, cs = VA[si_ % 2], CS[si_ % 2]
                va4 = va[:, 0:nb * 130].rearrange("p (b h c) -> p b h c", b=nb, h=2, c=65)
                k.op("dve", "tensor_tensor", R=[VA0, j.Atok[d]], W=[va], out=va4, in0=VA04[:, b0:b0 + nb, :, :],
                     in1=bc_last(A3[:, b0:b0 + nb, 2 * hp:2 * hp + 2], 65), op=ALU.mult)
                bdc = k.bank()
                for b in range(nb):
                    for h in range(2):
                        k.op("pe", "matmul", R=[Ktok, va], W=[bdc], out=bdc[h * 64:(h + 1) * 64, b * 65:(b + 1) * 65],
                             lhsT=Ktok[:, (b0 + b) * 128 + h * 64:(b0 + b) * 128 + (h + 1) * 64], rhs=va4[:, b, h, :],
                             start=True, stop=True)
                order = range(nb) if d == 0 else range(nb - 1, -1, -1)
                for b in order:
                    c = b0 + b
                    k.op("dve", "tensor_scalar", R=[C, Fbc], W=[cs], out=cs[:, b * 65:(b + 1) * 65], in0=C.ap,
                         scalar1=Fbc[:, c:c + 1], scalar2=None, op0=ALU.mult)
                    k.op("dve", "scalar_tensor_tensor", R=[C, Fbc, bdc], W=[C], out=C.ap, in0=C.ap, scalar=Fbc[:, c:c + 1],
                         in1=bdc[:, b * 65:(b + 1) * 65], op0=ALU.mult, op1=ALU.add)
                for h in range(2):
                    bst = k.bank()
                    for b in range(nb):
                        tk = slice(s0 + b * 128, s0 + (b + 1) * 128)
                        k.op("pe", "matmul", R=[KT_, QT], W=[bst], out=bst[:, b * 128:(b + 1) * 128],
                             lhsT=KT_[h * 64:(h + 1) * 64, tk], rhs=QT[h * 64:(h + 1) * 64, tk], start=True, stop=True)
                    sm = Sm[si_ % 2][h]
                    k.op("dve", "tensor_tensor", R=[bst, k.masksb], W=[sm], out=v3(sm[:, 0:n], nb, 128),
                         in0=v3(bst[:, 0:n], nb, 128), in1=bc_mid(mask3[:, d, :], nb), op=ALU.mult)
                xx = X[si_ % 2]
                for p0 in range(0, nb, 2):
                    n2 = min(2, nb - p0)
                    bo = k.bank()
                    for b in range(p0, p0 + n2):
                        for h in range(2):
                            sm = Sm[si_ % 2][h]
                            off = (b - p0) * 130 + h * 65
                            oo = bo[:, off:off + 65]
                            k.op("pe", "matmul", R=[sm, va], W=[bo], out=oo, lhsT=sm[:, b * 128:(b + 1) * 128],
                                 rhs=va4[:, b, h, :], start=True, stop=False)
                            k.op("pe", "matmul", R=[QT, cs], W=[bo], out=oo,
                                 lhsT=QT[h * 64:(h + 1) * 64, s0 + b * 128:s0 + (b + 1) * 128],
                                 rhs=cs[h * 64:(h + 1) * 64, b * 65:(b + 1) * 65], start=False, stop=True)
                    bo4 = bo[:, 0:n2 * 130].rearrange("p (b h c) -> p b h c", b=n2, h=2, c=65)
                    dn = den[(p0 // 2) % 2]
                    dn3 = v3(dn[:, 0:n2 * 2], n2, 2)
                    k.op("act", "activation", R=[bo], W=[dn], out=dn3, in_=bo4[:, :, :, 64], func=AF.Abs)
                    k.op("dve", "tensor_tensor", R=[dn, j.Etok[d]], W=[dn], out=dn3, in0=dn3,
                         in1=E3[:, b0 + p0:b0 + p0 + n2, 2 * hp:2 * hp + 2], op=ALU.max)
                    k.op("dve", "reciprocal", R=[dn], W=[dn], out=dn[:, 0:n2 * 2], in_=dn[:, 0:n2 * 2])
                    t0 = s0 + p0 * 128
                    if d == 0:
                        k.op("dve", "tensor_tensor", R=[bo, dn], W=[HF],
                             out=HF[:, t0:t0 + n2 * 128].rearrange("p (b h c) -> p b h c", b=n2, h=2, c=64),
                             in0=bo4[:, :, :, 0:64], in1=bc_last(dn3, 64), op=ALU.mult)
                    else:
                        hh = hd[(p0 // 2) % 2]
                        k.op("dve", "tensor_tensor", R=[bo, dn], W=[hh],
                             out=hh[:, 0:n2 * 128].rearrange("p (b h c) -> p b h c", b=n2, h=2, c=64),
                             in0=bo4[:, :, :, 0:64], in1=bc_last(dn3, 64), op=ALU.mult)
                        k.op("pool", "tensor_tensor", R=[hh, HF], W=[hh], out=hh[:, 0:n2 * 128], in0=hh[:, 0:n2 * 128],
                             in1=HF[:, t0:t0 + n2 * 128], op=ALU.add)
                        k.op("pool", "tensor_tensor", R=[hh, SO], W=[xx], out=xx[:, p0 * 128:(p0 + n2) * 128],
                             in0=hh[:, 0:n2 * 128], in1=SO[:, t0:t0 + n2 * 128], op=ALU.mult)
                if d == 1:
                    self.tail(j, 0, hp, s0, nb, xx, SG[:, s0:s0 + n], SG, ttm[si_ % 2])
            if j.kind == "p":
                k.dma("sp", self.so["c"][j.idx, l, d, 2 * hp:2 * hp + 2].rearrange("h a v -> (h a) v"), C[:, 0:64], R=[C])
                k.dma("sp", self.so["n"][j.idx, l, d, 2 * hp:2 * hp + 2].rearrange("h (a o) -> (h a) o", o=1),
                      C[:, 64:65], R=[C])

    def shiftmix(self, j, S, c0_ap, cd_ap, co, s0, n, out):
        k = self
        N = j.N
        k.op("dve", "tensor_scalar", R=[S, co], W=[out], out=out[:, 0:n], in0=S[:, s0:s0 + n], scalar1=c0_ap,
             scalar2=None, op0=ALU.mult)

        def acc(o_ap, i_ap, dirn):
            k.op("dve", "scalar_tensor_tensor", R=[S, co, out], W=[out], out=o_ap, in0=i_ap,
                 scalar=cd_ap[:, dirn:dirn + 1], in1=o_ap, op0=ALU.mult, op1=ALU.add)
        if j.kind == "s":
            o3 = v3(out[:, 0:n], n // 64, 64)
            s3 = v3(S[:, s0:s0 + n], n // 64, 64)
            acc(o3[:, :, 1:64], s3[:, :, 0:63], 0)
            acc(o3[:, :, 0:63], s3[:, :, 1:64], 1)
            i0 = 0 if s0 >= 64 else 64
            if n > i0:
                acc(out[:, i0:n], S[:, s0 + i0 - 64:s0 + n - 64], 2)
            i1 = n if s0 + n + 64 <= N else n - 64
            if i1 > 0:
                acc(out[:, 0:i1], S[:, s0 + 64:s0 + i1 + 64], 3)
        else:
            i0 = 0 if s0 > 0 else 1
            acc(out[:, i0:n], S[:, s0 + i0 - 1:s0 + n - 1], 0)
            i1 = n if s0 + n < N else n - 1
            acc(out[:, 0:i1], S[:, s0 + 1:s0 + i1 + 1], 1)

    def rwkv_prepass(self, j, l):
        k = self
        self.new_phase()
        AFa, ABa = self.AF_, self.AB_
        N, NB = j.N, j.NB
        if not hasattr(j, "LW"):
            j.LW = self.scratch(f"LW_{j.name}", [128, N], BF16)
            j.bLW = TT(None, self.P.buf())
        W, W3 = self.load_w(self.wDl[l], 128)
        mu = k.tt(AFa, 2)
        shl = k.tt(AFa, 4)
        k.dma("sp", mu.ap, self.rwLP[l], W=[mu])
        k.dma("sp", shl.ap, self.c_shl[j.kind], W=[shl])
        co = k.tt(AFa, 8)
        k.op("dve", "tensor_scalar", R=[mu], W=[co], out=co[:, 0:1], in0=mu[:, 0:1], scalar1=-1.0, scalar2=1.0,
             op0=ALU.mult, op1=ALU.add)
        k.op("dve", "tensor_scalar", R=[shl, mu], W=[co], out=co[:, 1:5], in0=shl.ap, scalar1=mu[:, 0:1], scalar2=None,
             op0=ALU.mult)
        S = k.tt(ABa, N)

        def blk(b5, tok0, nt, h3, ht):
            bk = k.bank()
            k.mm_fm(bk, bk[:, 0:nt], W, W3, 0, 128, h3, ht, nt)
            k.op("act", "activation", R=[bk], W=[S], out=S[:, tok0:tok0 + nt], in_=bk[:, 0:nt], func=AF.Copy)
        self.stream_ht(j, blk)
        SEG = min(self.SEGB, NB) * 128
        xo = [k.tt(AFa, SEG) for _ in range(2)]
        lo = [k.tt(ABa, SEG) for _ in range(2)]
        for si_, (b0, nb) in enumerate(self.segs(j, 0)):
            s0, n = b0 * 128, nb * 128
            x, lw = xo[si_ % 2], lo[si_ % 2]
            self.shiftmix(j, S, co[:, 0:1], co[:, 1:5], co, s0, n, x)
            k.op("act", "activation", R=[x], W=[lw], out=lw[0:64, 0:n], in_=x[0:64, 0:n], func=AF.Tanh)
            k.op("act", "activation", R=[x], W=[lw], out=lw[64:128, 0:n], in_=x[64:128, 0:n], func=AF.Copy)
            k.dma("sp", j.LW[:, s0:s0 + n], lw[:, 0:n], R=[lw], W=[j.bLW])

    def rwkv_unit(self, j, l, hp):
        k = self
        self.new_phase()
        AFa, ABa = self.AF_, self.AB_
        N, NB = j.N, j.NB
        CE = 0.6065306597126334
        rw = k.tt(AFa, 12)
        k.dma("sp", rw.ap, self.rwP[l, hp], W=[rw])
        shm = k.tt(AFa, 12)
        k.dma("sp", v3(shm.ap, 3, 4), self.c_shm[j.kind][hp], W=[shm])
        co = k.tt(AFa, 16)
        k.op("dve", "tensor_scalar", R=[rw], W=[co], out=co[:, 0:3], in0=rw[:, 0:3], scalar1=-1.0, scalar2=1.0,
             op0=ALU.mult, op1=ALU.add)
        k.op("dve", "tensor_tensor", R=[shm, rw], W=[co], out=v3(co[:, 4:16], 3, 4), in0=v3(shm.ap, 3, 4),
             in1=bc_last(rw[:, 0:3], 4), op=ALU.mult)
        w2 = k.tt(ABa, 256)
        for d in range(2):
            k.dma("pool", w2[:, d * 128:(d + 1) * 128], self.rwW2[l, d][:, hp * 128:(hp + 1) * 128], W=[w2])
        Rr, Kk, KK, BON, SGT, Vtok = (k.tt(ABa, N) for _ in range(6))
        YF = k.tt(AFa, N)
        Sst = k.tt(AFa, 64)
        mark_f, mark_b = AFa.pos, ABa.pos
        Sx = [k.tt(ABa, N) for _ in range(3)]
        W, W3 = self.load_w(self.wD[l, hp], 512)

        def blk(b5, tok0, nt, h3, ht):
            for c in range(3):
                bk = k.bank()
                k.mm_fm(bk, bk[:, 0:nt], W, W3, c * 128, 128, h3, ht, nt)
                k.op("act", "activation", R=[bk], W=[Sx[c]], out=Sx[c][:, tok0:tok0 + nt], in_=bk[:, 0:nt], func=AF.Copy)
            bk = k.bank()
            k.mm_fm(bk, bk[:, 0:nt], W, W3, 384, 128, h3, ht, nt)
            k.op("act", "activation", R=[bk], W=[SGT], out=SGT[:, tok0:tok0 + nt], in_=bk[:, 0:nt], func=AF.Silu)
        self.stream_ht(j, blk)
        SB = min(self.SEGB, NB)
        SEG = SB * 128
        X = [[k.tt(AFa, SEG) for _ in range(3)] for _ in range(2)]
        t1 = [k.tt(AFa, SEG) for _ in range(2)]
        t2 = [k.tt(AFa, SEG) for _ in range(2)]
        vb = [k.tt(ABa, SEG)] * 2
        for si_, (b0, nb) in enumerate(self.segs(j, 0)):
            s0, n = b0 * 128, nb * 128
            Xr, Xk, Xv = X[si_ % 2]
            a1, a2 = t1[si_ % 2], t2[si_ % 2]
            for c, xx in enumerate((Xr, Xk, Xv)):
                self.shiftmix(j, Sx[c], co[:, c:c + 1], v3(co[:, 4:16], 3, 4)[:, c, :], co, s0, n, xx)
            k.op("pool", "tensor_copy", R=[Xr], W=[Rr], out=Rr[:, s0:s0 + n], in_=Xr[:, 0:n])
            k.op("pool", "tensor_copy", R=[Xk], W=[Kk], out=Kk[:, s0:s0 + n], in_=Xk[:, 0:n])
            v_ = vb[si_ % 2]
            k.op("act", "activation", R=[Xk, rw], W=[v_], out=v_[:, 0:n], in_=Xk[:, 0:n], func=AF.Square, scale=rw[:, 3:4])
            bk = k.bank()
            k.op("pe", "matmul", R=[k.bonesb, v_], W=[bk], out=bk[:, 0:n], lhsT=k.bonesb.ap, rhs=v_[:, 0:n], start=True, stop=True)
            k.op("dve", "tensor_scalar", R=[bk], W=[a1], out=a1[:, 0:n], in0=bk[:, 0:n], scalar1=1e-24, scalar2=None, op0=ALU.max)
            k.op("act", "activation", R=[a1], W=[a1], out=a1[:, 0:n], in_=a1[:, 0:n], func=AF.Ln)
            k.op("act", "activation", R=[a1], W=[a1], out=a1[:, 0:n], in_=a1[:, 0:n], func=AF.Exp, scale=-0.5)
            k.op("dve", "scalar_tensor_tensor", R=[Xk, rw, a1], W=[KK], out=KK[:, s0:s0 + n], in0=Xk[:, 0:n],
                 scalar=rw[:, 3:4], in1=a1[:, 0:n], op0=ALU.mult, op1=ALU.mult)
            k.op("dve", "scalar_tensor_tensor", R=[Xr, rw, Xk], W=[v_], out=v_[:, 0:n], in0=Xr[:, 0:n],
                 scalar=rw[:, 5:6], in1=Xk[:, 0:n], op0=ALU.mult, op1=ALU.mult)
            bk2 = k.bank()
            k.op("pe", "matmul", R=[k.bonesb, v_], W=[bk2], out=bk2[:, 0:n], lhsT=k.bonesb.ap, rhs=v_[:, 0:n], start=True, stop=True)
            k.op("dve", "tensor_tensor", R=[bk2, Xv], W=[BON], out=BON[:, s0:s0 + n], in0=bk2[:, 0:n], in1=Xv[:, 0:n], op=ALU.mult)
            k.op("pool", "tensor_copy", R=[Xv], W=[v_], out=v_[:, 0:n], in_=Xv[:, 0:n])
            bb = k.bbank()
            for b in range(nb):
                k.op("pe", "transpose", R=[v_, k.identb], W=[bb], out=bb[:, b * 128:(b + 1) * 128],
                     in_=v_[:, b * 128:(b + 1) * 128], identity=k.identb.ap)
            k.op("act", "activation", R=[bb], W=[Vtok], out=Vtok[:, s0:s0 + n], in_=bb[:, 0:n], func=AF.Copy)
        RS_ = self.cfg.get("rw_stop", 9)
        if RS_ <= 1:
            return
        self.P.barrier()
        AFa.pos, ABa.pos = mark_f, mark_b
        F_ = {nm: k.tt(AFa, SEG) for nm in ("sgw", "a", "G", "Gm", "EG", "EnG", "EGm", "EGL", "kt", "bb")}
        Bt = {nm: k.tt(ABa, SEG) for nm in ("lw", "rT", "aT", "bT", "kT", "BpT", "Atok", "Bptok", "MT",
                                            "AhT", "Ubf")}
        BrbS, BrkS = k.tt(ABa, SB * 256), k.tt(ABa, SB * 256)
        Kp32 = k.tt(AFa, SEG)
        SS = k.tt(ABa, SB * 64)
        GSZ = self.cfg.get("rw_gsz", 1)
        NG = (SB + GSZ - 1) // GSZ
        G_ = [{nm: [k.tt(ABa, GSZ * 256) for _ in range(1 if nm in ("T", "TT") else 2)] for nm in ("X", "XT", "T", "TT", "X0", "I")}
              for _ in range(NG)]
        AakS = [k.tt(ABa, GSZ * 256) for _ in range(NG)]
        Zs = [k.tt(ABa, GSZ * 256) for _ in range(NG)]
        N32 = [k.tt(AFa, GSZ * 256) for _ in range(NG)]
        ysum = k.tt(AFa, SEG)
        ttm = self.tail_tmps(SEG, 1) * 2
        ptmp = k.tt(ABa, SEG)
        rm3 = v3(k.rmask.ap, 2, 512)
        mask3 = v3(k.masksb.ap, 4, 128)
        sti = k.tt(AFa, 128)
        for d in range(2):
            fl = (lambda a: a) if d == 0 else flip
            eidx = 127 if d == 0 else 0
            mSU, mSL, mU = (2, 3, 0) if d == 0 else (3, 2, 1)
            if j.kind == "s":
                k.dma("sp", v3(sti[0:64, :], 2, 64), self.st_in["s"][l, d, 2 * hp:2 * hp + 2].rearrange("h i j -> i h j"), W=[sti])
                bk = k.bank()
                k.op("pe", "transpose", R=[sti, k.ident], W=[bk], out=bk[:, 0:64], in_=sti[0:64, :], identity=k.ident[0:64, 0:64])
                k.op("act", "activation", R=[bk], W=[Sst], out=Sst.ap, in_=bk[:, 0:64], func=AF.Copy)
            else:
                k.op("dve", "memset", W=[Sst], ap=Sst.ap, constant=0.0)
            for si_, (b0, nb) in enumerate(self.segs(j, d)):
                s0, n = b0 * 128, nb * 128
                lw = Bt["lw"]
                k.dma("sp", lw[:, 0:n], j.LW[:, s0:s0 + n], R=[j.bLW], W=[lw])
                bzw, bza = k.bank(), k.bank()
                k.op("pe", "matmul", R=[w2, lw], W=[bzw], out=bzw[:, 0:n], lhsT=w2[0:64, d * 128:(d + 1) * 128],
                     rhs=lw[0:64, 0:n], start=True, stop=True)
                k.op("pe", "matmul", R=[w2, lw], W=[bza], out=bza[:, 0:n], lhsT=w2[64:128, d * 128:(d + 1) * 128],
                     rhs=lw[64:128, 0:n], start=True, stop=True)
                f = F_
                k.op("act", "activation", R=[bzw, rw], W=[f["sgw"]], out=f["sgw"][:, 0:n], in_=bzw[:, 0:n], func=AF.Sigmoid,
                     bias=rw[:, 6 + d:7 + d])
                k.op("act", "activation", R=[bza, rw], W=[f["a"]], out=f["a"][:, 0:n], in_=bza[:, 0:n], func=AF.Sigmoid,
                     bias=rw[:, 8 + d:9 + d])
                k.op("dve", "tensor_tensor_scan", R=[f["sgw"], k.rmask], W=[f["G"]], out=fl(f["G"][:, 0:n]),
                     data0=rm3[:, 0, 0:n], data1=fl(f["sgw"][:, 0:n]), initial=0.0, op0=ALU.mult, op1=ALU.add)
                k.op("act", "activation", R=[f["G"]], W=[f["EG"]], out=f["EG"][:, 0:n], in_=f["G"][:, 0:n], func=AF.Exp, scale=-CE)
                k.op("act", "activation", R=[f["G"]], W=[f["EnG"]], out=f["EnG"][:, 0:n], in_=f["G"][:, 0:n], func=AF.Exp, scale=CE)
                k.op("dve", "tensor_tensor", R=[f["G"], f["sgw"]], W=[f["Gm"]], out=f["Gm"][:, 0:n], in0=f["G"][:, 0:n],
                     in1=f["sgw"][:, 0:n], op=ALU.subtract)
                k.op("act", "activation", R=[f["Gm"]], W=[f["EGm"]], out=f["EGm"][:, 0:n], in_=f["Gm"][:, 0:n], func=AF.Exp, scale=-CE)
                G3 = v3(f["G"][:, 0:n], nb, 128)
                k.op("dve", "tensor_tensor", R=[f["G"]], W=[f["Gm"]], out=v3(f["Gm"][:, 0:n], nb, 128),
                     in0=bc_last(G3[:, :, eidx], 128), in1=G3, op=ALU.subtract)
                k.op("act", "activation", R=[f["Gm"]], W=[f["EGL"]], out=f["EGL"][:, 0:n], in_=f["Gm"][:, 0:n], func=AF.Exp, scale=-CE)
                k.op("dve", "tensor_scalar", R=[f["a"], rw], W=[f["kt"]], out=f["kt"][:, 0:n], in0=f["a"][:, 0:n], scalar1=-1.0,
                     scalar2=rw[:, 4:5], op0=ALU.add, op1=ALU.mult)
                k.op("dve", "scalar_tensor_tensor", R=[f["kt"], Kk], W=[f["kt"]], out=f["kt"][:, 0:n], in0=f["kt"][:, 0:n],
                     scalar=1.0, in1=Kk[:, s0:s0 + n], op0=ALU.add, op1=ALU.mult)
                k.op("pool", "tensor_tensor", R=[KK, f["a"]], W=[f["bb"]], out=f["bb"][:, 0:n], in0=KK[:, s0:s0 + n],
                     in1=f["a"][:, 0:n], op=ALU.mult)
                k.op("pool", "tensor_tensor", R=[Rr, f["EG"]], W=[Bt["rT"]], out=Bt["rT"][:, 0:n], in0=Rr[:, s0:s0 + n],
                     in1=f["EG"][:, 0:n], op=ALU.mult)
                k.op("dve", "scalar_tensor_tensor", R=[KK, f["EGm"]], W=[Bt["aT"]], out=Bt["aT"][:, 0:n], in0=KK[:, s0:s0 + n],
                     scalar=-1.0, in1=f["EGm"][:, 0:n], op0=ALU.mult, op1=ALU.mult)
                k.op("pool", "tensor_tensor", R=[f["bb"], f["EnG"]], W=[Bt["bT"]], out=Bt["bT"][:, 0:n], in0=f["bb"][:, 0:n],
                     in1=f["EnG"][:, 0:n], op=ALU.mult)
                k.op("dve", "tensor_tensor", R=[f["kt"], f["EnG"]], W=[Bt["kT"]], out=Bt["kT"][:, 0:n], in0=f["kt"][:, 0:n],
                     in1=f["EnG"][:, 0:n], op=ALU.mult)
                k.op("pool", "tensor_tensor", R=[f["bb"], f["EGL"]], W=[Bt["BpT"]], out=Bt["BpT"][:, 0:n], in0=f["bb"][:, 0:n],
                     in1=f["EGL"][:, 0:n], op=ALU.mult)
                k.op("dve", "tensor_tensor", R=[f["kt"], f["EGL"]], W=[f["Gm"]], out=f["Gm"][:, 0:n], in0=f["kt"][:, 0:n],
                     in1=f["EGL"][:, 0:n], op=ALU.mult)
                bkp = k.bank()
                for b in range(nb):
                    k.op("pe", "transpose", R=[f["Gm"], k.ident], W=[bkp], out=bkp[:, b * 128:(b + 1) * 128],
                         in_=f["Gm"][:, b * 128:(b + 1) * 128], identity=k.ident.ap)
                k.op("act", "activation", R=[bkp], W=[Kp32], out=Kp32[:, 0:n], in_=bkp[:, 0:n], func=AF.Copy)
                for src, dst in (("aT", "Atok"), ("BpT", "Bptok")):
                    bb = k.bbank()
                    for b in range(nb):
                        k.op("pe", "transpose", R=[Bt[src], k.identb], W=[bb], out=bb[:, b * 128:(b + 1) * 128],
                             in_=Bt[src][:, b * 128:(b + 1) * 128], identity=k.identb.ap)
                    k.op("act", "activation", R=[bb], W=[Bt[dst]], out=Bt[dst][:, 0:n], in_=bb[:, 0:n], func=AF.Copy)
                if RS_ <= 2:
                    continue
                groups = [(g0, min(GSZ, nb - g0)) for g0 in range(0, nb, GSZ)]

                refine = (j.kind == "p")

                def gen_group(gi, g0, ng):
                    T = G_[gi]
                    ni = ng * 2
                    w_ = ni * 128

                    def prod(lname, rname, dstTT, dst_ap_fn, midx):
                        for h in range(2):
                            bk = k.bank()
                            for bl in range(ng):
                                tk = slice((g0 + bl) * 128, (g0 + bl + 1) * 128)
                                k.op("pe", "matmul", R=[Bt[lname], Bt[rname]], W=[bk], out=bk[:, bl * 128:(bl + 1) * 128],
                                     lhsT=Bt[lname][h * 64:(h + 1) * 64, tk], rhs=Bt[rname][h * 64:(h + 1) * 64, tk],
                                     start=True, stop=True)
                            k.op("dve", "tensor_tensor", R=[bk, k.masksb], W=[dstTT], out=dst_ap_fn(h),
                                 in0=v3(bk[:, 0:ng * 128], ng, 128), in1=bc_mid(mask3[:, midx, :], ng), op=ALU.mult)
                    slot3 = lambda t: (lambda h: v3(t[:, h * ng * 128:(h + 1) * ng * 128], ng, 128))
                    X0, X0T = T["X0"][0], T["X0"][1]
                    prod("bT", "aT", X0T, slot3(X0T), mSU)
                    prod("aT", "bT", X0, slot3(X0), mSL)
                    yield
                    prod("aT", "kT", AakS[gi], slot3(AakS[gi]), mSL)
                    brb4 = v3(BrbS[:, 0:nb * 256], nb * 2, 128)
                    brk4 = v3(BrkS[:, 0:nb * 256], nb * 2, 128)
                    prod("bT", "rT", BrbS, lambda h: v3(BrbS[:, 0:nb * 256], nb, 256)[:, g0:g0 + ng, h * 128:(h + 1) * 128], mU)
                    prod("kT", "rT", BrkS, lambda h: v3(BrkS[:, 0:nb * 256], nb, 256)[:, g0:g0 + ng, h * 128:(h + 1) * 128], mU)
                    yield
                    if RS_ <= 3:
                        return
                    idb = bc_mid(k.identb.ap, ni)
                    hm = v3(k.hmaskb.ap, 4, 128)
                    IX, IXT = T["I"]

                    def msk(src, dst, mi):
                        k.op("pool", "tensor_tensor", R=[src, k.hmaskb], W=[dst], out=v3(dst[:, 0:w_], ni, 128),
                             in0=v3(src[:, 0:w_], ni, 128), in1=bc_mid(hm[:, mi, :], ni), op=ALU.mult)

                    def addid(eng, src, dst, srcTT=None):
                        k.op(eng, "tensor_tensor", R=[src, k.identb], W=[dst], out=v3(dst[:, 0:w_], ni, 128),
                             in0=v3(src[:, 0:w_], ni, 128), in1=idb, op=ALU.add)

                    def mm4(lt, rt):
                        bk = k.bank()
                        for it in range(ni):
                            sl = slice(it * 128, (it + 1) * 128)
                            k.op("pe", "matmul", R=[lt, rt], W=[bk], out=bk[:, sl], lhsT=lt[:, sl], rhs=rt[:, sl], start=True, stop=True)
                        return bk
                    msk(X0, T["X"][0], 0)
                    msk(X0T, T["XT"][0], 0)
                    addid("pool", T["X"][0], T["T"][0])
                    addid("pool", T["XT"][0], T["TT"][0])
                    cur, tc = 0, 0
                    if RS_ <= 3.2:
                        return
                    for lvl in range(1, 4):
                        nxt = 1 - cur
                        bx = mm4(T["XT"][cur], T["X"][cur])
                        bxt = mm4(T["X"][cur], T["XT"][cur])
                        addid("dve", bx, IX)
                        addid("dve", bxt, IXT)
                        if lvl < 3:
                            k.op("act", "activation", R=[bx], W=[T["X"][nxt]], out=T["X"][nxt][:, 0:w_], in_=bx[:, 0:w_], func=AF.Copy)
                            k.op("act", "activation", R=[bxt], W=[T["XT"][nxt]], out=T["XT"][nxt][:, 0:w_], in_=bxt[:, 0:w_], func=AF.Copy)
                        if RS_ <= 3.3:
                            return
                        yield
                        btt = mm4(IX, T["TT"][tc])
                        bt = mm4(IXT, T["T"][tc])
                        k.op("act", "activation", R=[btt], W=[T["TT"][tc]], out=T["TT"][tc][:, 0:w_], in_=btt[:, 0:w_], func=AF.Copy)
                        k.op("dve", "tensor_copy", R=[bt], W=[T["T"][tc]], out=T["T"][tc][:, 0:w_], in_=bt[:, 0:w_])
                        cur = nxt
                        if RS_ <= 3.4:
                            return
                        yield
                    if RS_ <= 3.5:
                        return
                    for li, mi in enumerate((1, 2, 3)):
                        lastl = li == 2
                        Ao, AoT, Q1, Q2 = T["X"][0], T["XT"][0], T["X"][1], T["XT"][1]
                        msk(X0, Ao, mi)
                        bq1 = mm4(Ao, T["TT"][tc])
                        k.op("act", "activation", R=[bq1], W=[Q1], out=Q1[:, 0:w_], in_=bq1[:, 0:w_], func=AF.Copy)
                        needT = refine or not lastl
                        if needT:
                            msk(X0T, AoT, mi)
                            bq2 = mm4(AoT, T["T"][tc])
                            k.op("dve", "tensor_copy", R=[bq2], W=[Q2], out=Q2[:, 0:w_], in_=bq2[:, 0:w_])
                        yield
                        btt = mm4(T["T"][tc], Q1)
                        if needT:
                            bt = mm4(T["TT"][tc], Q2)
                        k.op("dve", "tensor_tensor", R=[btt, T["TT"][tc]], W=[T["TT"][tc]], out=T["TT"][tc][:, 0:w_],
                             in0=btt[:, 0:w_], in1=T["TT"][tc][:, 0:w_], op=ALU.add)
                        if needT:
                            k.op("dve", "tensor_tensor", R=[bt, T["T"][tc]], W=[T["T"][tc]], out=T["T"][tc][:, 0:w_],
                                 in0=bt[:, 0:w_], in1=T["T"][tc][:, 0:w_], op=ALU.add)
                        yield
                    T0, TT0 = T["T"][tc], T["TT"][tc]
                    if not refine:
                        tparts = [TT0]
                    else:
                        tparts = None
                    Rm, TTh, TTl, s32 = T["I"][0], T["I"][1], T["X"][0], N32[gi]
                    self.dbgaps = dict(Rm=Rm.ap, TTh=TTh.ap, TTl=TTl.ap, s32=s32.ap, T0=T0.ap, TT0=TT0.ap, X0T=X0T.ap, X0=X0.ap)
                    if refine:
                        bA = mm4(X0, TT0)
                        k.op("dve", "tensor_tensor", R=[bA, TT0], W=[s32], out=s32[:, 0:w_], in0=bA[:, 0:w_], in1=TT0[:, 0:w_], op=ALU.subtract)
                        addid("dve", s32, Rm)
                        yield
                        bD = mm4(T0, Rm)
                        k.op("dve", "scalar_tensor_tensor", R=[bD, TT0], W=[s32], out=s32[:, 0:w_], in0=bD[:, 0:w_], scalar=float(self.cfg.get("nwt", 1.0)), in1=TT0[:, 0:w_], op0=ALU.mult, op1=ALU.add)
                        k.op("act", "activation", R=[s32], W=[TTh], out=TTh[:, 0:w_], in_=s32[:, 0:w_], func=AF.Copy)
                        k.op("dve", "tensor_tensor", R=[s32, TTh], W=[TTl], out=TTl[:, 0:w_], in0=s32[:, 0:w_], in1=TTh[:, 0:w_], op=ALU.subtract)
                        yield
                        tparts = [TTh, TTl]
                    cur = tc
                    TT_ = None
                    ba = k.bank()
                    for bl in range(ng):
                        for h in range(2):
                            it = h * ng + bl
                            for pi_, tpart in enumerate(tparts):
                                k.op("pe", "matmul", R=[Bt["Atok"], tpart], W=[ba], out=ba[h * 64:(h + 1) * 64, bl * 128:(bl + 1) * 128],
                                     lhsT=Bt["Atok"][:, (g0 + bl) * 128 + h * 64:(g0 + bl) * 128 + (h + 1) * 64],
                                     rhs=tpart[:, it * 128:(it + 1) * 128], start=(pi_ == 0), stop=(pi_ == len(tparts) - 1))
                    k.op("act", "activation", R=[ba], W=[Bt["AhT"]], out=Bt["AhT"][:, g0 * 128:(g0 + ng) * 128], in_=ba[:, 0:ng * 128],
                         func=AF.Copy)
                    bz = k.bank()
                    for it in range(ni):
                        sl = slice(it * 128, (it + 1) * 128)
                        for pi_, tpart in enumerate(tparts):
                            k.op("pe", "matmul", R=[tpart, AakS[gi]], W=[bz], out=bz[:, sl], lhsT=tpart[:, sl], rhs=AakS[gi][:, sl],
                                 start=(pi_ == 0), stop=(pi_ == len(tparts) - 1))
                    k.op("act", "activation", R=[bz], W=[Zs[gi]], out=Zs[gi][:, 0:w_], in_=bz[:, 0:w_], func=AF.Copy)
                    yield
                    bm = k.bank()
                    for bl in range(ng):
                        for h in range(2):
                            it = h * ng + bl
                            k.op("pe", "matmul", R=[Zs[gi], Bt["Bptok"]], W=[bm], out=bm[:, (bl * 2 + h) * 64:(bl * 2 + h + 1) * 64],
                                 lhsT=Zs[gi][:, it * 128:(it + 1) * 128],
                                 rhs=Bt["Bptok"][:, (g0 + bl) * 128 + h * 64:(g0 + bl) * 128 + (h + 1) * 64], start=True, stop=True)
                    k.op("dve", "tensor_tensor", R=[bm, Kp32], W=[Bt["MT"]], out=Bt["MT"][:, g0 * 128:(g0 + ng) * 128],
                         in0=bm[:, 0:ng * 128], in1=Kp32[:, g0 * 128:(g0 + ng) * 128], op=ALU.add)
                    bmy = k.bank()
                    for bl in range(ng):
                        for h in range(2):
                            it = h * ng + bl
                            sl = slice(((g0 + bl) * 2 + h) * 128, ((g0 + bl) * 2 + h + 1) * 128)
                            k.op("pe", "matmul", R=[Zs[gi], BrbS], W=[bmy], out=bmy[:, (bl * 2 + h) * 128:(bl * 2 + h + 1) * 128],
                                 lhsT=Zs[gi][:, it * 128:(it + 1) * 128], rhs=BrbS[:, sl], start=True, stop=True)
                    k.op("dve", "tensor_tensor", R=[bmy, BrkS], W=[BrkS], out=BrkS[:, g0 * 256:(g0 + ng) * 256],
                         in0=bmy[:, 0:ng * 256], in1=BrkS[:, g0 * 256:(g0 + ng) * 256], op=ALU.add)
                    yield
                gens = [gen_group(gi, g0, ng) for gi, (g0, ng) in enumerate(groups)]
                while gens:
                    for g in list(gens):
                        try:
                            next(g)
                        except StopIteration:
                            gens.remove(g)
                if RS_ <= 4:
                    continue
                order = range(nb) if d == 0 else range(nb - 1, -1, -1)
                Ub4 = Bt["Ubf"][:, 0:n].rearrange("p (b h c) -> p b h c", b=nb, h=2, c=64)
                for b in order:
                    k.op("pool", "tensor_copy", R=[Sst], W=[SS], out=SS[:, b * 64:(b + 1) * 64], in_=Sst.ap)
                    for h in range(2):
                        bu = k.bank()
                        k.op("pe", "matmul", R=[Bt["AhT"], SS], W=[bu], out=bu[:, 0:64], lhsT=Bt["AhT"][h * 64:(h + 1) * 64, b * 128:(b + 1) * 128],
                             rhs=SS[h * 64:(h + 1) * 64, b * 64:(b + 1) * 64], start=True, stop=True)
                        k.op("act", "activation", R=[bu], W=[Bt["Ubf"]], out=Ub4[:, b, h, :], in_=bu[:, 0:64], func=AF.Copy)
                    bd = k.bank()
                    for h in range(2):
                        oo = bd[h * 64:(h + 1) * 64, 0:64]
                        vs = Vtok[:, (b0 + b) * 128 + h * 64:(b0 + b) * 128 + (h + 1) * 64]
                        k.op("pe", "matmul", R=[Bt["Bptok"], Bt["Ubf"]], W=[bd], out=oo, lhsT=Bt["Bptok"][:, b * 128 + h * 64:b * 128 + (h + 1) * 64],
                             rhs=Ub4[:, b, h, :], start=True, stop=False)
                        k.op("pe", "matmul", R=[Bt["MT"], Vtok], W=[bd], out=oo, lhsT=Bt["MT"][:, b * 128 + h * 64:b * 128 + (h + 1) * 64],
                             rhs=vs, start=False, stop=True)
                    gcol = b * 128 + eidx
                    k.op("dve", "scalar_tensor_tensor", R=[Sst, f["EG"], bd], W=[Sst], out=Sst.ap, in0=Sst.ap, scalar=f["EG"][:, gcol:gcol + 1],
                         in1=bd[:, 0:64], op0=ALU.mult, op1=ALU.add)
                if RS_ <= 5:
                    continue
                by = k.bank()
                for b in range(nb):
                    for h in range(2):
                        oo = by[:, b * 128 + h * 64:b * 128 + (h + 1) * 64]
                        vs = Vtok[:, (b0 + b) * 128 + h * 64:(b0 + b) * 128 + (h + 1) * 64]
                        sl = slice((b * 2 + h) * 128, (b * 2 + h + 1) * 128)
                        k.op("pe", "matmul", R=[Bt["rT"], SS], W=[by], out=oo, lhsT=Bt["rT"][h * 64:(h + 1) * 64, b * 128:(b + 1) * 128],
                             rhs=SS[h * 64:(h + 1) * 64, b * 64:(b + 1) * 64], start=True, stop=False)
                        k.op("pe", "matmul", R=[BrbS, Bt["Ubf"]], W=[by], out=oo, lhsT=BrbS[:, sl], rhs=Ub4[:, b, h, :], start=False, stop=False)
                        k.op("pe", "matmul", R=[BrkS, Vtok], W=[by], out=oo, lhsT=BrkS[:, sl], rhs=vs, start=False, stop=True)
                if d == 0:
                    k.op("act", "activation", R=[by], W=[YF], out=YF[:, s0:s0 + n], in_=by[:, 0:n], func=AF.Copy)
                else:
                    k.op("dve", "tensor_tensor", R=[by, YF], W=[ysum], out=ysum[:, 0:n], in0=by[:, 0:n], in1=YF[:, s0:s0 + n], op=ALU.add)

                    def post(bb, yt, n_, s0=s0):
                        k.op("dve", "tensor_tensor", R=[bb, BON], W=[ptmp], out=ptmp[:, 0:n_], in0=bb[:, 0:n_], in1=BON[:, s0:s0 + n_], op=ALU.add)
                        k.op("pool", "tensor_tensor", R=[ptmp, SGT], W=[yt], out=yt[:, 0:n_], in0=ptmp[:, 0:n_], in1=SGT[:, s0:s0 + n_], op=ALU.mult)
                    self.tail(j, 3, hp, s0, nb, ysum, None, None, ttm[si_ % 2], post=post)
            if j.kind == "p" and RS_ > 6:
                bk = k.bank()
                k.op("pe", "transpose", R=[Sst, k.ident], W=[bk], out=bk[0:64, 0:128], in_=Sst.ap, identity=k.ident.ap)
                k.op("act", "activation", R=[bk], W=[sti], out=sti[0:64, :], in_=bk[0:64, 0:128], func=AF.Copy)
                k.dma("sp", self.so["s"][j.idx, l, d, 2 * hp:2 * hp + 2].rearrange("h i j -> i h j"), v3(sti[0:64, :], 2, 64), R=[sti])


def _kt(w):
    C = w.shape[1]
    return np.ascontiguousarray(w.reshape(KT, 128, C).transpose(1, 0, 2))


def prep_shared(inp, cfg):
    L = cfg["DEPTH"]
    f = np.float32
    w_in = np.asarray(inp["w_in"], f)
    out = {}
    wA = np.zeros((L, 4, 128, KT, 640), f)
    wAg = np.zeros((L, 128, KT, 32), f)
    wB = np.zeros((L, 4, 128, KT, 256), f)
    wC = np.zeros((L, 4, 128, KT, 768), f)
    wD = np.zeros((L, 4, 128, KT, 512), f)
    wDl = np.zeros((L, 128, KT, 128), f)
    sw = np.concatenate([np.arange(32, 64), np.arange(0, 32), np.arange(96, 128), np.arange(64, 96)])
    for l in range(L):
        w = w_in[l]
        wAg[l] = _kt(w[:, OFF_A + 2560:OFF_A + 2592])
        wDl[l] = _kt(w[:, OFF_D + 1536:OFF_D + 1664])
        for hp in range(4):
            sl = lambda base, comp: w[:, base + comp * 512 + hp * 128: base + comp * 512 + (hp + 1) * 128]
            wA[l, hp] = _kt(np.concatenate([sl(OFF_A, c) for c in range(5)], 1))
            wB[l, hp] = _kt(np.concatenate([sl(OFF_B, 0), sl(OFF_B, 1)], 1))
            q, k_, v, g = (sl(OFF_C, c) for c in range(4))
            wC[l, hp] = _kt(np.concatenate([q, q[:, sw], k_, k_[:, sw], v, g], 1))
            gD = w[:, OFF_D + 1664 + hp * 128: OFF_D + 1664 + (hp + 1) * 128]
            wD[l, hp] = _kt(np.concatenate([sl(OFF_D, 0), sl(OFF_D, 1), sl(OFF_D, 2), gD], 1))
    out.update(wA=wA, wAg=wAg, wB=wB, wC=wC, wD=wD, wDl=wDl)
    w_out = np.asarray(inp["w_out"], f)
    out["wout"] = np.ascontiguousarray(w_out.reshape(L, 16, 128, D).transpose(0, 2, 1, 3))
    w_mod = np.asarray(inp["w_mod"], f)
    out["wmod"] = np.ascontiguousarray(w_mod.reshape(L, KT, 128, 3 * D).transpose(0, 2, 1, 3))
    out["ng2"] = np.ascontiguousarray(np.repeat(np.asarray(inp["norm_g"], f)[:, None, :], 2, 1))
    out["bmod2"] = np.ascontiguousarray(np.repeat(np.asarray(inp["b_mod"], f)[:, None, :], 2, 1))
    out["fgb"] = np.ascontiguousarray(np.repeat(np.asarray(inp["final_g"], f)[None, :], 128, 0))
    out["gbA"] = np.ascontiguousarray(np.asarray(inp["mlstm_gate_b"], f).reshape(L, 4, 8).transpose(0, 2, 1))
    lruP = np.zeros((L, 4, 128, 12), f)
    lruG = np.zeros((L, 4, 128, 4, 128), f)
    retP = np.zeros((L, 4, 128, 6), f)
    rwP = np.zeros((L, 4, 128, 12), f)
    cw, cb = np.asarray(inp["lru_conv_w"], f), np.asarray(inp["lru_conv_b"], f)
    gw, gb = np.asarray(inp["lru_gate_w"], f), np.asarray(inp["lru_gate_b"], f)
    lam, th = np.asarray(inp["lru_lambda"], f), np.asarray(inp["ret_theta"], f)
    mu = np.asarray(inp["rwkv_mu"], f)
    kk_, ka_, rk_ = (np.asarray(inp[n], f) for n in ("rwkv_kk", "rwkv_ka", "rwkv_rk"))
    w0, a0 = np.asarray(inp["rwkv_w0"], f), np.asarray(inp["rwkv_a0"], f)
    for l in range(L):
        for hp in range(4):
            ch = slice(hp * 128, (hp + 1) * 128)
            for t in range(4):
                lruP[l, hp, :, t] = cw[l, t, ch]
            lruP[l, hp, :, 4] = cb[l, ch]
            for d in range(2):
                for g_ in range(2):
                    lruP[l, hp, :, 5 + d * 2 + g_] = gb[l, d, g_, ch]
                    for hl in range(2):
                        lruG[l, hp, hl * 64:(hl + 1) * 64, d * 2 + g_, hl * 64:(hl + 1) * 64] = gw[l, d, g_, 2 * hp + hl]
                lruP[l, hp, :, 9 + d] = lam[l, d, ch]
                retP[l, hp, 0:64, d] = th[l, d, 2 * hp]
                retP[l, hp, 64:128, d] = th[l, d, 2 * hp + 1]
                for hl in range(2):
                    retP[l, hp, :, 2 + d * 2 + hl] = th[l, d, 2 * hp + hl]
                rwP[l, hp, :, 6 + d] = w0[l, d, ch]
                rwP[l, hp, :, 8 + d] = a0[l, d, ch]
            for c in range(3):
                rwP[l, hp, :, c] = mu[l, c * 512 + hp * 128: c * 512 + (hp + 1) * 128]
            rwP[l, hp, :, 3] = kk_[l, ch]
            rwP[l, hp, :, 4] = ka_[l, ch]
            rwP[l, hp, :, 5] = rk_[l, ch]
    out.update(lruP=lruP, lruG=lruG, retP=retP, rwP=rwP)
    rwLP = np.zeros((L, 128, 2), f)
    rwLP[:, :, 0] = mu[:, 1536:1664]
    out["rwLP"] = rwLP
    out["rwW2"] = np.ascontiguousarray(np.concatenate([np.asarray(inp["rwkv_w2"], f), np.asarray(inp["rwkv_a2"], f)], 2))
    p = np.arange(128)[:, None]
    fr = np.arange(128)[None, :]
    out["c_ident"] = np.eye(128, dtype=f)
    out["c_masks"] = np.stack([(fr >= p), (fr <= p), (fr > p), (fr < p)], 1).astype(f)
    out["c_diff"] = np.stack([np.maximum(fr - p, 0), np.maximum(p - fr, 0)], 1).astype(f)
    pos = np.broadcast_to(fr, (128, 128))
    out["c_pos"] = np.stack([pos + 1, 128 - pos, 127 - pos, pos], 1).astype(f)
    sel = np.zeros((8, 4, 128), f)
    for hp in range(4):
        sel[2 * hp, hp, 0:64] = 1
        sel[2 * hp + 1, hp, 64:128] = 1
    out["c_sel"] = sel
    sel2 = np.zeros((2, 2, 128), f)
    sel2[0, 0] = 1
    sel2[1, 1] = 1
    out["c_sel2"] = sel2
    bo = np.zeros((128, 128), f)
    bo[0:64, 0:64] = 1
    bo[64:, 64:] = 1
    out["c_bones"] = bo
    t5 = np.arange(512)
    rm = np.zeros((128, 2, 512), f)
    rm[:, 0, :] = (t5 % 128 != 0)
    rm[:, 1, :] = np.where(t5 % 128 == 0, -1e30, 0.0)
    out["c_rmask"] = rm
    bdm = lambda sz: ((fr // sz) == (p // sz))
    out["c_hmask"] = np.stack([bdm(16), bdm(32) & ~bdm(16), bdm(64) & ~bdm(32), bdm(128) & ~bdm(64)], 1).astype(f)
    for kind in ("p", "s"):
        m = np.zeros((4, 128, 3, 4), f)
        ml = np.zeros((128, 4), f)

        def dirof(c):
            return (c // 416) if kind == "s" else (0 if c < 832 else 1)
        for hp in range(4):
            for comp in range(3):
                for pp in range(128):
                    m[hp, pp, comp, dirof(comp * 512 + hp * 128 + pp)] = 1
        for pp in range(128):
            ml[pp, dirof(1536 + pp)] = 1
        out[f"c_shm_{kind}"] = m
        out[f"c_shl_{kind}"] = ml
    NS = cfg["NS"]
    if NS:
        rows = NS // 64
        row_idx = np.repeat(np.arange(rows, dtype=f), 64)
        col_idx = np.tile(np.arange(64, dtype=f), rows)
        nfreq = 16
        freqs = (100.0 ** (-np.arange(nfreq, dtype=f) / nfreq)).astype(f)
        ang = np.concatenate([row_idx[:, None] * freqs, col_idx[:, None] * freqs], -1).astype(f)
        cs, sn = np.cos(ang).astype(f), np.sin(ang).astype(f)
        rope = np.zeros((2, 128, NS), f)
        for pp in range(128):
            dd = pp % 64
            rope[0, pp] = cs[:, dd % 32]
            rope[1, pp] = (-sn[:, dd % 32]) if dd < 32 else sn[:, dd % 32]
        out["rope"] = rope
    return out


def prep_core(inp, cfg, core, shared):
    f = np.float32
    NPS, NS = cfg["NPS"], cfg["NS"]
    m = dict(shared)
    cvec = np.zeros((2, D), f)
    cvec[0] = np.asarray(inp["c_ctx"], f)
    if NPS:
        m["xp"] = np.ascontiguousarray(np.asarray(inp["x_prompt"], f)[core * NPS:(core + 1) * NPS])
    if NS:
        nb = np.asarray(inp["x_sample"]).shape[0]
        b = (core * nb) // cfg.get("NCORES", 8)
        m["xs"] = np.ascontiguousarray(np.asarray(inp["x_sample"], f)[b])
        cvec[1] = np.asarray(inp["c"], f)[b]
        m["st_c"] = np.ascontiguousarray(np.asarray(inp["state_mlstm_c"], f)[b])
        m["st_n"] = np.ascontiguousarray(np.asarray(inp["state_mlstm_n"], f)[b])
        m["st_m"] = np.ascontiguousarray(np.asarray(inp["state_mlstm_m"], f)[b])
        m["st_h"] = np.ascontiguousarray(np.asarray(inp["state_lru_h"], f)[b])
        m["st_r"] = np.ascontiguousarray(np.asarray(inp["state_ret_r"], f)[b])
        m["st_s"] = np.ascontiguousarray(np.asarray(inp["state_rwkv_s"], f)[b])
    m["cvT"] = np.ascontiguousarray(cvec.reshape(2, KT, 128).transpose(2, 1, 0))
    return m


FULL_CFG = dict(DEPTH=4, NP=256, NPS=2, NS=4096, NCORES=8)
_CACHE = {}


def kernel(**inputs):
    cfg = FULL_CFG
    if "nc" not in _CACHE:
        kb = KB(cfg)
        _CACHE["nc"] = kb.build()
        _CACHE["kb"] = kb
    nc, kb = _CACHE["nc"], _CACHE["kb"]
    shared = prep_shared(inputs, cfg)
    in_maps = []
    for core in range(8):
        m = prep_core(inputs, cfg, core, shared)
        in_maps.append({k_: m[k_] for k_ in kb.din})
    res = run_bass_kernel_spmd(nc, in_maps, core_ids=list(range(8)))
    r = res.results
    L = cfg["DEPTH"]
    y_prompt = np.concatenate([r[i]["yp"] for i in range(8)], 0)
    y_sample = np.stack([r[0]["ys"], r[4]["ys"]], 0)
    outs = [y_prompt, y_sample]
    for nm in ("o_c", "o_n", "o_m", "o_h", "o_r", "o_s"):
        outs.append(np.concatenate([r[i][nm] for i in range(8)], 0))
    return tuple(np.asarray(o, np.float32) for o in outs)
```

```python
import contextlib
import numpy as np
import concourse.bass as bass
import concourse.mybir as mybir
from concourse.bass_utils import run_bass_kernel_spmd

F32 = mybir.dt.float32
BF16 = mybir.dt.bfloat16
AF = mybir.ActivationFunctionType
ALU = mybir.AluOpType
AX = mybir.AxisListType

D = 1024
KT = 8
NH = 8
HD = 64
DB = 512
EPS = 1e-6
P_IN = 7840
OFF_A, OFF_B, OFF_C, OFF_D = 0, 2592, 3616, 5664

ENGS = ("pe", "dve", "act", "pool", "sp")


class Buf:
    __slots__ = ("name", "w", "rs", "psum")

    def __init__(self, name):
        self.name = name
        self.w = None
        self.rs = []
        self.psum = False


class Op:
    __slots__ = ("eng", "meth", "kw", "deps", "dma", "sig", "sem", "val", "idx")


class Prog:
    def __init__(self, nc, n_dma_sems=8):
        self.nc = nc
        self.ops = []
        self.stack = contextlib.ExitStack()
        self.n_dma_sems = n_dma_sems
        self.nbuf = 0
        self.last = {e: None for e in ENGS}
        self.dmas_since_barrier = []

    def sb(self, name, shape, dt):
        return self.stack.enter_context(self.nc.sbuf_tensor(name, list(shape), dt))

    def ps(self, name, shape, dt):
        return self.stack.enter_context(self.nc.psum_tensor(name, list(shape), dt))

    def buf(self, name=None):
        self.nbuf += 1
        return Buf(name or f"b{self.nbuf}")

    def bufs(self, n):
        return [self.buf() for _ in range(n)]

    def _rec(self, eng, meth, kw, R, W, dma=False, extra_deps=()):
        deps = set(extra_deps)
        for b in R:
            if b.w is not None:
                deps.add(b.w)
            if b.psum:
                for r in b.rs:
                    if self.ops[r].eng != eng:
                        deps.add(r)
        for b in W:
            if b.w is not None:
                deps.add(b.w)
            deps.update(b.rs)
        i = len(self.ops)
        op = Op()
        op.eng, op.meth, op.kw, op.dma = eng, meth, kw, dma
        op.deps = sorted(deps)
        op.sig = bool(dma)
        op.sem = None
        op.val = 0
        op.idx = i
        self.ops.append(op)
        for d in deps:
            self.ops[d].sig = True
        for b in R:
            b.rs.append(i)
        for b in W:
            b.w = i
            b.rs = []
        self.last[eng] = i
        if dma:
            self.dmas_since_barrier.append(i)
        return op

    def op(self, eng, meth, R=(), W=(), **kw):
        return self._rec(eng, meth, kw, R, W)

    def dma(self, q, out, in_, R=(), W=(), **kw):
        kw = dict(kw)
        kw["out"] = out
        kw["in_"] = in_
        return self._rec(q, "dma_start", kw, R, W, dma=True)

    def barrier(self):
        lasts = [v for v in self.last.values() if v is not None] + list(self.dmas_since_barrier)
        self.dmas_since_barrier = []
        for e in ("pe", "dve", "act", "pool", "sp"):
            self._rec(e, "nop", {}, (), (), extra_deps=lasts)

    def emit(self):
        nc, st = self.nc, self.stack
        sems = {e: st.enter_context(nc.semaphore(f"s_{e}")) for e in ("pe", "dve", "act", "pool", "sp")}
        dsems = {q: [st.enter_context(nc.semaphore(f"d_{q}{i}")) for i in range(self.n_dma_sems)]
                 for q in ("sp", "act", "pool")}
        cnt = {e: 0 for e in sems}
        dcnt = {q: 0 for q in dsems}
        dval = {q: [0] * self.n_dma_sems for q in dsems}
        prev_on_sem = {}
        for op in self.ops:
            if op.dma:
                k = dcnt[op.eng] % self.n_dma_sems
                dcnt[op.eng] += 1
                dval[op.eng][k] += 16
                op.sem = ("d", op.eng, k)
                op.val = dval[op.eng][k]
                p = prev_on_sem.get(op.sem)
                if p is not None and p not in op.deps:
                    op.deps = sorted(set(op.deps) | {p})
                prev_on_sem[op.sem] = op.idx
            elif op.sig:
                cnt[op.eng] += 1
                op.sem = ("c", op.eng)
                op.val = cnt[op.eng]

        def semobj(key):
            return sems[key[1]] if key[0] == "c" else dsems[key[1]][key[2]]

        clocks = [None] * len(self.ops)
        known = {e: {} for e in ENGS}
        streams = {e: [] for e in ENGS}
        for op in self.ops:
            kn = known[op.eng]
            wm = {}
            for d in op.deps:
                dop = self.ops[d]
                if kn.get(dop.sem, 0) >= dop.val:
                    continue
                wm[dop.sem] = max(wm.get(dop.sem, 0), dop.val)
                for s, v in clocks[d].items():
                    if kn.get(s, 0) < v:
                        kn[s] = v
            ck = dict(kn)
            if op.sem is not None:
                ck[op.sem] = max(ck.get(op.sem, 0), op.val)
            clocks[op.idx] = ck
            streams[op.eng].append((op, wm))
        self.n_instr = {e: len(streams[e]) for e in ENGS}
        finals = {}
        for q in dsems:
            for k in range(self.n_dma_sems):
                if dval[q][k] > 0:
                    finals[("d", q, k)] = dval[q][k]
        block = st.enter_context(nc.Block())

        def run(engobj, ename):
            for op, wm in streams[ename]:
                for s, v in wm.items():
                    engobj.wait_ge(semobj(s), v)
                if op.meth == "nop":
                    ins = engobj.nop()
                else:
                    ins = getattr(engobj, op.meth)(**op.kw)
                if op.sem is not None:
                    ins.then_inc(semobj(op.sem), 16 if op.dma else 1)
            if ename == "sp":
                for s, v in finals.items():
                    engobj.wait_ge(semobj(s), v)

        block.tensor(lambda e: run(e, "pe"))
        block.vector(lambda e: run(e, "dve"))
        block.scalar(lambda e: run(e, "act"))
        block.gpsimd(lambda e: run(e, "pool"))
        block.sync(lambda e: run(e, "sp"))

    def close(self):
        self.stack.close()


def flip(a, dims=(-1,)):
    ap = [list(x) for x in a.ap]
    off = a.offset
    for d in dims:
        s, n = ap[d]
        off = off + (n - 1) * s
        ap[d] = [-s, n]
    return bass.AP(a.tensor, off, ap)


class Arena:
    def __init__(self, P, name, words, dt):
        self.t = P.sb(name, [128, words], dt)
        self.words = words
        self.pos = 0
        self.P = P
        self.hi = 0

    def reset(self):
        self.pos = 0

    def alloc(self, n, parts=128):
        n2 = (n + 7) // 8 * 8
        assert self.pos + n2 <= self.words, f"arena overflow {self.pos}+{n2}>{self.words}"
        a = self.t[0:parts, self.pos:self.pos + n]
        self.pos += n2
        self.hi = max(self.hi, self.pos)
        return a


class TT:
    __slots__ = ("ap", "b")

    def __init__(self, ap, b):
        self.ap, self.b = ap, b

    def __getitem__(self, k):
        return self.ap[k]


def v3(ap, a, b):
    return ap.rearrange("p (a b) -> p a b", a=a, b=b)


def bc_mid(ap, n):
    p, m = ap.shape
    return ap.unsqueeze(1).broadcast_to([p, n, m])


def bc_last(ap, n):
    shp = list(ap.shape)
    return ap.unsqueeze(len(shp)).broadcast_to(shp + [n])


class Job:
    pass


class KB:
    def __init__(self, cfg):
        self.cfg = cfg
        self.L = cfg["DEPTH"]
        self.mixers = cfg.get("mixers", "ABCD")
        nc = bass.Bass("TRN2", target_bir_lowering=False)
        self.nc = nc
        self.P = Prog(nc)
        self.din = {}
        self.dout = {}
        self.in_shapes = {}

    def inp(self, name, shape):
        t = self.nc.dram_tensor(name, list(shape), F32, kind="ExternalInput")
        self.din[name] = t
        self.in_shapes[name] = tuple(shape)
        return t.ap()

    def outp(self, name, shape):
        t = self.nc.dram_tensor(name, list(shape), F32, kind="ExternalOutput")
        self.dout[name] = t
        return t.ap()

    def scratch(self, name, shape, dt):
        return self.nc.dram_tensor(name, list(shape), dt, kind="Internal").ap()

    def tt(self, arena, n, parts=128):
        return TT(arena.alloc(n, parts), self.P.buf())

    def bank(self):
        i = self.bank_i
        self.bank_i = (i + 1) % len(self.banks)
        return self.banks[i]

    def bbank(self):
        i = self.bbank_i
        self.bbank_i = (i + 1) % len(self.bbanks)
        return self.bbanks[i]

    def op(self, eng, meth, R=(), W=(), **kw):
        return self.P.op(eng, meth, R=[x.b for x in R], W=[x.b for x in W], **kw)

    def dma(self, q, out, in_, R=(), W=(), **kw):
        return self.P.dma(q, out, in_, R=[x.b for x in R], W=[x.b for x in W], **kw)

    def build(self):
        cfg, P, nc, L = self.cfg, self.P, self.nc, self.L
        NP, NPS, NS = cfg["NP"], cfg["NPS"], cfg["NS"]
        self.SEGB = cfg.get("SEGB", 4)
        I = self.inp
        xp = I("xp", [NPS, NP, D]) if NPS else None
        xs = I("xs", [NS, D]) if NS else None
        cvT = I("cvT", [128, KT, 2])
        self.wA = I("wA", [L, 4, 128, KT, 640])
        self.wAg = I("wAg", [L, 128, KT, 32])
        self.wB = I("wB", [L, 4, 128, KT, 256])
        self.wC = I("wC", [L, 4, 128, KT, 768])
        self.wD = I("wD", [L, 4, 128, KT, 512])
        self.wDl = I("wDl", [L, 128, KT, 128])
        self.wout = I("wout", [L, 128, 16, D])
        self.wmod = I("wmod", [L, 128, KT, 3 * D])
        self.ng2 = I("ng2", [L, 2, D])
        self.bmod2 = I("bmod2", [L, 2, 3 * D])
        self.fgb = I("fgb", [128, D])
        self.gbA = I("gbA", [L, 8, 4])
        self.lruP = I("lruP", [L, 4, 128, 12])
        self.lruG = I("lruG", [L, 4, 128, 4, 128])
        self.retP = I("retP", [L, 4, 128, 6])
        self.rwP = I("rwP", [L, 4, 128, 12])
        self.rwLP = I("rwLP", [L, 128, 2])
        self.rwW2 = I("rwW2", [L, 2, 128, DB])
        c_ident = I("c_ident", [128, 128])
        c_masks = I("c_masks", [128, 4, 128])
        c_diff = I("c_diff", [128, 2, 128])
        c_pos = I("c_pos", [128, 4, 128])
        c_sel = I("c_sel", [8, 4, 128])
        c_sel2 = I("c_sel2", [2, 2, 128])
        c_bones = I("c_bones", [128, 128])
        c_hmask = I("c_hmask", [128, 4, 128])
        c_rmask = I("c_rmask", [128, 2, 512])
        self.c_shm = {}
        self.c_shm["p"] = I("c_shm_p", [4, 128, 3, 4])
        self.c_shm["s"] = I("c_shm_s", [4, 128, 3, 4])
        self.c_shl = {"p": I("c_shl_p", [128, 4]), "s": I("c_shl_s", [128, 4])}
        if NS:
            self.rope = I("rope", [2, 128, NS])
            self.st_in = dict(
                c=I("st_c", [L, 2, NH, HD, HD]), n=I("st_n", [L, 2, NH, HD]), m=I("st_m", [L, 2, NH]),
                h=I("st_h", [L, 2, DB]), r=I("st_r", [L, 2, NH, HD, HD]), s=I("st_s", [L, 2, NH, HD, HD]))
        O = self.outp
        if NPS:
            yp = O("yp", [NPS, NP, D])
            self.so = dict(c=O("o_c", [NPS, L, 2, NH, HD, HD]), n=O("o_n", [NPS, L, 2, NH, HD]),
                           m=O("o_m", [NPS, L, 2, NH]), h=O("o_h", [NPS, L, 2, DB]),
                           r=O("o_r", [NPS, L, 2, NH, HD, HD]), s=O("o_s", [NPS, L, 2, NH, HD, HD]))
        if NS:
            ys = O("ys", [NS, D])
        self.dbg = cfg.get("dbg", False)
        jobs = []
        for i in range(NPS):
            j = Job()
            j.name, j.N, j.kind, j.g, j.idx = f"p{i}", NP, "p", 0, i
            j.x_in, j.y_out = xp[i], yp[i]
            jobs.append(j)
        if NS:
            j = Job()
            j.name, j.N, j.kind, j.g, j.idx = "s", NS, "s", 1, 0
            j.x_in, j.y_out = xs, ys
            jobs.append(j)
        for j in jobs:
            j.NB = j.N // 128
            j.HT = self.scratch(f"HT_{j.name}", [KT, 128, j.N], BF16)
            j.YT = self.scratch(f"YT_{j.name}", [16, 128, j.N], BF16)
            j.XR = self.scratch(f"XR_{j.name}", [j.N, D], F32)
            j.GB = self.scratch(f"GB_{j.name}", [2, 2, 8, j.N], F32)
            j.bHT, j.bYT, j.bXR, j.bGB = TT(None, P.buf()), TT(None, P.buf()), TT(None, P.buf()), TT(None, P.buf())
            if self.dbg:
                j.dbgY = O(f"dbgY_{j.name}", [16, 128, j.N])
        self.jobs = jobs
        NMAX = max(j.N for j in jobs)
        NBMAX = NMAX // 128
        self.banks = [TT(P.ps(f"bk{i}", [128, 512], F32)[:], P.buf()) for i in range(6)]
        self.bbanks = [TT(P.ps(f"bb{i}", [128, 1024], BF16)[:], P.buf()) for i in range(2)]
        for t_ in self.banks + self.bbanks:
            t_.b.psum = True
        self.bank_i = 0
        self.bbank_i = 0
        cw = cfg.get("CARENA", 5000)
        CA = Arena(P, "carena", cw, F32)
        self.CA = CA
        CB = Arena(P, "cbarena", 1536, BF16)
        k = self
        k.ident = k.tt(CA, 128)
        k.masks = k.tt(CA, 512)
        k.diff = k.tt(CA, 256)
        k.pos = k.tt(CA, 512)
        k.sel = k.tt(CA, 512, parts=8)
        k.bones = k.tt(CA, 128)
        k.sel2 = k.tt(CA, 256, parts=2)
        k.rmask = k.tt(CA, 1024)
        k.identb = k.tt(CB, 128)
        k.bonesb = k.tt(CB, 128)
        k.masksb = k.tt(CB, 512)
        k.hmaskb = k.tt(CB, 512)
        k.dma("pool", v3(k.hmaskb.ap, 4, 128), c_hmask, W=[k.hmaskb])
        k.dma("sp", k.ident.ap, c_ident, W=[k.ident])
        k.dma("sp", v3(k.masks.ap, 4, 128), c_masks, W=[k.masks])
        k.dma("sp", v3(k.diff.ap, 2, 128), c_diff, W=[k.diff])
        k.dma("sp", v3(k.pos.ap, 4, 128), c_pos, W=[k.pos])
        k.dma("sp", v3(k.sel.ap, 4, 128), c_sel, W=[k.sel])
        k.dma("sp", k.bones.ap, c_bones, W=[k.bones])
        k.dma("sp", v3(k.sel2.ap, 2, 128), c_sel2, W=[k.sel2])
        k.dma("sp", v3(k.rmask.ap, 2, 512), c_rmask, W=[k.rmask])
        k.op("act", "activation", R=[k.ident], W=[k.identb], out=k.identb.ap, in_=k.ident.ap, func=AF.Copy)
        k.op("act", "activation", R=[k.bones], W=[k.bonesb], out=k.bonesb.ap, in_=k.bones.ap, func=AF.Copy)
        k.op("act", "activation", R=[k.masks], W=[k.masksb], out=k.masksb.ap, in_=k.masks.ap, func=AF.Copy)
        k.eps = k.tt(CA, 1)
        k.op("dve", "memset", W=[k.eps], ap=k.eps.ap, constant=EPS)
        k.cv = k.tt(CA, KT * 2)
        k.scT = k.tt(CB, KT * 2)
        k.dma("sp", v3(k.cv.ap, KT, 2), cvT, W=[k.cv])
        k.op("act", "activation", R=[k.cv], W=[k.scT], out=k.scT.ap, in_=k.cv.ap, func=AF.Silu)
        self.MODS = self.scratch("MODS", [L, 2, 3, D], F32)
        self.bMODS = TT(None, P.buf())
        for j in jobs:
            j.Atok = [k.tt(CA, j.NB * 8) for _ in range(2)]
            j.Etok = [k.tt(CA, j.NB * 8) for _ in range(2)]
            j.Fch = [k.tt(CA, j.NB, parts=8) for _ in range(2)]
            j.mfin = [k.tt(CA, 1, parts=8) for _ in range(2)]
        self.AF_ = Arena(P, "arena_f", cfg.get("AF", 19000), F32)
        self.AB_ = Arena(P, "arena_b", cfg.get("AB", 50000), BF16)

        if cfg.get("zero_yt"):
            z = k.tt(self.AB_, 512)
            k.op("dve", "memset", W=[z], ap=z.ap, constant=0.0)
            for j in jobs:
                for kt in range(16):
                    for t0 in range(0, j.N, 512):
                        n = min(512, j.N - t0)
                        k.dma("sp", j.YT[kt][:, t0:t0 + n], z[:, 0:n], R=[z], W=[j.bYT])
        for l in range(L):
            self.modvec(l)
        for l in range(L):
            for j in jobs:
                self.norm_phase(j, l, final=False)
            if "A" in self.mixers:
                for j in jobs:
                    self.mlstm_prepass(j, l)
                for hp in range(4):
                    sh = self.shared_w(self.wA[l, hp], 640)
                    for j in jobs:
                        self.mlstm_unit(j, l, hp, shared=sh)
            if "B" in self.mixers:
                for hp in range(4):
                    sh = self.shared_w(self.wB[l, hp], 256)
                    for j in jobs:
                        self.lru_unit(j, l, hp, shared=sh)
            if "C" in self.mixers:
                for hp in range(4):
                    sh = self.shared_w(self.wC[l, hp], 768)
                    for j in jobs:
                        self.ret_unit(j, l, hp, shared=sh)
            for j in jobs:
                if "D" in self.mixers:
                    self.rwkv_prepass(j, l)
                    for hp in range(4 if cfg.get("rw_stop", 9) > 0 else 0):
                        self.rwkv_unit(j, l, hp)
            for j in jobs:
                self.outproj_phase(j, l)
        for j in jobs:
            self.norm_phase(j, L, final=True)
        P.emit()
        P.close()
        return nc

    def new_phase(self):
        self.P.barrier()
        self.AF_.reset()
        self.AB_.reset()

    def unit_phase(self, shared):
        if shared is None:
            self.new_phase()
        else:
            self.P.barrier()
            self.AF_.pos, self.AB_.pos = shared[2]

    def shared_w(self, dram_ap, ncol):
        self.new_phase()
        W, W3 = self.load_w(dram_ap, ncol)
        return (W, W3, (self.AF_.pos, self.AB_.pos))

    def modvec(self, l):
        k, P = self, self.P
        self.new_phase()
        AFa, ABa = self.AF_, self.AB_
        mrow = k.tt(AFa, 3 * D, parts=2)
        brow = k.tt(AFa, 3 * D, parts=2)
        grow = k.tt(AFa, D, parts=2)
        k.dma("sp", brow.ap, self.bmod2[l], W=[brow])
        k.dma("sp", grow.ap, self.ng2[l], W=[grow])
        wbufs = [k.tt(ABa, KT * 512) for _ in range(2)]
        for cb in range(6):
            wb = wbufs[cb % 2]
            k.dma("pool", v3(wb.ap, KT, 512), self.wmod[l][:, :, cb * 512:(cb + 1) * 512], W=[wb])
            bk = k.bank()
            for kt in range(KT):
                k.op("pe", "matmul", R=[wb, k.scT], W=[bk], out=bk[0:2, :],
                     lhsT=v3(k.scT.ap, KT, 2)[:, kt, :], rhs=v3(wb.ap, KT, 512)[:, kt, :],
                     start=(kt == 0), stop=(kt == KT - 1))
            k.op("dve", "tensor_tensor", R=[bk, brow], W=[mrow], out=mrow[:, cb * 512:(cb + 1) * 512],
                 in0=bk[0:2, :], in1=brow[:, cb * 512:(cb + 1) * 512], op=ALU.add)
        arow = k.tt(AFa, D, parts=2)
        k.op("dve", "scalar_tensor_tensor", R=[mrow, grow], W=[arow], out=arow.ap, in0=mrow[:, D:2 * D],
             scalar=1.0, in1=grow.ap, op0=ALU.add, op1=ALU.mult)
        k.dma("sp", self.MODS[l, :, 0, :], arow.ap, R=[arow], W=[self.bMODS])
        k.dma("sp", self.MODS[l, :, 1, :], mrow[:, 0:D], R=[mrow], W=[self.bMODS])
        k.dma("sp", self.MODS[l, :, 2, :], mrow[:, 2 * D:3 * D], R=[mrow], W=[self.bMODS])

    def bcast_row(self, dram_ap_1d, n):
        a = dram_ap_1d
        return bass.AP(a.tensor, a.offset, [[0, 128], [1, n]])

    def norm_phase(self, j, l, final):
        k = self
        self.new_phase()
        AFa, ABa = self.AF_, self.AB_
        GBk = min(4, j.NB)
        src = j.x_in if l == 0 else j.XR
        if final:
            modA = k.tt(AFa, D)
            k.dma("sp", modA.ap, self.fgb, W=[modA])
        else:
            modA, modS = k.tt(AFa, D), k.tt(AFa, D)
            k.dma("sp", modA.ap, self.bcast_row(self.MODS[l, j.g, 0, :], D), R=[self.bMODS], W=[modA])
            k.dma("sp", modS.ap, self.bcast_row(self.MODS[l, j.g, 1, :], D), R=[self.bMODS], W=[modS])
        xg = [k.tt(AFa, GBk * D) for _ in range(2)]
        t1 = [k.tt(AFa, D) for _ in range(2)]
        ss = [k.tt(AFa, GBk) for _ in range(2)]
        rs = [k.tt(AFa, GBk) for _ in range(2)]
        junk = k.tt(ABa, D)
        hb = [k.tt(ABa, D) for _ in range(2)]
        htg = [k.tt(ABa, KT * GBk * 128) for _ in range(2)]
        for gi in range(j.NB // GBk):
            x, s_, r_, ht = xg[gi % 2], ss[gi % 2], rs[gi % 2], htg[gi % 2]
            tok0 = gi * GBk * 128
            ntk = GBk * 128
            x3 = v3(x.ap, GBk, D)
            k.dma("sp", x3, src[tok0:tok0 + ntk, :].rearrange("(b p) d -> p b d", p=128),
                  R=[] if l == 0 else [j.bXR], W=[x])
            for b in range(GBk):
                k.op("act", "activation", R=[x], W=[junk, s_], out=junk.ap, in_=x3[:, b, :], func=AF.Square,
                     accum_out=s_[:, b:b + 1])
            k.op("act", "activation", R=[s_], W=[r_], out=r_.ap, in_=s_.ap, func=AF.Ln, scale=1.0 / D,
                 bias=k.eps.ap)
            k.op("act", "activation", R=[r_], W=[r_], out=r_.ap, in_=r_.ap, func=AF.Exp, scale=-0.5)
            for b in range(GBk):
                tt1 = t1[b % 2]
                k.op("dve", "scalar_tensor_tensor", R=[x, r_, modA], W=[tt1], out=tt1.ap, in0=x3[:, b, :],
                     scalar=r_[:, b:b + 1], in1=modA.ap, op0=ALU.mult, op1=ALU.mult)
                if final:
                    k.dma("sp", j.y_out[tok0 + b * 128: tok0 + (b + 1) * 128, :], tt1.ap, R=[tt1])
                    continue
                h = hb[b % 2]
                k.op("pool", "tensor_tensor", R=[tt1, modS], W=[h], out=h.ap, in0=tt1.ap, in1=modS.ap, op=ALU.add)
                bb = k.bbank()
                for kt in range(KT):
                    k.op("pe", "transpose", R=[h, k.identb], W=[bb], out=bb[:, kt * 128:(kt + 1) * 128],
                         in_=h[:, kt * 128:(kt + 1) * 128], identity=k.identb.ap)
                k.op("act", "activation", R=[bb], W=[ht], out=v3(ht.ap, KT, ntk)[:, :, b * 128:(b + 1) * 128],
                     in_=v3(bb.ap, KT, 128), func=AF.Copy)
            if not final:
                k.dma("sp", j.HT.rearrange("k p n -> p k n")[:, :, tok0:tok0 + ntk], v3(ht.ap, KT, ntk),
                      R=[ht], W=[j.bHT])

    def outproj_phase(self, j, l):
        k = self
        self.new_phase()
        AFa, ABa = self.AF_, self.AB_
        GBk = min(4, j.NB)
        src = j.x_in if l == 0 else j.XR
        modG = k.tt(AFa, D)
        k.dma("sp", modG.ap, self.bcast_row(self.MODS[l, j.g, 2, :], D), R=[self.bMODS], W=[modG])
        wo = k.tt(ABa, 16 * D)
        wo3 = v3(wo.ap, 16, D)
        for q in range(4):
            k.dma("pool", wo3[:, q * 4:(q + 1) * 4, :], self.wout[l][:, q * 4:(q + 1) * 4, :], W=[wo])
        xg = [k.tt(AFa, GBk * D) for _ in range(2)]
        tmp = [k.tt(AFa, 512) for _ in range(2)]
        ytg = [k.tt(ABa, 16 * GBk * 128) for _ in range(2)]
        for gi in range(j.NB // GBk):
            x, yt = xg[gi % 2], ytg[gi % 2]
            tok0 = gi * GBk * 128
            ntk = GBk * 128
            x3 = v3(x.ap, GBk, D)
            yt3 = v3(yt.ap, 16, ntk)
            k.dma("sp", x3, src[tok0:tok0 + ntk, :].rearrange("(b p) d -> p b d", p=128),
                  R=[] if l == 0 else [j.bXR], W=[x])
            k.dma("sp", yt3, j.YT.rearrange("k p n -> p k n")[:, :, tok0:tok0 + ntk], R=[j.bYT], W=[yt])
            for b in range(GBk):
                for hf in range(2):
                    bk = k.bank()
                    for kt in range(16):
                        k.op("pe", "matmul", R=[yt, wo], W=[bk], out=bk.ap, lhsT=yt3[:, kt, b * 128:(b + 1) * 128],
                             rhs=wo3[:, kt, hf * 512:(hf + 1) * 512], start=(kt == 0), stop=(kt == 15))
                    tm = tmp[hf]
                    k.op("dve", "tensor_tensor", R=[bk, modG], W=[tm], out=tm.ap, in0=bk.ap,
                         in1=modG[:, hf * 512:(hf + 1) * 512], op=ALU.mult)
                    k.op("pool", "tensor_tensor", R=[tm, x], W=[x], out=x3[:, b, hf * 512:(hf + 1) * 512],
                         in0=tm.ap, in1=x3[:, b, hf * 512:(hf + 1) * 512], op=ALU.add)
            k.dma("sp", j.XR[tok0:tok0 + ntk, :].rearrange("(b p) d -> p b d", p=128), x3, R=[x], W=[j.bXR])

    def stream_ht(self, j, fn):
        k = self
        nblk = (j.N + 511) // 512
        hts = [k.tt(self.AB_, KT * 512) for _ in range(2)]
        for b5 in range(nblk):
            tok0 = b5 * 512
            nt = min(512, j.N - tok0)
            ht = hts[b5 % 2]
            h3 = v3(ht.ap, KT, 512)[:, :, 0:nt]
            k.dma("sp", h3, j.HT.rearrange("k p n -> p k n")[:, :, tok0:tok0 + nt], R=[j.bHT], W=[ht])
            fn(b5, tok0, nt, h3, ht)

    def mm_fm(self, bk, out_ap, W, W3, c0, ncol, h3, ht, nt):
        for kt in range(KT):
            self.op("pe", "matmul", R=[W, ht], W=[bk], out=out_ap, lhsT=W3[:, kt, c0:c0 + ncol],
                    rhs=h3[:, kt, 0:nt], start=(kt == 0), stop=(kt == KT - 1))

    def mm_tm(self, bk, out_ap, W, W3, c0, ncol, h3, ht, t0):
        for kt in range(KT):
            self.op("pe", "matmul", R=[W, ht], W=[bk], out=out_ap, lhsT=h3[:, kt, t0:t0 + 128],
                    rhs=W3[:, kt, c0:c0 + ncol], start=(kt == 0), stop=(kt == KT - 1))

    def load_w(self, dram_ap, ncol):
        W = self.tt(self.AB_, KT * ncol)
        W3 = v3(W.ap, KT, ncol)
        self.dma("pool", W3, dram_ap, W=[W])
        return W, W3

    def segs(self, j, d):
        SB = min(self.SEGB, j.NB)
        lst = [(b0, min(SB, j.NB - b0)) for b0 in range(0, j.NB, SB)]
        return lst if d == 0 else lst[::-1]

    def yt_store(self, j, mixer, hp, tok0, ntk, yt_tt, yt_ap):
        kt = mixer * 4 + hp
        self.dma("sp", j.YT[kt][:, tok0:tok0 + ntk], yt_ap, R=[yt_tt], W=[j.bYT])
        if self.dbg:
            pass

    def lru_unit(self, j, l, hp, shared=None):
        k = self
        self.unit_phase(shared)
        AFa, ABa = self.AF_, self.AB_
        N = j.N
        W, W3 = shared[0:2] if shared else self.load_w(self.wB[l, hp], 256)
        lp = k.tt(AFa, 12)
        k.dma("sp", lp.ap, self.lruP[l, hp], W=[lp])
        gwf = k.tt(AFa, 512)
        k.dma("sp", v3(gwf.ap, 4, 128), self.lruG[l, hp], W=[gwf])
        gw = k.tt(ABa, 512)
        k.op("act", "activation", R=[gwf], W=[gw], out=gw.ap, in_=gwf.ap, func=AF.Copy)
        gw3 = v3(gw.ap, 4, 128)
        nsp = k.tt(AFa, 2)
        k.op("act", "activation", R=[lp], W=[nsp], out=nsp.ap, in_=lp[:, 9:11], func=AF.Exp, scale=-1.0)
        k.op("act", "activation", R=[nsp], W=[nsp], out=nsp.ap, in_=nsp.ap, func=AF.Ln, bias=1.0)
        k.op("dve", "tensor_scalar", R=[nsp], W=[nsp], out=nsp.ap, in0=nsp.ap, scalar1=-8.0, scalar2=None,
             op0=ALU.mult)
        XB = k.tt(AFa, N + 3)
        XC = k.tt(AFa, N)
        HF = k.tt(AFa, N)
        XCb = k.tt(ABa, N)
        SG = k.tt(ABa, N)
        k.op("dve", "memset", W=[XB], ap=XB[:, 0:1], constant=0.0)
        k.op("dve", "memset", W=[XB], ap=XB[:, N + 1:N + 3], constant=0.0)

        def blk(b5, tok0, nt, h3, ht):
            bk = k.bank()
            k.mm_fm(bk, bk[:, 0:nt], W, W3, 0, 128, h3, ht, nt)
            k.op("act", "activation", R=[bk], W=[XB], out=XB[:, 1 + tok0:1 + tok0 + nt], in_=bk[:, 0:nt], func=AF.Copy)
            bk2 = k.bank()
            k.mm_fm(bk2, bk2[:, 0:nt], W, W3, 128, 128, h3, ht, nt)
            k.op("act", "activation", R=[bk2], W=[SG], out=SG[:, tok0:tok0 + nt], in_=bk2[:, 0:nt], func=AF.Silu)
        self.stream_ht(j, blk)
        k.op("dve", "tensor_scalar", R=[XB, lp], W=[XC], out=XC.ap, in0=XB[:, 0:N], scalar1=lp[:, 0:1],
             scalar2=lp[:, 4:5], op0=ALU.mult, op1=ALU.add)
        for t in range(1, 4):
            k.op("dve", "scalar_tensor_tensor", R=[XB, lp, XC], W=[XC], out=XC.ap, in0=XB[:, t:t + N],
                 scalar=lp[:, t:t + 1], in1=XC.ap, op0=ALU.mult, op1=ALU.add)
        k.op("pool", "tensor_copy", R=[XC], W=[XCb], out=XCb.ap, in_=XC.ap)
        SEG = min(self.SEGB, j.NB) * 128
        tm = {n: [k.tt(AFa, SEG) for _ in range(2)] for n in ("sr", "si", "a", "u", "bt", "h")}
        carry = k.tt(AFa, 2)
        for d in range(2):
            if j.kind == "s":
                k.dma("sp", carry[:, d:d + 1], self.st_in["h"][l, d, hp * 128:(hp + 1) * 128].unsqueeze(1), W=[carry])
            else:
                k.op("dve", "memset", W=[carry], ap=carry[:, d:d + 1], constant=0.0)
        ytb = [k.tt(ABa, SEG) for _ in range(2)]
        for d in range(2):
            cur = (carry, carry[:, d:d + 1])
            for si_, (b0, nb) in enumerate(self.segs(j, d)):
                s0, n = b0 * 128, nb * 128
                T = {nme: tm[nme][si_ % 2] for nme in tm}
                bkr, bki = k.bank(), k.bank()
                k.op("pe", "matmul", R=[gw, XCb], W=[bkr], out=bkr[:, 0:n], lhsT=gw3[:, d * 2 + 0, :],
                     rhs=XCb[:, s0:s0 + n], start=True, stop=True)
                k.op("pe", "matmul", R=[gw, XCb], W=[bki], out=bki[:, 0:n], lhsT=gw3[:, d * 2 + 1, :],
                     rhs=XCb[:, s0:s0 + n], start=True, stop=True)
                k.op("act", "activation", R=[bkr, lp], W=[T["sr"]], out=T["sr"][:, 0:n], in_=bkr[:, 0:n],
                     func=AF.Sigmoid, bias=lp[:, 5 + d * 2:6 + d * 2])
                k.op("act", "activation", R=[bki, lp], W=[T["si"]], out=T["si"][:, 0:n], in_=bki[:, 0:n],
                     func=AF.Sigmoid, bias=lp[:, 6 + d * 2:7 + d * 2])
                k.op("act", "activation", R=[T["sr"], nsp], W=[T["a"]], out=T["a"][:, 0:n], in_=T["sr"][:, 0:n],
                     func=AF.Exp, scale=nsp[:, d:d + 1])
                k.op("dve", "scalar_tensor_tensor", R=[T["a"]], W=[T["u"]], out=T["u"][:, 0:n], in0=T["a"][:, 0:n],
                     scalar=0.99999994, in1=T["a"][:, 0:n], op0=ALU.min, op1=ALU.mult)
                k.op("act", "activation", R=[T["u"]], W=[T["bt"]], out=T["bt"][:, 0:n], in_=T["u"][:, 0:n],
                     func=AF.Ln, scale=-1.0, bias=1.0)
                k.op("act", "activation", R=[T["bt"]], W=[T["bt"]], out=T["bt"][:, 0:n], in_=T["bt"][:, 0:n],
                     func=AF.Exp, scale=0.5)
                k.op("dve", "tensor_tensor", R=[T["si"], XC], W=[T["si"]], out=T["si"][:, 0:n], in0=T["si"][:, 0:n],
                     in1=XC[:, s0:s0 + n], op=ALU.mult)
                k.op("pool", "tensor_tensor", R=[T["si"], T["bt"]], W=[T["bt"]], out=T["bt"][:, 0:n],
                     in0=T["si"][:, 0:n], in1=T["bt"][:, 0:n], op=ALU.mult)
                a_ap, b_ap, h_ap = T["a"][:, 0:n], T["bt"][:, 0:n], T["h"][:, 0:n]
                if d == 1:
                    a_ap, b_ap, h_ap = flip(a_ap), flip(b_ap), flip(h_ap)
                k.op("dve", "tensor_tensor_scan", R=[T["a"], T["bt"], cur[0]], W=[T["h"]], out=h_ap, data0=a_ap,
                     data1=b_ap, initial=cur[1], op0=ALU.mult, op1=ALU.add)
                cur = (T["h"], T["h"][:, n - 1:n] if d == 0 else T["h"][:, 0:1])
                if d == 0:
                    k.op("pool", "tensor_copy", R=[T["h"]], W=[HF], out=HF[:, s0:s0 + n], in_=T["h"][:, 0:n])
                else:
                    yt = ytb[si_ % 2]
                    k.op("dve", "tensor_tensor", R=[T["h"], HF], W=[T["u"]], out=T["u"][:, 0:n], in0=T["h"][:, 0:n],
                         in1=HF[:, s0:s0 + n], op=ALU.add)
                    k.op("pool", "tensor_tensor", R=[T["u"], SG], W=[yt], out=yt[:, 0:n], in0=T["u"][:, 0:n],
                         in1=SG[:, s0:s0 + n], op=ALU.mult)
                    k.yt_store(j, 1, hp, s0, n, yt, yt[:, 0:n])
            if j.kind == "p":
                k.dma("sp", self.so["h"][j.idx, l, d, hp * 128:(hp + 1) * 128].unsqueeze(1), cur[1], R=[cur[0]])

    def tail(self, j, mixer, hp, s0, nb, X, SGap, SG, tmps, post=None):
        k = self
        n = nb * 128
        sq, ss, yb, yt = tmps
        k.op("act", "activation", R=[X], W=[sq], out=sq[:, 0:n], in_=X[:, 0:n], func=AF.Square)
        k.op("dve", "tensor_reduce", R=[sq], W=[ss], out=ss[:, 0:nb * 2], in_=v3(sq[:, 0:n], nb * 2, 64),
             axis=AX.X, op=ALU.add)
        k.op("act", "activation", R=[ss], W=[ss], out=ss[:, 0:nb * 2], in_=ss[:, 0:nb * 2], func=AF.Ln,
             scale=1.0 / 64, bias=k.eps.ap)
        k.op("act", "activation", R=[ss], W=[ss], out=ss[:, 0:nb * 2], in_=ss[:, 0:nb * 2], func=AF.Exp, scale=-0.5)
        if SGap is not None:
            k.op("dve", "tensor_tensor", R=[X, ss], W=[sq], out=v3(sq[:, 0:n], nb * 2, 64),
                 in0=v3(X[:, 0:n], nb * 2, 64), in1=bc_last(ss[:, 0:nb * 2], 64), op=ALU.mult)
            k.op("pool", "tensor_tensor", R=[sq, SG], W=[yb], out=yb[:, 0:n], in0=sq[:, 0:n], in1=SGap, op=ALU.mult)
        else:
            k.op("dve", "tensor_tensor", R=[X, ss], W=[yb], out=v3(yb[:, 0:n], nb * 2, 64),
                 in0=v3(X[:, 0:n], nb * 2, 64), in1=bc_last(ss[:, 0:nb * 2], 64), op=ALU.mult)
        bb = k.bbank()
        for b in range(nb):
            k.op("pe", "transpose", R=[yb, k.identb], W=[bb], out=bb[:, b * 128:(b + 1) * 128],
                 in_=yb[:, b * 128:(b + 1) * 128], identity=k.identb.ap)
        if post is None:
            k.op("act", "activation", R=[bb], W=[yt], out=yt[:, 0:n], in_=bb[:, 0:n], func=AF.Copy)
        else:
            post(bb, yt, n)
        k.yt_store(j, mixer, hp, s0, n, yt, yt[:, 0:n])

    def tail_tmps(self, SEG, nbuf=2):
        k = self
        return [(k.tt(self.AF_, SEG), k.tt(self.AF_, 16), k.tt(self.AB_, SEG), k.tt(self.AB_, SEG)) for _ in range(nbuf)]

    def ret_unit(self, j, l, hp, shared=None):
        k = self
        self.unit_phase(shared)
        AFa, ABa = self.AF_, self.AB_
        N, NB = j.N, j.NB
        rope = j.kind == "s"
        W, W3 = shared[0:2] if shared else self.load_w(self.wC[l, hp], 768)
        rp = k.tt(AFa, 6)
        k.dma("sp", rp.ap, self.retP[l, hp], W=[rp])
        lg = k.tt(AFa, 6)
        k.op("act", "activation", R=[rp], W=[lg], out=lg.ap, in_=rp.ap, func=AF.Exp, scale=-1.0)
        k.op("act", "activation", R=[lg], W=[lg], out=lg.ap, in_=lg.ap, func=AF.Ln, bias=1.0)
        k.op("dve", "tensor_scalar", R=[lg], W=[lg], out=lg.ap, in0=lg.ap, scalar1=-1.0, scalar2=None, op0=ALU.mult)
        DT = [k.tt(AFa, 256) for _ in range(2)]
        XI = [k.tt(AFa, 128) for _ in range(2)]
        WK = [k.tt(AFa, 128) for _ in range(2)]
        dec = k.tt(AFa, 2)
        diff3, mask3, pos3 = v3(k.diff.ap, 2, 128), v3(k.masks.ap, 4, 128), v3(k.pos.ap, 4, 128)
        for d in range(2):
            for h in range(2):
                dt = v3(DT[d].ap, 2, 128)[:, h, :]
                k.op("act", "activation", R=[k.diff, lg], W=[DT[d]], out=dt, in_=diff3[:, d, :], func=AF.Exp,
                     scale=lg[:, 2 + d * 2 + h:3 + d * 2 + h])
                k.op("dve", "tensor_tensor", R=[DT[d], k.masks], W=[DT[d]], out=dt, in0=dt, in1=mask3[:, d, :],
                     op=ALU.mult)
            k.op("act", "activation", R=[k.pos, lg], W=[XI[d]], out=XI[d].ap, in_=pos3[:, d, :], func=AF.Exp,
                 scale=lg[:, d:d + 1])
            k.op("act", "activation", R=[k.pos, lg], W=[WK[d]], out=WK[d].ap, in_=pos3[:, 2 + d, :], func=AF.Exp,
                 scale=lg[:, d:d + 1])
        k.op("act", "activation", R=[lg], W=[dec], out=dec.ap, in_=lg[:, 0:2], func=AF.Exp, scale=128.0)
        QT, KT_ = k.tt(ABa, N), k.tt(ABa, N)
        V, SG = k.tt(ABa, N), k.tt(ABa, N)
        YF = k.tt(AFa, N)
        V3_, SG3 = v3(V.ap, NB, 128), v3(SG.ap, NB, 128)
        if rope:
            cs = [k.tt(AFa, 512) for _ in range(2)]
            sn = [k.tt(AFa, 512) for _ in range(2)]
            rt = [k.tt(AFa, 512) for _ in range(4)]

        def blk(b5, tok0, nt, h3, ht):
            if rope:
                c_, s_ = cs[b5 % 2], sn[b5 % 2]
                k.dma("sp", c_[:, 0:nt], self.rope[0][:, tok0:tok0 + nt], W=[c_])
                k.dma("sp", s_[:, 0:nt], self.rope[1][:, tok0:tok0 + nt], W=[s_])
            for qi, (dst, scl) in enumerate(((QT, 1.0), (KT_, 0.125))):
                bka = k.bank()
                k.mm_fm(bka, bka[:, 0:nt], W, W3, qi * 256, 128, h3, ht, nt)
                if not rope:
                    k.op("act", "activation", R=[bka], W=[dst], out=dst[:, tok0:tok0 + nt], in_=bka[:, 0:nt],
                         func=AF.Copy, scale=scl)
                    continue
                bkb = k.bank()
                k.mm_fm(bkb, bkb[:, 0:nt], W, W3, qi * 256 + 128, 128, h3, ht, nt)
                t1, t2 = rt[qi * 2], rt[qi * 2 + 1]
                k.op("dve", "scalar_tensor_tensor", R=[bka, c_], W=[t1], out=t1[:, 0:nt], in0=bka[:, 0:nt], scalar=scl,
                     in1=c_[:, 0:nt], op0=ALU.mult, op1=ALU.mult)
                k.op("dve", "scalar_tensor_tensor", R=[bkb, s_], W=[t2], out=t2[:, 0:nt], in0=bkb[:, 0:nt], scalar=scl,
                     in1=s_[:, 0:nt], op0=ALU.mult, op1=ALU.mult)
                k.op("pool", "tensor_tensor", R=[t1, t2], W=[dst], out=dst[:, tok0:tok0 + nt], in0=t1[:, 0:nt],
                     in1=t2[:, 0:nt], op=ALU.add)
            for jj in range(0, nt // 128, 2):
                nj = min(2, nt // 128 - jj)
                bk = k.bank()
                for q in range(nj):
                    k.mm_tm(bk, bk[:, q * 256:(q + 1) * 256], W, W3, 512, 256, h3, ht, (jj + q) * 128)
                blk0 = tok0 // 128 + jj
                pv = v3(bk[:, 0:nj * 256], nj, 256)
                k.op("act", "activation", R=[bk], W=[V], out=V3_[:, blk0:blk0 + nj, :], in_=pv[:, :, 0:128], func=AF.Copy)
                k.op("act", "activation", R=[bk], W=[SG], out=SG3[:, blk0:blk0 + nj, :], in_=pv[:, :, 128:256],
                     func=AF.Silu)
        self.stream_ht(j, blk)
        SB = min(self.SEGB, NB)
        SEG = SB * 128
        qs_ = [k.tt(ABa, SEG) for _ in range(2)]
        ks_ = [k.tt(ABa, SEG) for _ in range(2)]
        ktk = [k.tt(ABa, SEG) for _ in range(2)]
        RS = [k.tt(ABa, SB * 64) for _ in range(2)]
        Sm = [[k.tt(ABa, SEG) for _ in range(2)] for _ in range(2)]
        ysum = [k.tt(AFa, SEG) for _ in range(2)]
        ttm = self.tail_tmps(SEG)
        Rst = k.tt(AFa, 64)
        for d in range(2):
            if j.kind == "s":
                k.dma("sp", Rst.ap, self.st_in["r"][l, d, 2 * hp:2 * hp + 2].rearrange("h a v -> (h a) v"), W=[Rst])
            else:
                k.op("dve", "memset", W=[Rst], ap=Rst.ap, constant=0.0)
            for si_, (b0, nb) in enumerate(self.segs(j, d)):
                s0, n = b0 * 128, nb * 128
                pq, pk, pt, prs = qs_[si_ % 2], ks_[si_ % 2], ktk[si_ % 2], RS[si_ % 2]
                k.op("dve", "tensor_tensor", R=[QT, XI[d]], W=[pq], out=v3(pq[:, 0:n], nb, 128),
                     in0=v3(QT[:, s0:s0 + n], nb, 128), in1=bc_mid(XI[d].ap, nb), op=ALU.mult)
                k.op("pool", "tensor_tensor", R=[KT_, WK[d]], W=[pk], out=v3(pk[:, 0:n], nb, 128),
                     in0=v3(KT_[:, s0:s0 + n], nb, 128), in1=bc_mid(WK[d].ap, nb), op=ALU.mult)
                bb = k.bbank()
                for b in range(nb):
                    k.op("pe", "transpose", R=[pk, k.identb], W=[bb], out=bb[:, b * 128:(b + 1) * 128],
                         in_=pk[:, b * 128:(b + 1) * 128], identity=k.identb.ap)
                k.op("act", "activation", R=[bb], W=[pt], out=pt[:, 0:n], in_=bb[:, 0:n], func=AF.Copy)
                bdr = k.bank()
                for b in range(nb):
                    for h in range(2):
                        k.op("pe", "matmul", R=[pt, V], W=[bdr], out=bdr[h * 64:(h + 1) * 64, b * 64:(b + 1) * 64],
                             lhsT=pt[:, b * 128 + h * 64:b * 128 + (h + 1) * 64], rhs=V3_[:, b0 + b, h * 64:(h + 1) * 64],
                             start=True, stop=True)
                order = range(nb) if d == 0 else range(nb - 1, -1, -1)
                for b in order:
                    k.op("pool", "tensor_copy", R=[Rst], W=[prs], out=prs[:, b * 64:(b + 1) * 64], in_=Rst.ap)
                    k.op("dve", "scalar_tensor_tensor", R=[Rst, dec, bdr], W=[Rst], out=Rst.ap, in0=Rst.ap,
                         scalar=dec[:, d:d + 1], in1=bdr[:, b * 64:(b + 1) * 64], op0=ALU.mult, op1=ALU.add)
                for h in range(2):
                    bst = k.bank()
                    for b in range(nb):
                        tk = slice(s0 + b * 128, s0 + (b + 1) * 128)
                        k.op("pe", "matmul", R=[KT_, QT], W=[bst], out=bst[:, b * 128:(b + 1) * 128],
                             lhsT=KT_[h * 64:(h + 1) * 64, tk], rhs=QT[h * 64:(h + 1) * 64, tk], start=True, stop=True)
                    sm = Sm[si_ % 2][h]
                    k.op("dve", "tensor_tensor", R=[bst, DT[d]], W=[sm], out=v3(sm[:, 0:n], nb, 128),
                         in0=v3(bst[:, 0:n], nb, 128), in1=bc_mid(v3(DT[d].ap, 2, 128)[:, h, :], nb), op=ALU.mult)
                bo = k.bank()
                for b in range(nb):
                    for h in range(2):
                        sm = Sm[si_ % 2][h]
                        oo = bo[:, b * 128 + h * 64:b * 128 + (h + 1) * 64]
                        k.op("pe", "matmul", R=[sm, V], W=[bo], out=oo, lhsT=sm[:, b * 128:(b + 1) * 128],
                             rhs=V3_[:, b0 + b, h * 64:(h + 1) * 64], start=True, stop=False)
                        k.op("pe", "matmul", R=[pq, prs], W=[bo], out=oo, lhsT=pq[h * 64:(h + 1) * 64, b * 128:(b + 1) * 128],
                             rhs=prs[h * 64:(h + 1) * 64, b * 64:(b + 1) * 64], start=False, stop=True)
                if d == 0:
                    k.op("act", "activation", R=[bo], W=[YF], out=YF[:, s0:s0 + n], in_=bo[:, 0:n], func=AF.Copy)
                else:
                    ys_ = ysum[si_ % 2]
                    k.op("dve", "tensor_tensor", R=[bo, YF], W=[ys_], out=ys_[:, 0:n], in0=bo[:, 0:n],
                         in1=YF[:, s0:s0 + n], op=ALU.add)
                    self.tail(j, 2, hp, s0, nb, ys_, SG[:, s0:s0 + n], SG, ttm[si_ % 2])
            if j.kind == "p":
                k.dma("sp", self.so["r"][j.idx, l, d, 2 * hp:2 * hp + 2].rearrange("h a v -> (h a) v"), Rst.ap, R=[Rst])

    def mlstm_prepass(self, j, l):
        k = self
        self.new_phase()
        AFa, ABa = self.AF_, self.AB_
        N, NB = j.N, j.NB
        W, W3 = self.load_w(self.wAg[l], 32)
        gbt = k.tt(AFa, 4, parts=8)
        ngb = k.tt(AFa, 4, parts=8)
        k.dma("sp", gbt.ap, self.gbA[l], W=[gbt])
        k.op("dve", "tensor_scalar", R=[gbt], W=[ngb], out=ngb.ap, in0=gbt.ap, scalar1=-1.0, scalar2=None, op0=ALU.mult)
        GS = [k.tt(AFa, NB, parts=8) for _ in range(2)]
        BL = [k.tt(AFa, NB, parts=8) for _ in range(2)]
        rm3 = v3(k.rmask.ap, 2, 512)
        tsp = [k.tt(AFa, 512, parts=8) for _ in range(2)]
        tb = [k.tt(AFa, 512, parts=8) for _ in range(2)]
        tg = [k.tt(AFa, 512, parts=8) for _ in range(2)]
        tpm = [k.tt(AFa, 512, parts=8) for _ in range(2)]

        def blk(b5, tok0, nt, h3, ht):
            nch = nt // 128
            c0 = tok0 // 128
            bks = []
            for gi in range(4):
                bk = k.bank()
                k.mm_fm(bk, bk[0:8, 0:nt], W, W3, gi * 8, 8, h3, ht, nt)
                bks.append(bk)
            for d in range(2):
                ig, fg = bks[2 * d], bks[2 * d + 1]
                sp_, b_, g_, pm_ = tsp[d], tb[d], tg[d], tpm[d]
                fl = (lambda a: a) if d == 0 else flip
                k.op("act", "activation", R=[fg, ngb], W=[sp_], out=sp_[:, 0:nt], in_=fg[0:8, 0:nt], func=AF.Exp,
                     scale=-1.0, bias=ngb[:, 2 * d + 1:2 * d + 2])
                k.op("act", "activation", R=[sp_], W=[sp_], out=sp_[:, 0:nt], in_=sp_[:, 0:nt], func=AF.Ln, bias=1.0)
                k.op("dve", "tensor_tensor_scan", R=[sp_, k.rmask], W=[b_], out=fl(b_[:, 0:nt]), data0=rm3[0:8, 0, 0:nt],
                     data1=fl(sp_[:, 0:nt]), initial=0.0, op0=ALU.mult, op1=ALU.subtract)
                k.op("dve", "scalar_tensor_tensor", R=[ig, gbt, b_], W=[g_], out=g_[:, 0:nt], in0=ig[0:8, 0:nt],
                     scalar=gbt[:, 2 * d:2 * d + 1], in1=b_[:, 0:nt], op0=ALU.add, op1=ALU.subtract)
                k.op("dve", "tensor_tensor_scan", R=[g_, k.rmask], W=[pm_], out=fl(pm_[:, 0:nt]), data0=rm3[0:8, 1, 0:nt],
                     data1=fl(g_[:, 0:nt]), initial=0.0, op0=ALU.add, op1=ALU.max)
                e = 127 if d == 0 else 0
                k.op("pool", "tensor_copy", R=[pm_], W=[GS[d]], out=GS[d][:, c0:c0 + nch],
                     in_=v3(pm_[:, 0:nt], nch, 128)[:, :, e])
                k.op("pool", "tensor_copy", R=[b_], W=[BL[d]], out=BL[d][:, c0:c0 + nch],
                     in_=v3(b_[:, 0:nt], nch, 128)[:, :, e])
                k.dma("sp", j.GB[d, 0, :, tok0:tok0 + nt], g_[:, 0:nt], R=[g_], W=[j.bGB])
                k.dma("sp", j.GB[d, 1, :, tok0:tok0 + nt], b_[:, 0:nt], R=[b_], W=[j.bGB])
        self.stream_ht(j, blk)
        ML = [k.tt(AFa, NB, parts=8) for _ in range(2)]
        for d in range(2):
            fl = (lambda a: a) if d == 0 else flip
            m0 = k.tt(AFa, 1, parts=8)
            if j.kind == "s":
                k.dma("sp", m0.ap, self.st_in["m"][l, d, :].unsqueeze(1), W=[m0])
            else:
                k.op("dve", "memset", W=[m0], ap=m0.ap, constant=0.0)
            Mall, MP = k.tt(AFa, NB, parts=8), k.tt(AFa, NB, parts=8)
            k.op("dve", "tensor_tensor_scan", R=[GS[d], BL[d], m0], W=[Mall], out=fl(Mall.ap), data0=fl(GS[d].ap),
                 data1=fl(BL[d].ap), initial=m0.ap, op0=ALU.max, op1=ALU.add)
            if d == 0:
                k.op("pool", "tensor_copy", R=[m0], W=[MP], out=MP[:, 0:1], in_=m0.ap)
                if NB > 1:
                    k.op("pool", "tensor_copy", R=[Mall], W=[MP], out=MP[:, 1:NB], in_=Mall[:, 0:NB - 1])
                k.op("pool", "tensor_copy", R=[Mall], W=[j.mfin[d]], out=j.mfin[d].ap, in_=Mall[:, NB - 1:NB])
            else:
                k.op("pool", "tensor_copy", R=[m0], W=[MP], out=MP[:, NB - 1:NB], in_=m0.ap)
                if NB > 1:
                    k.op("pool", "tensor_copy", R=[Mall], W=[MP], out=MP[:, 0:NB - 1], in_=Mall[:, 1:NB])
                k.op("pool", "tensor_copy", R=[Mall], W=[j.mfin[d]], out=j.mfin[d].ap, in_=Mall[:, 0:1])
            k.op("dve", "tensor_tensor", R=[MP, GS[d]], W=[ML[d]], out=ML[d].ap, in0=MP.ap, in1=GS[d].ap, op=ALU.max)
            k.op("dve", "tensor_tensor", R=[MP, ML[d]], W=[MP], out=MP.ap, in0=MP.ap, in1=ML[d].ap, op=ALU.subtract)
            k.op("act", "activation", R=[MP], W=[j.Fch[d]], out=j.Fch[d].ap, in_=MP.ap, func=AF.Exp)
            if j.kind == "p":
                k.dma("sp", self.so["m"][j.idx, l, d, :].unsqueeze(1), j.mfin[d].ap, R=[j.mfin[d]])
        ta = [k.tt(AFa, 512, parts=8) for _ in range(2)]
        te = [k.tt(AFa, 512, parts=8) for _ in range(2)]
        for b5 in range((N + 511) // 512):
            tok0 = b5 * 512
            nt = min(512, N - tok0)
            nch, c0 = nt // 128, tok0 // 128
            for d in range(2):
                a_, e_ = ta[d], te[d]
                k.dma("sp", a_[:, 0:nt], j.GB[d, 0, :, tok0:tok0 + nt], R=[j.bGB], W=[a_])
                k.dma("sp", e_[:, 0:nt], j.GB[d, 1, :, tok0:tok0 + nt], R=[j.bGB], W=[e_])
                mlb = bc_last(ML[d][:, c0:c0 + nch], 128)
                k.op("dve", "tensor_tensor", R=[a_, ML[d]], W=[a_], out=v3(a_[:, 0:nt], nch, 128),
                     in0=v3(a_[:, 0:nt], nch, 128), in1=mlb, op=ALU.subtract)
                k.op("act", "activation", R=[a_], W=[a_], out=a_[:, 0:nt], in_=a_[:, 0:nt], func=AF.Exp)
                k.op("dve", "tensor_tensor", R=[e_, ML[d]], W=[e_], out=v3(e_[:, 0:nt], nch, 128),
                     in0=v3(e_[:, 0:nt], nch, 128), in1=mlb, op=ALU.add)
                k.op("act", "activation", R=[e_], W=[e_], out=e_[:, 0:nt], in_=e_[:, 0:nt], func=AF.Exp, scale=-1.0)
                bk = k.bank()
                for c in range(nch):
                    k.op("pe", "transpose", R=[a_, k.ident], W=[bk], out=bk[:, c * 8:(c + 1) * 8],
                         in_=a_[:, c * 128:(c + 1) * 128], identity=k.ident[0:8, 0:8])
                    k.op("pe", "transpose", R=[e_, k.ident], W=[bk], out=bk[:, 64 + c * 8:64 + (c + 1) * 8],
                         in_=e_[:, c * 128:(c + 1) * 128], identity=k.ident[0:8, 0:8])
                k.op("act", "activation", R=[bk], W=[j.Atok[d]], out=j.Atok[d][:, c0 * 8:(c0 + nch) * 8],
                     in_=bk[:, 0:nch * 8], func=AF.Copy)
                k.op("act", "activation", R=[bk], W=[j.Etok[d]], out=j.Etok[d][:, c0 * 8:(c0 + nch) * 8],
                     in_=bk[:, 64:64 + nch * 8], func=AF.Copy)

    def mlstm_unit(self, j, l, hp, shared=None):
        k = self
        self.unit_phase(shared)
        AFa, ABa = self.AF_, self.AB_
        N, NB = j.N, j.NB
        W, W3 = shared[0:2] if shared else self.load_w(self.wA[l, hp], 640)
        QT, KT_ = k.tt(ABa, N), k.tt(ABa, N)
        Ktok = k.tt(ABa, N)
        VA0 = k.tt(ABa, NB * 130)
        SO, SG = k.tt(ABa, N), k.tt(ABa, N)
        HF = k.tt(AFa, N)
        VA04 = VA0.ap.rearrange("p (b h c) -> p b h c", b=NB, h=2, c=65)
        SO3, SG3 = v3(SO.ap, NB, 128), v3(SG.ap, NB, 128)
        k.op("dve", "memset", W=[VA0], ap=v3(VA0.ap, NB * 2, 65)[:, :, 64:65], constant=1.0)

        def blk(b5, tok0, nt, h3, ht):
            for qi, (dst, scl) in enumerate(((QT, 1.0), (KT_, 0.125))):
                bka = k.bank()
                k.mm_fm(bka, bka[:, 0:nt], W, W3, qi * 128, 128, h3, ht, nt)
                k.op("act", "activation", R=[bka], W=[dst], out=dst[:, tok0:tok0 + nt], in_=bka[:, 0:nt],
                     func=AF.Copy, scale=scl)
            bb = k.bbank()
            for b in range(nt // 128):
                k.op("pe", "transpose", R=[KT_, k.identb], W=[bb], out=bb[:, b * 128:(b + 1) * 128],
                     in_=KT_[:, tok0 + b * 128:tok0 + (b + 1) * 128], identity=k.identb.ap)
            k.op("act", "activation", R=[bb], W=[Ktok], out=Ktok[:, tok0:tok0 + nt], in_=bb[:, 0:nt], func=AF.Copy)
            for jj in range(nt // 128):
                bk = k.bank()
                k.mm_tm(bk, bk[:, 0:384], W, W3, 256, 384, h3, ht, jj * 128)
                bi = tok0 // 128 + jj
                k.op("act", "activation", R=[bk], W=[VA0], out=VA04[:, bi, :, 0:64], in_=v3(bk[:, 0:128], 2, 64),
                     func=AF.Copy)
                k.op("act", "activation", R=[bk], W=[SO], out=SO3[:, bi, :], in_=bk[:, 128:256], func=AF.Sigmoid)
                k.op("act", "activation", R=[bk], W=[SG], out=SG3[:, bi, :], in_=bk[:, 256:384], func=AF.Silu)
        self.stream_ht(j, blk)
        SB = min(self.SEGB, NB)
        SEG = SB * 128
        VA = [k.tt(ABa, SB * 130) for _ in range(2)]
        CS = [k.tt(ABa, SB * 65) for _ in range(2)]
        Sm = [[k.tt(ABa, SEG) for _ in range(2)] for _ in range(2)]
        den = [k.tt(AFa, 8) for _ in range(2)]
        hd = [k.tt(AFa, 256) for _ in range(2)]
        X = [k.tt(AFa, SEG) for _ in range(2)]
        ttm = self.tail_tmps(SEG)
        C = k.tt(AFa, 65)
        Fbc = k.tt(AFa, NB)
        mask3 = v3(k.masksb.ap, 4, 128)
        for d in range(2):
            bkf = k.bank()
            k.op("pe", "matmul", R=[k.sel, j.Fch[d]], W=[bkf], out=bkf[:, 0:NB], lhsT=v3(k.sel.ap, 4, 128)[:, hp, :],
                 rhs=j.Fch[d].ap, start=True, stop=True)
            k.op("act", "activation", R=[bkf], W=[Fbc], out=Fbc.ap, in_=bkf[:, 0:NB], func=AF.Copy)
            if j.kind == "s":
                k.dma("sp", C[:, 0:64], self.st_in["c"][l, d, 2 * hp:2 * hp + 2].rearrange("h a v -> (h a) v"), W=[C])
                k.dma("sp", C[:, 64:65], self.st_in["n"][l, d, 2 * hp:2 * hp + 2].rearrange("h (a o) -> (h a) o", o=1),
                      W=[C])
            else:
                k.op("dve", "memset", W=[C], ap=C.ap, constant=0.0)
            A3 = v3(j.Atok[d].ap, NB, 8)
            E3 = v3(j.Etok[d].ap, NB, 8)
            for si_, (b0, nb) in enumerate(self.segs(j, d)):
                s0, n = b0 * 128, nb * 128
                va, cs = VA[si_ % 2], CS[si_ % 2]
                va4 = va[:, 0:nb * 130].rearrange("p (b h c) -> p b h c", b=nb, h=2, c=65)
                k.op("dve", "tensor_tensor", R=[VA0, j.Atok[d]], W=[va], out=va4, in0=VA04[:, b0:b0 + nb, :, :],
                     in1=bc_last(A3[:, b0:b0 + nb, 2 * hp:2 * hp + 2], 65), op=ALU.mult)
                bdc = k.bank()
                for b in range(nb):
                    for h in range(2):
                        k.op("pe", "matmul", R=[Ktok, va], W=[bdc], out=bdc[h * 64:(h + 1) * 64, b * 65:(b + 1) * 65],
                             lhsT=Ktok[:, (b0 + b) * 128 + h * 64:(b0 + b) * 128 + (h + 1) * 64], rhs=va4[:, b, h, :],
                             start=True, stop=True)
                order = range(nb) if d == 0 else range(nb - 1, -1, -1)
                for b in order:
                    c = b0 + b
                    k.op("dve", "tensor_scalar", R=[C, Fbc], W=[cs], out=cs[:, b * 65:(b + 1) * 65], in0=C.ap,
                         scalar1=Fbc[:, c:c + 1], scalar2=None, op0=ALU.mult)
                    k.op("dve", "scalar_tensor_tensor", R=[C, Fbc, bdc], W=[C], out=C.ap, in0=C.ap, scalar=Fbc[:, c:c + 1],
                         in1=bdc[:, b * 65:(b + 1) * 65], op0=ALU.mult, op1=ALU.add)
                for h in range(2):
                    bst = k.bank()
                    for b in range(nb):
                        tk = slice(s0 + b * 128, s0 + (b + 1) * 128)
                        k.op("pe", "matmul", R=[KT_, QT], W=[bst], out=bst[:, b * 128:(b + 1) * 128],
                             lhsT=KT_[h * 64:(h + 1) * 64, tk], rhs=QT[h * 64:(h + 1) * 64, tk], start=True, stop=True)
                    sm = Sm[si_ % 2][h]
                    k.op("dve", "tensor_tensor", R=[bst, k.masksb], W=[sm], out=v3(sm[:, 0:n], nb, 128),
                         in0=v3(bst[:, 0:n], nb, 128), in1=bc_mid(mask3[:, d, :], nb), op=ALU.mult)
                xx = X[si_ % 2]
                for p0 in range(0, nb, 2):
                    n2 = min(2, nb - p0)
                    bo = k.bank()
                    for b in range(p0, p0 + n2):
                        for h in range(2):
                            sm = Sm[si_ % 2][h]
                            off = (b - p0) * 130 + h * 65
                            oo = bo[:, off:off + 65]
                            k.op("pe", "matmul", R=[sm, va], W=[bo], out=oo, lhsT=sm[:, b * 128:(b + 1) * 128],
                                 rhs=va4[:, b, h, :], start=True, stop=False)
                            k.op("pe", "matmul", R=[QT, cs], W=[bo], out=oo,
                                 lhsT=QT[h * 64:(h + 1) * 64, s0 + b * 128:s0 + (b + 1) * 128],
                                 rhs=cs[h * 64:(h + 1) * 64, b * 65:(b + 1) * 65], start=False, stop=True)
                    bo4 = bo[:, 0:n2 * 130].rearrange("p (b h c) -> p b h c", b=n2, h=2, c=65)
                    dn = den[(p0 // 2) % 2]
                    dn3 = v3(dn[:, 0:n2 * 2], n2, 2)
                    k.op("act", "activation", R=[bo], W=[dn], out=dn3, in_=bo4[:, :, :, 64], func=AF.Abs)
                    k.op("dve", "tensor_tensor", R=[dn, j.Etok[d]], W=[dn], out=dn3, in0=dn3,
                         in1=E3[:, b0 + p0:b0 + p0 + n2, 2 * hp:2 * hp + 2], op=ALU.max)
                    k.op("dve", "reciprocal", R=[dn], W=[dn], out=dn[:, 0:n2 * 2], in_=dn[:, 0:n2 * 2])
                    t0 = s0 + p0 * 128
                    if d == 0:
                        k.op("dve", "tensor_tensor", R=[bo, dn], W=[HF],
                             out=HF[:, t0:t0 + n2 * 128].rearrange("p (b h c) -> p b h c", b=n2, h=2, c=64),
                             in0=bo4[:, :, :, 0:64], in1=bc_last(dn3, 64), op=ALU.mult)
                    else:
                        hh = hd[(p0 // 2) % 2]
                        k.op("dve", "tensor_tensor", R=[bo, dn], W=[hh],
                             out=hh[:, 0:n2 * 128].rearrange("p (b h c) -> p b h c", b=n2, h=2, c=64),
                             in0=bo4[:, :, :, 0:64], in1=bc_last(dn3, 64), op=ALU.mult)
                        k.op("pool", "tensor_tensor", R=[hh, HF], W=[hh], out=hh[:, 0:n2 * 128], in0=hh[:, 0:n2 * 128],
                             in1=HF[:, t0:t0 + n2 * 128], op=ALU.add)
                        k.op("pool", "tensor_tensor", R=[hh, SO], W=[xx], out=xx[:, p0 * 128:(p0 + n2) * 128],
                             in0=hh[:, 0:n2 * 128], in1=SO[:, t0:t0 + n2 * 128], op=ALU.mult)
                if d == 1:
                    self.tail(j, 0, hp, s0, nb, xx, SG[:, s0:s0 + n], SG, ttm[si_ % 2])
            if j.kind == "p":
                k.dma("sp", self.so["c"][j.idx, l, d, 2 * hp:2 * hp + 2].rearrange("h a v -> (h a) v"), C[:, 0:64], R=[C])
                k.dma("sp", self.so["n"][j.idx, l, d, 2 * hp:2 * hp + 2].rearrange("h (a o) -> (h a) o", o=1),
                      C[:, 64:65], R=[C])

    def shiftmix(self, j, S, c0_ap, cd_ap, co, s0, n, out):
        k = self
        N = j.N
        k.op("dve", "tensor_scalar", R=[S, co], W=[out], out=out[:, 0:n], in0=S[:, s0:s0 + n], scalar1=c0_ap,
             scalar2=None, op0=ALU.mult)

        def acc(o_ap, i_ap, dirn):
            k.op("dve", "scalar_tensor_tensor", R=[S, co, out], W=[out], out=o_ap, in0=i_ap,
                 scalar=cd_ap[:, dirn:dirn + 1], in1=o_ap, op0=ALU.mult, op1=ALU.add)
        if j.kind == "s":
            o3 = v3(out[:, 0:n], n // 64, 64)
            s3 = v3(S[:, s0:s0 + n], n // 64, 64)
            acc(o3[:, :, 1:64], s3[:, :, 0:63], 0)
            acc(o3[:, :, 0:63], s3[:, :, 1:64], 1)
            i0 = 0 if s0 >= 64 else 64
            if n > i0:
                acc(out[:, i0:n], S[:, s0 + i0 - 64:s0 + n - 64], 2)
            i1 = n if s0 + n + 64 <= N else n - 64
            if i1 > 0:
                acc(out[:, 0:i1], S[:, s0 + 64:s0 + i1 + 64], 3)
        else:
            i0 = 0 if s0 > 0 else 1
            acc(out[:, i0:n], S[:, s0 + i0 - 1:s0 + n - 1], 0)
            i1 = n if s0 + n < N else n - 1
            acc(out[:, 0:i1], S[:, s0 + 1:s0 + i1 + 1], 1)

    def rwkv_prepass(self, j, l):
        k = self
        self.new_phase()
        AFa, ABa = self.AF_, self.AB_
        N, NB = j.N, j.NB
        if not hasattr(j, "LW"):
            j.LW = self.scratch(f"LW_{j.name}", [128, N], BF16)
            j.bLW = TT(None, self.P.buf())
        W, W3 = self.load_w(self.wDl[l], 128)
        mu = k.tt(AFa, 2)
        shl = k.tt(AFa, 4)
        k.dma("sp", mu.ap, self.rwLP[l], W=[mu])
        k.dma("sp", shl.ap, self.c_shl[j.kind], W=[shl])
        co = k.tt(AFa, 8)
        k.op("dve", "tensor_scalar", R=[mu], W=[co], out=co[:, 0:1], in0=mu[:, 0:1], scalar1=-1.0, scalar2=1.0,
             op0=ALU.mult, op1=ALU.add)
        k.op("dve", "tensor_scalar", R=[shl, mu], W=[co], out=co[:, 1:5], in0=shl.ap, scalar1=mu[:, 0:1], scalar2=None,
             op0=ALU.mult)
        S = k.tt(ABa, N)

        def blk(b5, tok0, nt, h3, ht):
            bk = k.bank()
            k.mm_fm(bk, bk[:, 0:nt], W, W3, 0, 128, h3, ht, nt)
            k.op("act", "activation", R=[bk], W=[S], out=S[:, tok0:tok0 + nt], in_=bk[:, 0:nt], func=AF.Copy)
        self.stream_ht(j, blk)
        SEG = min(self.SEGB, NB) * 128
        xo = [k.tt(AFa, SEG) for _ in range(2)]
        lo = [k.tt(ABa, SEG) for _ in range(2)]
        for si_, (b0, nb) in enumerate(self.segs(j, 0)):
            s0, n = b0 * 128, nb * 128
            x, lw = xo[si_ % 2], lo[si_ % 2]
            self.shiftmix(j, S, co[:, 0:1], co[:, 1:5], co, s0, n, x)
            k.op("act", "activation", R=[x], W=[lw], out=lw[0:64, 0:n], in_=x[0:64, 0:n], func=AF.Tanh)
            k.op("act", "activation", R=[x], W=[lw], out=lw[64:128, 0:n], in_=x[64:128, 0:n], func=AF.Copy)
            k.dma("sp", j.LW[:, s0:s0 + n], lw[:, 0:n], R=[lw], W=[j.bLW])

    def rwkv_unit(self, j, l, hp):
        k = self
        self.new_phase()
        AFa, ABa = self.AF_, self.AB_
        N, NB = j.N, j.NB
        CE = 0.6065306597126334
        rw = k.tt(AFa, 12)
        k.dma("sp", rw.ap, self.rwP[l, hp], W=[rw])
        shm = k.tt(AFa, 12)
        k.dma("sp", v3(shm.ap, 3, 4), self.c_shm[j.kind][hp], W=[shm])
        co = k.tt(AFa, 16)
        k.op("dve", "tensor_scalar", R=[rw], W=[co], out=co[:, 0:3], in0=rw[:, 0:3], scalar1=-1.0, scalar2=1.0,
             op0=ALU.mult, op1=ALU.add)
        k.op("dve", "tensor_tensor", R=[shm, rw], W=[co], out=v3(co[:, 4:16], 3, 4), in0=v3(shm.ap, 3, 4),
             in1=bc_last(rw[:, 0:3], 4), op=ALU.mult)
        w2 = k.tt(ABa, 256)
        for d in range(2):
            k.dma("pool", w2[:, d * 128:(d + 1) * 128], self.rwW2[l, d][:, hp * 128:(hp + 1) * 128], W=[w2])
        Rr, Kk, KK, BON, SGT, Vtok = (k.tt(ABa, N) for _ in range(6))
        YF = k.tt(AFa, N)
        Sst = k.tt(AFa, 64)
        mark_f, mark_b = AFa.pos, ABa.pos
        Sx = [k.tt(ABa, N) for _ in range(3)]
        W, W3 = self.load_w(self.wD[l, hp], 512)

        def blk(b5, tok0, nt, h3, ht):
            for c in range(3):
                bk = k.bank()
                k.mm_fm(bk, bk[:, 0:nt], W, W3, c * 128, 128, h3, ht, nt)
                k.op("act", "activation", R=[bk], W=[Sx[c]], out=Sx[c][:, tok0:tok0 + nt], in_=bk[:, 0:nt], func=AF.Copy)
            bk = k.bank()
            k.mm_fm(bk, bk[:, 0:nt], W, W3, 384, 128, h3, ht, nt)
            k.op("act", "activation", R=[bk], W=[SGT], out=SGT[:, tok0:tok0 + nt], in_=bk[:, 0:nt], func=AF.Silu)
        self.stream_ht(j, blk)
        SB = min(self.SEGB, NB)
        SEG = SB * 128
        X = [[k.tt(AFa, SEG) for _ in range(3)] for _ in range(2)]
        t1 = [k.tt(AFa, SEG) for _ in range(2)]
        t2 = [k.tt(AFa, SEG) for _ in range(2)]
        vb = [k.tt(ABa, SEG)] * 2
        for si_, (b0, nb) in enumerate(self.segs(j, 0)):
            s0, n = b0 * 128, nb * 128
            Xr, Xk, Xv = X[si_ % 2]
            a1, a2 = t1[si_ % 2], t2[si_ % 2]
            for c, xx in enumerate((Xr, Xk, Xv)):
                self.shiftmix(j, Sx[c], co[:, c:c + 1], v3(co[:, 4:16], 3, 4)[:, c, :], co, s0, n, xx)
            k.op("pool", "tensor_copy", R=[Xr], W=[Rr], out=Rr[:, s0:s0 + n], in_=Xr[:, 0:n])
            k.op("pool", "tensor_copy", R=[Xk], W=[Kk], out=Kk[:, s0:s0 + n], in_=Xk[:, 0:n])
            v_ = vb[si_ % 2]
            k.op("act", "activation", R=[Xk, rw], W=[v_], out=v_[:, 0:n], in_=Xk[:, 0:n], func=AF.Square, scale=rw[:, 3:4])
            bk = k.bank()
            k.op("pe", "matmul", R=[k.bonesb, v_], W=[bk], out=bk[:, 0:n], lhsT=k.bonesb.ap, rhs=v_[:, 0:n], start=True, stop=True)
            k.op("dve", "tensor_scalar", R=[bk], W=[a1], out=a1[:, 0:n], in0=bk[:, 0:n], scalar1=1e-24, scalar2=None, op0=ALU.max)
            k.op("act", "activation", R=[a1], W=[a1], out=a1[:, 0:n], in_=a1[:, 0:n], func=AF.Ln)
            k.op("act", "activation", R=[a1], W=[a1], out=a1[:, 0:n], in_=a1[:, 0:n], func=AF.Exp, scale=-0.5)
            k.op("dve", "scalar_tensor_tensor", R=[Xk, rw, a1], W=[KK], out=KK[:, s0:s0 + n], in0=Xk[:, 0:n],
                 scalar=rw[:, 3:4], in1=a1[:, 0:n], op0=ALU.mult, op1=ALU.mult)
            k.op("dve", "scalar_tensor_tensor", R=[Xr, rw, Xk], W=[v_], out=v_[:, 0:n], in0=Xr[:, 0:n],
                 scalar=rw[:, 5:6], in1=Xk[:, 0:n], op0=ALU.mult, op1=ALU.mult)
            bk2 = k.bank()
            k.op("pe", "matmul", R=[k.bonesb, v_], W=[bk2], out=bk2[:, 0:n], lhsT=k.bonesb.ap, rhs=v_[:, 0:n], start=True, stop=True)
            k.op("dve", "tensor_tensor", R=[bk2, Xv], W=[BON], out=BON[:, s0:s0 + n], in0=bk2[:, 0:n], in1=Xv[:, 0:n], op=ALU.mult)
            k.op("pool", "tensor_copy", R=[Xv], W=[v_], out=v_[:, 0:n], in_=Xv[:, 0:n])
            bb = k.bbank()
            for b in range(nb):
                k.op("pe", "transpose", R=[v_, k.identb], W=[bb], out=bb[:, b * 128:(b + 1) * 128],
                     in_=v_[:, b * 128:(b + 1) * 128], identity=k.identb.ap)
            k.op("act", "activation", R=[bb], W=[Vtok], out=Vtok[:, s0:s0 + n], in_=bb[:, 0:n], func=AF.Copy)
        RS_ = self.cfg.get("rw_stop", 9)
        if RS_ <= 1:
            return
        self.P.barrier()
        AFa.pos, ABa.pos = mark_f, mark_b
        F_ = {nm: k.tt(AFa, SEG) for nm in ("sgw", "a", "G", "Gm", "EG", "EnG", "EGm", "EGL", "kt", "bb")}
        Bt = {nm: k.tt(ABa, SEG) for nm in ("lw", "rT", "aT", "bT", "kT", "BpT", "Atok", "Bptok", "MT",
                                            "AhT", "Ubf")}
        BrbS, BrkS = k.tt(ABa, SB * 256), k.tt(ABa, SB * 256)
        Kp32 = k.tt(AFa, SEG)
        SS = k.tt(ABa, SB * 64)
        GSZ = self.cfg.get("rw_gsz", 1)
        NG = (SB + GSZ - 1) // GSZ
        G_ = [{nm: [k.tt(ABa, GSZ * 256) for _ in range(1 if nm in ("T", "TT") else 2)] for nm in ("X", "XT", "T", "TT", "X0", "I")}
              for _ in range(NG)]
        AakS = [k.tt(ABa, GSZ * 256) for _ in range(NG)]
        Zs = [k.tt(ABa, GSZ * 256) for _ in range(NG)]
        N32 = [k.tt(AFa, GSZ * 256) for _ in range(NG)]
        ysum = k.tt(AFa, SEG)
        ttm = self.tail_tmps(SEG, 1) * 2
        ptmp = k.tt(ABa, SEG)
        rm3 = v3(k.rmask.ap, 2, 512)
        mask3 = v3(k.masksb.ap, 4, 128)
        sti = k.tt(AFa, 128)
        for d in range(2):
            fl = (lambda a: a) if d == 0 else flip
            eidx = 127 if d == 0 else 0
            mSU, mSL, mU = (2, 3, 0) if d == 0 else (3, 2, 1)
            if j.kind == "s":
                k.dma("sp", v3(sti[0:64, :], 2, 64), self.st_in["s"][l, d, 2 * hp:2 * hp + 2].rearrange("h i j -> i h j"), W=[sti])
                bk = k.bank()
                k.op("pe", "transpose", R=[sti, k.ident], W=[bk], out=bk[:, 0:64], in_=sti[0:64, :], identity=k.ident[0:64, 0:64])
                k.op("act", "activation", R=[bk], W=[Sst], out=Sst.ap, in_=bk[:, 0:64], func=AF.Copy)
            else:
                k.op("dve", "memset", W=[Sst], ap=Sst.ap, constant=0.0)
            for si_, (b0, nb) in enumerate(self.segs(j, d)):
                s0, n = b0 * 128, nb * 128
                lw = Bt["lw"]
                k.dma("sp", lw[:, 0:n], j.LW[:, s0:s0 + n], R=[j.bLW], W=[lw])
                bzw, bza = k.bank(), k.bank()
                k.op("pe", "matmul", R=[w2, lw], W=[bzw], out=bzw[:, 0:n], lhsT=w2[0:64, d * 128:(d + 1) * 128],
                     rhs=lw[0:64, 0:n], start=True, stop=True)
                k.op("pe", "matmul", R=[w2, lw], W=[bza], out=bza[:, 0:n], lhsT=w2[64:128, d * 128:(d + 1) * 128],
                     rhs=lw[64:128, 0:n], start=True, stop=True)
                f = F_
                k.op("act", "activation", R=[bzw, rw], W=[f["sgw"]], out=f["sgw"][:, 0:n], in_=bzw[:, 0:n], func=AF.Sigmoid,
                     bias=rw[:, 6 + d:7 + d])
                k.op("act", "activation", R=[bza, rw], W=[f["a"]], out=f["a"][:, 0:n], in_=bza[:, 0:n], func=AF.Sigmoid,
                     bias=rw[:, 8 + d:9 + d])
                k.op("dve", "tensor_tensor_scan", R=[f["sgw"], k.rmask], W=[f["G"]], out=fl(f["G"][:, 0:n]),
                     data0=rm3[:, 0, 0:n], data1=fl(f["sgw"][:, 0:n]), initial=0.0, op0=ALU.mult, op1=ALU.add)
                k.op("act", "activation", R=[f["G"]], W=[f["EG"]], out=f["EG"][:, 0:n], in_=f["G"][:, 0:n], func=AF.Exp, scale=-CE)
                k.op("act", "activation", R=[f["G"]], W=[f["EnG"]], out=f["EnG"][:, 0:n], in_=f["G"][:, 0:n], func=AF.Exp, scale=CE)
                k.op("dve", "tensor_tensor", R=[f["G"], f["sgw"]], W=[f["Gm"]], out=f["Gm"][:, 0:n], in0=f["G"][:, 0:n],
                     in1=f["sgw"][:, 0:n], op=ALU.subtract)
                k.op("act", "activation", R=[f["Gm"]], W=[f["EGm"]], out=f["EGm"][:, 0:n], in_=f["Gm"][:, 0:n], func=AF.Exp, scale=-CE)
                G3 = v3(f["G"][:, 0:n], nb, 128)
                k.op("dve", "tensor_tensor", R=[f["G"]], W=[f["Gm"]], out=v3(f["Gm"][:, 0:n], nb, 128),
                     in0=bc_last(G3[:, :, eidx], 128), in1=G3, op=ALU.subtract)
                k.op("act", "activation", R=[f["Gm"]], W=[f["EGL"]], out=f["EGL"][:, 0:n], in_=f["Gm"][:, 0:n], func=AF.Exp, scale=-CE)
                k.op("dve", "tensor_scalar", R=[f["a"], rw], W=[f["kt"]], out=f["kt"][:, 0:n], in0=f["a"][:, 0:n], scalar1=-1.0,
                     scalar2=rw[:, 4:5], op0=ALU.add, op1=ALU.mult)
                k.op("dve", "scalar_tensor_tensor", R=[f["kt"], Kk], W=[f["kt"]], out=f["kt"][:, 0:n], in0=f["kt"][:, 0:n],
                     scalar=1.0, in1=Kk[:, s0:s0 + n], op0=ALU.add, op1=ALU.mult)
                k.op("pool", "tensor_tensor", R=[KK, f["a"]], W=[f["bb"]], out=f["bb"][:, 0:n], in0=KK[:, s0:s0 + n],
                     in1=f["a"][:, 0:n], op=ALU.mult)
                k.op("pool", "tensor_tensor", R=[Rr, f["EG"]], W=[Bt["rT"]], out=Bt["rT"][:, 0:n], in0=Rr[:, s0:s0 + n],
                     in1=f["EG"][:, 0:n], op=ALU.mult)
                k.op("dve", "scalar_tensor_tensor", R=[KK, f["EGm"]], W=[Bt["aT"]], out=Bt["aT"][:, 0:n], in0=KK[:, s0:s0 + n],
                     scalar=-1.0, in1=f["EGm"][:, 0:n], op0=ALU.mult, op1=ALU.mult)
                k.op("pool", "tensor_tensor", R=[f["bb"], f["EnG"]], W=[Bt["bT"]], out=Bt["bT"][:, 0:n], in0=f["bb"][:, 0:n],
                     in1=f["EnG"][:, 0:n], op=ALU.mult)
                k.op("dve", "tensor_tensor", R=[f["kt"], f["EnG"]], W=[Bt["kT"]], out=Bt["kT"][:, 0:n], in0=f["kt"][:, 0:n],
                     in1=f["EnG"][:, 0:n], op=ALU.mult)
                k.op("pool", "tensor_tensor", R=[f["bb"], f["EGL"]], W=[Bt["BpT"]], out=Bt["BpT"][:, 0:n], in0=f["bb"][:, 0:n],
                     in1=f["EGL"][:, 0:n], op=ALU.mult)
                k.op("dve", "tensor_tensor", R=[f["kt"], f["EGL"]], W=[f["Gm"]], out=f["Gm"][:, 0:n], in0=f["kt"][:, 0:n],
                     in1=f["EGL"][:, 0:n], op=ALU.mult)
                bkp = k.bank()
                for b in range(nb):
                    k.op("pe", "transpose", R=[f["Gm"], k.ident], W=[bkp], out=bkp[:, b * 128:(b + 1) * 128],
                         in_=f["Gm"][:, b * 128:(b + 1) * 128], identity=k.ident.ap)
                k.op("act", "activation", R=[bkp], W=[Kp32], out=Kp32[:, 0:n], in_=bkp[:, 0:n], func=AF.Copy)
                for src, dst in (("aT", "Atok"), ("BpT", "Bptok")):
                    bb = k.bbank()
                    for b in range(nb):
                        k.op("pe", "transpose", R=[Bt[src], k.identb], W=[bb], out=bb[:, b * 128:(b + 1) * 128],
                             in_=Bt[src][:, b * 128:(b + 1) * 128], identity=k.identb.ap)
                    k.op("act", "activation", R=[bb], W=[Bt[dst]], out=Bt[dst][:, 0:n], in_=bb[:, 0:n], func=AF.Copy)
                if RS_ <= 2:
                    continue
                groups = [(g0, min(GSZ, nb - g0)) for g0 in range(0, nb, GSZ)]

                refine = (j.kind == "p")

                def gen_group(gi, g0, ng):
                    T = G_[gi]
                    ni = ng * 2
                    w_ = ni * 128

                    def prod(lname, rname, dstTT, dst_ap_fn, midx):
                        for h in range(2):
                            bk = k.bank()
                            for bl in range(ng):
                                tk = slice((g0 + bl) * 128, (g0 + bl + 1) * 128)
                                k.op("pe", "matmul", R=[Bt[lname], Bt[rname]], W=[bk], out=bk[:, bl * 128:(bl + 1) * 128],
                                     lhsT=Bt[lname][h * 64:(h + 1) * 64, tk], rhs=Bt[rname][h * 64:(h + 1) * 64, tk],
                                     start=True, stop=True)
                            k.op("dve", "tensor_tensor", R=[bk, k.masksb], W=[dstTT], out=dst_ap_fn(h),
                                 in0=v3(bk[:, 0:ng * 128], ng, 128), in1=bc_mid(mask3[:, midx, :], ng), op=ALU.mult)
                    slot3 = lambda t: (lambda h: v3(t[:, h * ng * 128:(h + 1) * ng * 128], ng, 128))
                    X0, X0T = T["X0"][0], T["X0"][1]
                    prod("bT", "aT", X0T, slot3(X0T), mSU)
                    prod("aT", "bT", X0, slot3(X0), mSL)
                    yield
                    prod("aT", "kT", AakS[gi], slot3(AakS[gi]), mSL)
                    brb4 = v3(BrbS[:, 0:nb * 256], nb * 2, 128)
                    brk4 = v3(BrkS[:, 0:nb * 256], nb * 2, 128)
                    prod("bT", "rT", BrbS, lambda h: v3(BrbS[:, 0:nb * 256], nb, 256)[:, g0:g0 + ng, h * 128:(h + 1) * 128], mU)
                    prod("kT", "rT", BrkS, lambda h: v3(BrkS[:, 0:nb * 256], nb, 256)[:, g0:g0 + ng, h * 128:(h + 1) * 128], mU)
                    yield
                    if RS_ <= 3:
                        return
                    idb = bc_mid(k.identb.ap, ni)
                    hm = v3(k.hmaskb.ap, 4, 128)
                    IX, IXT = T["I"]

                    def msk(src, dst, mi):
                        k.op("pool", "tensor_tensor", R=[src, k.hmaskb], W=[dst], out=v3(dst[:, 0:w_], ni, 128),
                             in0=v3(src[:, 0:w_], ni, 128), in1=bc_mid(hm[:, mi, :], ni), op=ALU.mult)

                    def addid(eng, src, dst, srcTT=None):
                        k.op(eng, "tensor_tensor", R=[src, k.identb], W=[dst], out=v3(dst[:, 0:w_], ni, 128),
                             in0=v3(src[:, 0:w_], ni, 128), in1=idb, op=ALU.add)

                    def mm4(lt, rt):
                        bk = k.bank()
                        for it in range(ni):
                            sl = slice(it * 128, (it + 1) * 128)
                            k.op("pe", "matmul", R=[lt, rt], W=[bk], out=bk[:, sl], lhsT=lt[:, sl], rhs=rt[:, sl], start=True, stop=True)
                        return bk
                    msk(X0, T["X"][0], 0)
                    msk(X0T, T["XT"][0], 0)
                    addid("pool", T["X"][0], T["T"][0])
                    addid("pool", T["XT"][0], T["TT"][0])
                    cur, tc = 0, 0
                    if RS_ <= 3.2:
                        return
                    for lvl in range(1, 4):
                        nxt = 1 - cur
                        bx = mm4(T["XT"][cur], T["X"][cur])
                        bxt = mm4(T["X"][cur], T["XT"][cur])
                        addid("dve", bx, IX)
                        addid("dve", bxt, IXT)
                        if lvl < 3:
                            k.op("act", "activation", R=[bx], W=[T["X"][nxt]], out=T["X"][nxt][:, 0:w_], in_=bx[:, 0:w_], func=AF.Copy)
                            k.op("act", "activation", R=[bxt], W=[T["XT"][nxt]], out=T["XT"][nxt][:, 0:w_], in_=bxt[:, 0:w_], func=AF.Copy)
                        if RS_ <= 3.3:
                            return
                        yield
                        btt = mm4(IX, T["TT"][tc])
                        bt = mm4(IXT, T["T"][tc])
                        k.op("act", "activation", R=[btt], W=[T["TT"][tc]], out=T["TT"][tc][:, 0:w_], in_=btt[:, 0:w_], func=AF.Copy)
                        k.op("dve", "tensor_copy", R=[bt], W=[T["T"][tc]], out=T["T"][tc][:, 0:w_], in_=bt[:, 0:w_])
                        cur = nxt
                        if RS_ <= 3.4:
                            return
                        yield
                    if RS_ <= 3.5:
                        return
                    for li, mi in enumerate((1, 2, 3)):
                        lastl = li == 2
                        Ao, AoT, Q1, Q2 = T["X"][0], T["XT"][0], T["X"][1], T["XT"][1]
                        msk(X0, Ao, mi)
                        bq1 = mm4(Ao, T["TT"][tc])
                        k.op("act", "activation", R=[bq1], W=[Q1], out=Q1[:, 0:w_], in_=bq1[:, 0:w_], func=AF.Copy)
                        needT = refine or not lastl
                        if needT:
                            msk(X0T, AoT, mi)
                            bq2 = mm4(AoT, T["T"][tc])
                            k.op("dve", "tensor_copy", R=[bq2], W=[Q2], out=Q2[:, 0:w_], in_=bq2[:, 0:w_])
                        yield
                        btt = mm4(T["T"][tc], Q1)
                        if needT:
                            bt = mm4(T["TT"][tc], Q2)
                        k.op("dve", "tensor_tensor", R=[btt, T["TT"][tc]], W=[T["TT"][tc]], out=T["TT"][tc][:, 0:w_],
                             in0=btt[:, 0:w_], in1=T["TT"][tc][:, 0:w_], op=ALU.add)
                        if needT:
                            k.op("dve", "tensor_tensor", R=[bt, T["T"][tc]], W=[T["T"][tc]], out=T["T"][tc][:, 0:w_],
                                 in0=bt[:, 0:w_], in1=T["T"][tc][:, 0:w_], op=ALU.add)
                        yield
                    T0, TT0 = T["T"][tc], T["TT"][tc]
                    if not refine:
                        tparts = [TT0]
                    else:
                        tparts = None
                    Rm, TTh, TTl, s32 = T["I"][0], T["I"][1], T["X"][0], N32[gi]
                    self.dbgaps = dict(Rm=Rm.ap, TTh=TTh.ap, TTl=TTl.ap, s32=s32.ap, T0=T0.ap, TT0=TT0.ap, X0T=X0T.ap, X0=X0.ap)
                    if refine:
                        bA = mm4(X0, TT0)
                        k.op("dve", "tensor_tensor", R=[bA, TT0], W=[s32], out=s32[:, 0:w_], in0=bA[:, 0:w_], in1=TT0[:, 0:w_], op=ALU.subtract)
                        addid("dve", s32, Rm)
                        yield
                        bD = mm4(T0, Rm)
                        k.op("dve", "scalar_tensor_tensor", R=[bD, TT0], W=[s32], out=s32[:, 0:w_], in0=bD[:, 0:w_], scalar=float(self.cfg.get("nwt", 1.0)), in1=TT0[:, 0:w_], op0=ALU.mult, op1=ALU.add)
                        k.op("act", "activation", R=[s32], W=[TTh], out=TTh[:, 0:w_], in_=s32[:, 0:w_], func=AF.Copy)
                        k.op("dve", "tensor_tensor", R=[s32, TTh], W=[TTl], out=TTl[:, 0:w_], in0=s32[:, 0:w_], in1=TTh[:, 0:w_], op=ALU.subtract)
                        yield
                        tparts = [TTh, TTl]
                    cur = tc
                    TT_ = None
                    ba = k.bank()
                    for bl in range(ng):
                        for h in range(2):
                            it = h * ng + bl
                            for pi_, tpart in enumerate(tparts):
                                k.op("pe", "matmul", R=[Bt["Atok"], tpart], W=[ba], out=ba[h * 64:(h + 1) * 64, bl * 128:(bl + 1) * 128],
                                     lhsT=Bt["Atok"][:, (g0 + bl) * 128 + h * 64:(g0 + bl) * 128 + (h + 1) * 64],
                                     rhs=tpart[:, it * 128:(it + 1) * 128], start=(pi_ == 0), stop=(pi_ == len(tparts) - 1))
                    k.op("act", "activation", R=[ba], W=[Bt["AhT"]], out=Bt["AhT"][:, g0 * 128:(g0 + ng) * 128], in_=ba[:, 0:ng * 128],
                         func=AF.Copy)
                    bz = k.bank()
                    for it in range(ni):
                        sl = slice(it * 128, (it + 1) * 128)
                        for pi_, tpart in enumerate(tparts):
                            k.op("pe", "matmul", R=[tpart, AakS[gi]], W=[bz], out=bz[:, sl], lhsT=tpart[:, sl], rhs=AakS[gi][:, sl],
                                 start=(pi_ == 0), stop=(pi_ == len(tparts) - 1))
                    k.op("act", "activation", R=[bz], W=[Zs[gi]], out=Zs[gi][:, 0:w_], in_=bz[:, 0:w_], func=AF.Copy)
                    yield
                    bm = k.bank()
                    for bl in range(ng):
                        for h in range(2):
                            it = h * ng + bl
                            k.op("pe", "matmul", R=[Zs[gi], Bt["Bptok"]], W=[bm], out=bm[:, (bl * 2 + h) * 64:(bl * 2 + h + 1) * 64],
                                 lhsT=Zs[gi][:, it * 128:(it + 1) * 128],
                                 rhs=Bt["Bptok"][:, (g0 + bl) * 128 + h * 64:(g0 + bl) * 128 + (h + 1) * 64], start=True, stop=True)
                    k.op("dve", "tensor_tensor", R=[bm, Kp32], W=[Bt["MT"]], out=Bt["MT"][:, g0 * 128:(g0 + ng) * 128],
                         in0=bm[:, 0:ng * 128], in1=Kp32[:, g0 * 128:(g0 + ng) * 128], op=ALU.add)
                    bmy = k.bank()
                    for bl in range(ng):
                        for h in range(2):
                            it = h * ng + bl
                            sl = slice(((g0 + bl) * 2 + h) * 128, ((g0 + bl) * 2 + h + 1) * 128)
                            k.op("pe", "matmul", R=[Zs[gi], BrbS], W=[bmy], out=bmy[:, (bl * 2 + h) * 128:(bl * 2 + h + 1) * 128],
                                 lhsT=Zs[gi][:, it * 128:(it + 1) * 128], rhs=BrbS[:, sl], start=True, stop=True)
                    k.op("dve", "tensor_tensor", R=[bmy, BrkS], W=[BrkS], out=BrkS[:, g0 * 256:(g0 + ng) * 256],
                         in0=bmy[:, 0:ng * 256], in1=BrkS[:, g0 * 256:(g0 + ng) * 256], op=ALU.add)
                    yield
                gens = [gen_group(gi, g0, ng) for gi, (g0, ng) in enumerate(groups)]
                while gens:
                    for g in list(gens):
                        try:
                            next(g)
                        except StopIteration:
                            gens.remove(g)
                if RS_ <= 4:
                    continue
                order = range(nb) if d == 0 else range(nb - 1, -1, -1)
                Ub4 = Bt["Ubf"][:, 0:n].rearrange("p (b h c) -> p b h c", b=nb, h=2, c=64)
                for b in order:
                    k.op("pool", "tensor_copy", R=[Sst], W=[SS], out=SS[:, b * 64:(b + 1) * 64], in_=Sst.ap)
                    for h in range(2):
                        bu = k.bank()
                        k.op("pe", "matmul", R=[Bt["AhT"], SS], W=[bu], out=bu[:, 0:64], lhsT=Bt["AhT"][h * 64:(h + 1) * 64, b * 128:(b + 1) * 128],
                             rhs=SS[h * 64:(h + 1) * 64, b * 64:(b + 1) * 64], start=True, stop=True)
                        k.op("act", "activation", R=[bu], W=[Bt["Ubf"]], out=Ub4[:, b, h, :], in_=bu[:, 0:64], func=AF.Copy)
                    bd = k.bank()
                    for h in range(2):
                        oo = bd[h * 64:(h + 1) * 64, 0:64]
                        vs = Vtok[:, (b0 + b) * 128 + h * 64:(b0 + b) * 128 + (h + 1) * 64]
                        k.op("pe", "matmul", R=[Bt["Bptok"], Bt["Ubf"]], W=[bd], out=oo, lhsT=Bt["Bptok"][:, b * 128 + h * 64:b * 128 + (h + 1) * 64],
                             rhs=Ub4[:, b, h, :], start=True, stop=False)
                        k.op("pe", "matmul", R=[Bt["MT"], Vtok], W=[bd], out=oo, lhsT=Bt["MT"][:, b * 128 + h * 64:b * 128 + (h + 1) * 64],
                             rhs=vs, start=False, stop=True)
                    gcol = b * 128 + eidx
                    k.op("dve", "scalar_tensor_tensor", R=[Sst, f["EG"], bd], W=[Sst], out=Sst.ap, in0=Sst.ap, scalar=f["EG"][:, gcol:gcol + 1],
                         in1=bd[:, 0:64], op0=ALU.mult, op1=ALU.add)
                if RS_ <= 5:
                    continue
                by = k.bank()
                for b in range(nb):
                    for h in range(2):
                        oo = by[:, b * 128 + h * 64:b * 128 + (h + 1) * 64]
                        vs = Vtok[:, (b0 + b) * 128 + h * 64:(b0 + b) * 128 + (h + 1) * 64]
                        sl = slice((b * 2 + h) * 128, (b * 2 + h + 1) * 128)
                        k.op("pe", "matmul", R=[Bt["rT"], SS], W=[by], out=oo, lhsT=Bt["rT"][h * 64:(h + 1) * 64, b * 128:(b + 1) * 128],
                             rhs=SS[h * 64:(h + 1) * 64, b * 64:(b + 1) * 64], start=True, stop=False)
                        k.op("pe", "matmul", R=[BrbS, Bt["Ubf"]], W=[by], out=oo, lhsT=BrbS[:, sl], rhs=Ub4[:, b, h, :], start=False, stop=False)
                        k.op("pe", "matmul", R=[BrkS, Vtok], W=[by], out=oo, lhsT=BrkS[:, sl], rhs=vs, start=False, stop=True)
                if d == 0:
                    k.op("act", "activation", R=[by], W=[YF], out=YF[:, s0:s0 + n], in_=by[:, 0:n], func=AF.Copy)
                else:
                    k.op("dve", "tensor_tensor", R=[by, YF], W=[ysum], out=ysum[:, 0:n], in0=by[:, 0:n], in1=YF[:, s0:s0 + n], op=ALU.add)

                    def post(bb, yt, n_, s0=s0):
                        k.op("dve", "tensor_tensor", R=[bb, BON], W=[ptmp], out=ptmp[:, 0:n_], in0=bb[:, 0:n_], in1=BON[:, s0:s0 + n_], op=ALU.add)
                        k.op("pool", "tensor_tensor", R=[ptmp, SGT], W=[yt], out=yt[:, 0:n_], in0=ptmp[:, 0:n_], in1=SGT[:, s0:s0 + n_], op=ALU.mult)
                    self.tail(j, 3, hp, s0, nb, ysum, None, None, ttm[si_ % 2], post=post)
            if j.kind == "p" and RS_ > 6:
                bk = k.bank()
                k.op("pe", "transpose", R=[Sst, k.ident], W=[bk], out=bk[0:64, 0:128], in_=Sst.ap, identity=k.ident.ap)
                k.op("act", "activation", R=[bk], W=[sti], out=sti[0:64, :], in_=bk[0:64, 0:128], func=AF.Copy)
                k.dma("sp", self.so["s"][j.idx, l, d, 2 * hp:2 * hp + 2].rearrange("h i j -> i h j"), v3(sti[0:64, :], 2, 64), R=[sti])


def _kt(w):
    C = w.shape[1]
    return np.ascontiguousarray(w.reshape(KT, 128, C).transpose(1, 0, 2))


def prep_shared(inp, cfg):
    L = cfg["DEPTH"]
    f = np.float32
    w_in = np.asarray(inp["w_in"], f)
    out = {}
    wA = np.zeros((L, 4, 128, KT, 640), f)
    wAg = np.zeros((L, 128, KT, 32), f)
    wB = np.zeros((L, 4, 128, KT, 256), f)
    wC = np.zeros((L, 4, 128, KT, 768), f)
    wD = np.zeros((L, 4, 128, KT, 512), f)
    wDl = np.zeros((L, 128, KT, 128), f)
    sw = np.concatenate([np.arange(32, 64), np.arange(0, 32), np.arange(96, 128), np.arange(64, 96)])
    for l in range(L):
        w = w_in[l]
        wAg[l] = _kt(w[:, OFF_A + 2560:OFF_A + 2592])
        wDl[l] = _kt(w[:, OFF_D + 1536:OFF_D + 1664])
        for hp in range(4):
            sl = lambda base, comp: w[:, base + comp * 512 + hp * 128: base + comp * 512 + (hp + 1) * 128]
            wA[l, hp] = _kt(np.concatenate([sl(OFF_A, c) for c in range(5)], 1))
            wB[l, hp] = _kt(np.concatenate([sl(OFF_B, 0), sl(OFF_B, 1)], 1))
            q, k_, v, g = (sl(OFF_C, c) for c in range(4))
            wC[l, hp] = _kt(np.concatenate([q, q[:, sw], k_, k_[:, sw], v, g], 1))
            gD = w[:, OFF_D + 1664 + hp * 128: OFF_D + 1664 + (hp + 1) * 128]
            wD[l, hp] = _kt(np.concatenate([sl(OFF_D, 0), sl(OFF_D, 1), sl(OFF_D, 2), gD], 1))
    out.update(wA=wA, wAg=wAg, wB=wB, wC=wC, wD=wD, wDl=wDl)
    w_out = np.asarray(inp["w_out"], f)
    out["wout"] = np.ascontiguousarray(w_out.reshape(L, 16, 128, D).transpose(0, 2, 1, 3))
    w_mod = np.asarray(inp["w_mod"], f)
    out["wmod"] = np.ascontiguousarray(w_mod.reshape(L, KT, 128, 3 * D).transpose(0, 2, 1, 3))
    out["ng2"] = np.ascontiguousarray(np.repeat(np.asarray(inp["norm_g"], f)[:, None, :], 2, 1))
    out["bmod2"] = np.ascontiguousarray(np.repeat(np.asarray(inp["b_mod"], f)[:, None, :], 2, 1))
    out["fgb"] = np.ascontiguousarray(np.repeat(np.asarray(inp["final_g"], f)[None, :], 128, 0))
    out["gbA"] = np.ascontiguousarray(np.asarray(inp["mlstm_gate_b"], f).reshape(L, 4, 8).transpose(0, 2, 1))
    lruP = np.zeros((L, 4, 128, 12), f)
    lruG = np.zeros((L, 4, 128, 4, 128), f)
    retP = np.zeros((L, 4, 128, 6), f)
    rwP = np.zeros((L, 4, 128, 12), f)
    cw, cb = np.asarray(inp["lru_conv_w"], f), np.asarray(inp["lru_conv_b"], f)
    gw, gb = np.asarray(inp["lru_gate_w"], f), np.asarray(inp["lru_gate_b"], f)
    lam, th = np.asarray(inp["lru_lambda"], f), np.asarray(inp["ret_theta"], f)
    mu = np.asarray(inp["rwkv_mu"], f)
    kk_, ka_, rk_ = (np.asarray(inp[n], f) for n in ("rwkv_kk", "rwkv_ka", "rwkv_rk"))
    w0, a0 = np.asarray(inp["rwkv_w0"], f), np.asarray(inp["rwkv_a0"], f)
    for l in range(L):
        for hp in range(4):
            ch = slice(hp * 128, (hp + 1) * 128)
            for t in range(4):
                lruP[l, hp, :, t] = cw[l, t, ch]
            lruP[l, hp, :, 4] = cb[l, ch]
            for d in range(2):
                for g_ in range(2):
                    lruP[l, hp, :, 5 + d * 2 + g_] = gb[l, d, g_, ch]
                    for hl in range(2):
                        lruG[l, hp, hl * 64:(hl + 1) * 64, d * 2 + g_, hl * 64:(hl + 1) * 64] = gw[l, d, g_, 2 * hp + hl]
                lruP[l, hp, :, 9 + d] = lam[l, d, ch]
                retP[l, hp, 0:64, d] = th[l, d, 2 * hp]
                retP[l, hp, 64:128, d] = th[l, d, 2 * hp + 1]
                for hl in range(2):
                    retP[l, hp, :, 2 + d * 2 + hl] = th[l, d, 2 * hp + hl]
                rwP[l, hp, :, 6 + d] = w0[l, d, ch]
                rwP[l, hp, :, 8 + d] = a0[l, d, ch]
            for c in range(3):
                rwP[l, hp, :, c] = mu[l, c * 512 + hp * 128: c * 512 + (hp + 1) * 128]
            rwP[l, hp, :, 3] = kk_[l, ch]
            rwP[l, hp, :, 4] = ka_[l, ch]
            rwP[l, hp, :, 5] = rk_[l, ch]
    out.update(lruP=lruP, lruG=lruG, retP=retP, rwP=rwP)
    rwLP = np.zeros((L, 128, 2), f)
    rwLP[:, :, 0] = mu[:, 1536:1664]
    out["rwLP"] = rwLP
    out["rwW2"] = np.ascontiguousarray(np.concatenate([np.asarray(inp["rwkv_w2"], f), np.asarray(inp["rwkv_a2"], f)], 2))
    p = np.arange(128)[:, None]
    fr = np.arange(128)[None, :]
    out["c_ident"] = np.eye(128, dtype=f)
    out["c_masks"] = np.stack([(fr >= p), (fr <= p), (fr > p), (fr < p)], 1).astype(f)
    out["c_diff"] = np.stack([np.maximum(fr - p, 0), np.maximum(p - fr, 0)], 1).astype(f)
    pos = np.broadcast_to(fr, (128, 128))
    out["c_pos"] = np.stack([pos + 1, 128 - pos, 127 - pos, pos], 1).astype(f)
    sel = np.zeros((8, 4, 128), f)
    for hp in range(4):
        sel[2 * hp, hp, 0:64] = 1
        sel[2 * hp + 1, hp, 64:128] = 1
    out["c_sel"] = sel
    sel2 = np.zeros((2, 2, 128), f)
    sel2[0, 0] = 1
    sel2[1, 1] = 1
    out["c_sel2"] = sel2
    bo = np.zeros((128, 128), f)
    bo[0:64, 0:64] = 1
    bo[64:, 64:] = 1
    out["c_bones"] = bo
    t5 = np.arange(512)
    rm = np.zeros((128, 2, 512), f)
    rm[:, 0, :] = (t5 % 128 != 0)
    rm[:, 1, :] = np.where(t5 % 128 == 0, -1e30, 0.0)
    out["c_rmask"] = rm
    bdm = lambda sz: ((fr // sz) == (p // sz))
    out["c_hmask"] = np.stack([bdm(16), bdm(32) & ~bdm(16), bdm(64) & ~bdm(32), bdm(128) & ~bdm(64)], 1).astype(f)
    for kind in ("p", "s"):
        m = np.zeros((4, 128, 3, 4), f)
        ml = np.zeros((128, 4), f)

        def dirof(c):
            return (c // 416) if kind == "s" else (0 if c < 832 else 1)
        for hp in range(4):
            for comp in range(3):
                for pp in range(128):
                    m[hp, pp, comp, dirof(comp * 512 + hp * 128 + pp)] = 1
        for pp in range(128):
            ml[pp, dirof(1536 + pp)] = 1
        out[f"c_shm_{kind}"] = m
        out[f"c_shl_{kind}"] = ml
    NS = cfg["NS"]
    if NS:
        rows = NS // 64
        row_idx = np.repeat(np.arange(rows, dtype=f), 64)
        col_idx = np.tile(np.arange(64, dtype=f), rows)
        nfreq = 16
        freqs = (100.0 ** (-np.arange(nfreq, dtype=f) / nfreq)).astype(f)
        ang = np.concatenate([row_idx[:, None] * freqs, col_idx[:, None] * freqs], -1).astype(f)
        cs, sn = np.cos(ang).astype(f), np.sin(ang).astype(f)
        rope = np.zeros((2, 128, NS), f)
        for pp in range(128):
            dd = pp % 64
            rope[0, pp] = cs[:, dd % 32]
            rope[1, pp] = (-sn[:, dd % 32]) if dd < 32 else sn[:, dd % 32]
        out["rope"] = rope
    return out


def prep_core(inp, cfg, core, shared):
    f = np.float32
    NPS, NS = cfg["NPS"], cfg["NS"]
    m = dict(shared)
    cvec = np.zeros((2, D), f)
    cvec[0] = np.asarray(inp["c_ctx"], f)
    if NPS:
        m["xp"] = np.ascontiguousarray(np.asarray(inp["x_prompt"], f)[core * NPS:(core + 1) * NPS])
    if NS:
        nb = np.asarray(inp["x_sample"]).shape[0]
        b = (core * nb) // cfg.get("NCORES", 8)
        m["xs"] = np.ascontiguousarray(np.asarray(inp["x_sample"], f)[b])
        cvec[1] = np.asarray(inp["c"], f)[b]
        m["st_c"] = np.ascontiguousarray(np.asarray(inp["state_mlstm_c"], f)[b])
        m["st_n"] = np.ascontiguousarray(np.asarray(inp["state_mlstm_n"], f)[b])
        m["st_m"] = np.ascontiguousarray(np.asarray(inp["state_mlstm_m"], f)[b])
        m["st_h"] = np.ascontiguousarray(np.asarray(inp["state_lru_h"], f)[b])
        m["st_r"] = np.ascontiguousarray(np.asarray(inp["state_ret_r"], f)[b])
        m["st_s"] = np.ascontiguousarray(np.asarray(inp["state_rwkv_s"], f)[b])
    m["cvT"] = np.ascontiguousarray(cvec.reshape(2, KT, 128).transpose(2, 1, 0))
    return m


FULL_CFG = dict(DEPTH=4, NP=256, NPS=2, NS=4096, NCORES=8)
_CACHE = {}


def kernel(**inputs):
    cfg = FULL_CFG
    if "nc" not in _CACHE:
        kb = KB(cfg)
        _CACHE["nc"] = kb.build()
        _CACHE["kb"] = kb
    nc, kb = _CACHE["nc"], _CACHE["kb"]
    shared = prep_shared(inputs, cfg)
    in_maps = []
    for core in range(8):
        m = prep_core(inputs, cfg, core, shared)
        in_maps.append({k_: m[k_] for k_ in kb.din})
    res = run_bass_kernel_spmd(nc, in_maps, core_ids=list(range(8)))
    r = res.results
    L = cfg["DEPTH"]
    y_prompt = np.concatenate([r[i]["yp"] for i in range(8)], 0)
    y_sample = np.stack([r[0]["ys"], r[4]["ys"]], 0)
    outs = [y_prompt, y_sample]
    for nm in ("o_c", "o_n", "o_m", "o_h", "o_r", "o_s"):
        outs.append(np.concatenate([r[i][nm] for i in range(8)], 0))
    return tuple(np.asarray(o, np.float32) for o in outs)
```
